# Optimizing a Trainium2 kernel written in Bass

```python
import math
import jax, jax.numpy as jnp
from jax import lax
import numpy as np

D_MODEL = 1024
BATCH = 16
SEQ = 2048
DEPTH = 1

CHUNK = 64
Q_BLOCK = 128
N_MEM = 256
N_HEADS_A = 8
HEAD_DIM_A = 64
D_A = N_HEADS_A * 2 * HEAD_DIM_A
D_CONV = D_MODEL
CONV_K = 31
N_HEADS_X = 4
HEAD_DIM_X = 256
D_X = N_HEADS_X * HEAD_DIM_X
N_BRANCH = 3
N_BUCKETS = 32
MAX_DIST = 128
EPS = 1e-6
NEG_INF = -1e30
IN_SIZES = (D_A, D_A, D_A, D_A, 2 * D_CONV, D_CONV, D_X, D_X, N_BRANCH * D_MODEL)
IN_WIDTH = sum(IN_SIZES)
IN_SPLITS = tuple(int(s) for s in np.cumsum(IN_SIZES)[:-1])

kernel_name = "hybrid_diffattn_conformer_memxattn_gated"


def _rmsnorm(x, g):
    xf = x.astype(jnp.float32)
    y = xf * lax.rsqrt(jnp.mean(xf * xf, axis=-1, keepdims=True) + EPS)
    return (y * g.astype(jnp.float32)).astype(x.dtype)


def _layernorm(x, g, b):
    xf = x.astype(jnp.float32)
    mu = jnp.mean(xf, axis=-1, keepdims=True)
    var = jnp.mean(jnp.square(xf - mu), axis=-1, keepdims=True)
    y = (xf - mu) * lax.rsqrt(var + EPS)
    return (y * g.astype(jnp.float32) + b.astype(jnp.float32)).astype(x.dtype)


def _t5_bucket(rel):
    nb = N_BUCKETS // 2
    max_exact = nb // 2
    ret = (rel > 0).astype(jnp.int32) * nb
    n = jnp.abs(rel)
    nf = jnp.maximum(n, 1).astype(jnp.float32)
    large = max_exact + (jnp.log(nf / max_exact) / math.log(MAX_DIST / max_exact)
                         * (nb - max_exact)).astype(jnp.int32)
    large = jnp.minimum(large, nb - 1)
    return ret + jnp.where(n < max_exact, n, large)


def _diff_attention(q, k, v, rel_bias, lam):
    S = q.shape[1]
    scale = HEAD_DIM_A ** -0.5
    outs = []
    for i in range(S // Q_BLOCK):
        q0 = i * Q_BLOCK
        lk = q0 + Q_BLOCK
        qb, kb, vb = q[:, q0:lk], k[:, :lk], v[:, :lk]
        qpos = q0 + jnp.arange(Q_BLOCK, dtype=jnp.int32)
        kpos = jnp.arange(lk, dtype=jnp.int32)
        rel = kpos[None, :] - qpos[:, None]
        bias = jnp.transpose(rel_bias[_t5_bucket(rel)].astype(jnp.float32), (2, 0, 1))
        allowed = (kpos[None, :] // CHUNK) <= (qpos[:, None] // CHUNK)
        s = jnp.einsum('bqhcd,bkhcd->bhcqk', qb, kb).astype(jnp.float32) * scale
        s = jnp.where(allowed, s + bias[None, :, None], NEG_INF)
        p = jax.nn.softmax(s, axis=-1)
        a = p[:, :, 0] - lam * p[:, :, 1]
        outs.append(jnp.einsum('bhqk,bkhe->bqhe', a.astype(vb.dtype), vb))
    return jnp.concatenate(outs, axis=1)


def _conv_module(u, conv_w, conv_b, ln_g, ln_b):
    a, g = jnp.split(u, 2, axis=-1)
    c = a * jax.nn.sigmoid(g)
    w = conv_w.astype(c.dtype)[:, None, :]
    c = lax.conv_general_dilated(c, w, window_strides=(1,), padding=[(CONV_K - 1, 0)],
                                 dimension_numbers=('NWC', 'WIO', 'NWC'),
                                 feature_group_count=D_CONV) + conv_b
    return jax.nn.silu(_layernorm(c, ln_g, ln_b))


def _cross_attention(q, mem_n, w_mem_kv):
    B, S = q.shape[:2]
    kv = (mem_n @ w_mem_kv).reshape(B, N_MEM, 2, N_HEADS_X, HEAD_DIM_X)
    s = jnp.einsum('bqhd,bkhd->bhqk', q, kv[:, :, 0]).astype(jnp.float32) * HEAD_DIM_X ** -0.5
    p = jax.nn.softmax(s, axis=-1).astype(q.dtype)
    return jnp.einsum('bhqk,bkhd->bqhd', p, kv[:, :, 1]).reshape(B, S, D_X)


def setup_inputs(seed: int = 0) -> dict:
    key = jax.random.key(seed)
    ks = jax.random.split(key, 24)
    f32 = jnp.float32
    L, D = DEPTH, D_MODEL
    nrm = lambda k, shape, s: jax.random.normal(k, shape, f32) * s
    return {
        "x": nrm(ks[0], (BATCH, SEQ, D), 1.0),
        "mem": nrm(ks[1], (BATCH, N_MEM, D), 1.0),
        "rel_bias": nrm(ks[2], (N_BUCKETS, N_HEADS_A), 0.5),
        "g_pre": 1.0 + nrm(ks[3], (L, D), 0.05),
        "g_mem": 1.0 + nrm(ks[4], (L, D), 0.05),
        "w_in": nrm(ks[5], (L, D, IN_WIDTH), D ** -0.5),
        "b_merge": nrm(ks[6], (L, N_BRANCH * D), 0.02),
        "lam_q1": nrm(ks[7], (L, HEAD_DIM_A), 0.1),
        "lam_k1": nrm(ks[8], (L, HEAD_DIM_A), 0.1),
        "lam_q2": nrm(ks[9], (L, HEAD_DIM_A), 0.1),
        "lam_k2": nrm(ks[10], (L, HEAD_DIM_A), 0.1),
        "g_subln": 1.0 + nrm(ks[11], (L, 2 * HEAD_DIM_A), 0.05),
        "w_oa": nrm(ks[12], (L, D_A, D), D_A ** -0.5),
        "conv_w": nrm(ks[13], (L, CONV_K, D_CONV), CONV_K ** -0.5),
        "conv_b": nrm(ks[14], (L, D_CONV), 0.02),
        "ln_g": 1.0 + nrm(ks[15], (L, D_CONV), 0.05),
        "ln_b": nrm(ks[16], (L, D_CONV), 0.02),
        "w_ob": nrm(ks[17], (L, D_CONV, D), D_CONV ** -0.5),
        "w_mem_kv": nrm(ks[18], (L, D, 2 * D_X), D ** -0.5),
        "w_oc": nrm(ks[19], (L, D_X, D), D_X ** -0.5),
        "w_out": nrm(ks[20], (L, D, D), D ** -0.5),
        "g_post": 1.0 + nrm(ks[21], (L, D), 0.05),
    }


def reference(x, mem, rel_bias, g_pre, g_mem, w_in, b_merge, lam_q1, lam_k1, lam_q2, lam_k2,
              g_subln, w_oa, conv_w, conv_b, ln_g, ln_b, w_ob, w_mem_kv, w_oc, w_out, g_post):
    B, S, D = x.shape
    for l in range(DEPTH):
        lam_init = 0.8 - 0.6 * math.exp(-0.3 * l)
        h = _rmsnorm(x, g_pre[l])
        proj = h @ w_in[l]
        qa, ka, va, za, ub, zb, qc, zc, gates = jnp.split(proj, IN_SPLITS, axis=-1)

        lam = (jnp.exp(jnp.sum(lam_q1[l].astype(jnp.float32) * lam_k1[l].astype(jnp.float32)))
               - jnp.exp(jnp.sum(lam_q2[l].astype(jnp.float32) * lam_k2[l].astype(jnp.float32)))
               + lam_init)
        oa = _diff_attention(qa.reshape(B, S, N_HEADS_A, 2, HEAD_DIM_A),
                             ka.reshape(B, S, N_HEADS_A, 2, HEAD_DIM_A),
                             va.reshape(B, S, N_HEADS_A, 2 * HEAD_DIM_A), rel_bias, lam)
        oa = (_rmsnorm(oa, g_subln[l]) * (1.0 - lam_init)).reshape(B, S, D_A)
        ya = (oa * jax.nn.silu(za)) @ w_oa[l]

        ob = _conv_module(ub, conv_w[l], conv_b[l], ln_g[l], ln_b[l])
        yb = (ob * jax.nn.silu(zb)) @ w_ob[l]

        oc = _cross_attention(qc.reshape(B, S, N_HEADS_X, HEAD_DIM_X), _rmsnorm(mem, g_mem[l]), w_mem_kv[l])
        yc = (oc * jax.nn.silu(zc)) @ w_oc[l]

        g = jax.nn.sigmoid(gates + b_merge[l]).reshape(B, S, N_BRANCH, D)
        y = (g[:, :, 0] * ya + g[:, :, 1] * yb + g[:, :, 2] * yc) @ w_out[l]
        x = x + _rmsnorm(y, g_post[l])
    return x
```

```python
import math
from contextlib import ExitStack
import numpy as np
import concourse.bass as bass
import concourse.mybir as mybir
from concourse.bass_utils import run_bass_kernel_spmd

F32 = mybir.dt.float32
BF16 = mybir.dt.bfloat16
AF = mybir.ActivationFunctionType
ALU = mybir.AluOpType
AX = mybir.AxisListType

SAME_ENGINE_SYNC = True
EPOCH = 20000
NEG = -30000.0
SEQ = 2048
DM = 1024
NCORES = 8


class Tk:
    __slots__ = ("name", "lw", "rd", "dsem", "dcnt")

    def __init__(self, name):
        self.name = name
        self.lw = None
        self.rd = {}
        self.dsem = None
        self.dcnt = 0


class Ins:
    __slots__ = ("eng", "fn", "cdeps", "raw", "dwaits", "sig", "val", "dma", "waits")

    def __init__(self, eng, fn):
        self.eng = eng
        self.fn = fn
        self.cdeps = set()
        self.raw = set()
        self.dwaits = {}
        self.sig = False
        self.val = None
        self.dma = None
        self.waits = []


class Sched:
    ENGS = ("pe", "act", "dve", "pool", "sp")
    COMPUTE = ("pe", "act", "dve", "pool")

    def __init__(self, nc, stack):
        self.nc = nc
        self.stack = stack
        self.prog = {e: [] for e in self.ENGS}
        self.esem = {e: [] for e in self.ENGS}
        self.toks = []
        self.last_compute = {e: None for e in self.ENGS}
        self.pending = {e: (set(), {}) for e in self.ENGS}

    def tok(self, name):
        t = Tk(name)
        self.toks.append(t)
        return t

    def _dep(self, ins, p, raw):
        if p is None or p is ins:
            return
        if p.dma is not None:
            t = p.dma
            ins.dwaits[t] = max(ins.dwaits.get(t, 0), 16 * t.dcnt)
        else:
            ins.cdeps.add(p)
            if raw:
                ins.raw.add(p)

    def emit(self, eng, fn, reads=(), writes=(), dma=None):
        ins = Ins(eng, fn)
        pc, pd = self.pending[eng]
        if pc or pd:
            for p in pc:
                if p.eng != eng:
                    ins.cdeps.add(p)
            for t, v in pd.items():
                ins.dwaits[t] = max(ins.dwaits.get(t, 0), v)
            self.pending[eng] = (set(), {})
        for t in reads:
            self._dep(ins, t.lw, True)
        for t in writes:
            self._dep(ins, t.lw, False)
            for r in t.rd.values():
                self._dep(ins, r, False)
        if dma is not None:
            if dma.dsem is None:
                dma.dsem = self.stack.enter_context(self.nc.semaphore("ds_" + dma.name))
            dma.dcnt += 1
            ins.dma = dma
        else:
            self.last_compute[eng] = ins
        for t in reads:
            key = eng if dma is None else ("dma", id(ins))
            t.rd[key] = ins
        for t in writes:
            t.lw = ins
            t.rd = {}
        self.prog[eng].append(ins)
        return ins

    def barrier(self, engs, dma_toks=()):
        lasts = set(self.last_compute[e] for e in self.COMPUTE if self.last_compute[e] is not None)
        for e in engs:
            pc, pd = self.pending[e]
            pc |= lasts
            for t in dma_toks:
                if t.dsem is not None:
                    pd[t] = max(pd.get(t, 0), 16 * t.dcnt)

    def dma_load(self, q, tok, out_ap, in_ap, **kw):
        return self.emit(q, lambda e: e.dma_start(out=out_ap, in_=in_ap, **kw), writes=[tok], dma=tok)

    def dma_store(self, q, tok, out_ap, in_ap, **kw):
        return self.emit(q, lambda e: e.dma_start(out=out_ap, in_=in_ap, **kw), reads=[tok], dma=tok)

    def _needs(self, ins, p):
        if p.eng != ins.eng:
            return True
        return SAME_ENGINE_SYNC and p.eng != "pe"

    def finalize(self):
        for e in self.ENGS:
            for ins in self.prog[e]:
                for p in ins.cdeps:
                    if self._needs(ins, p):
                        p.sig = True
        for e in self.ENGS:
            c = 0
            for ins in self.prog[e]:
                if ins.sig:
                    ep, v = divmod(c, EPOCH)
                    if ep >= len(self.esem[e]):
                        self.esem[e].append(self.stack.enter_context(self.nc.semaphore("es_%s%d" % (e, ep))))
                    ins.val = (ep, v + 1)
                    c += 1
        self.sigcount = {e: sum(1 for i in self.prog[e] if i.sig) for e in self.ENGS}
        self.nwaits = 0
        for e in self.ENGS:
            known = {}
            for ins in self.prog[e]:
                need = {}
                for p in ins.cdeps:
                    if not self._needs(ins, p):
                        continue
                    need[p.eng] = max(need.get(p.eng, (0, 0)), p.val)
                for pe_, v in need.items():
                    if known.get(pe_, (0, 0)) < v:
                        ins.waits.append((self.esem[pe_][v[0]], v[1]))
                        known[pe_] = v
                for t, v in ins.dwaits.items():
                    k = ("d", id(t))
                    if known.get(k, 0) < v:
                        ins.waits.append((t.dsem, v))
                        known[k] = v
                self.nwaits += len(ins.waits)

    def simulate(self):
        sem = {}
        pc = {e: 0 for e in self.ENGS}
        progress = True
        while progress:
            progress = False
            for e in self.ENGS:
                while pc[e] < len(self.prog[e]):
                    ins = self.prog[e][pc[e]]
                    if all(sem.get(id(s), 0) >= v for (s, v) in ins.waits):
                        if ins.dma is not None:
                            sem[id(ins.dma.dsem)] = sem.get(id(ins.dma.dsem), 0) + 16
                        elif ins.sig:
                            k = id(self.esem[e][ins.val[0]])
                            sem[k] = sem.get(k, 0) + 1
                        pc[e] += 1
                        progress = True
                    else:
                        break
        stuck = {e: (pc[e], len(self.prog[e])) for e in self.ENGS if pc[e] < len(self.prog[e])}
        if stuck:
            for e in stuck:
                ins = self.prog[e][pc[e]]
                print("STUCK", e, pc[e], [(s.name, v, sem.get(id(s), 0)) for (s, v) in ins.waits])
            raise RuntimeError("deadlock in generated program: %s" % stuck)

    def run_engine(self, ename, e):
        for ins in self.prog[ename]:
            for (s, v) in ins.waits:
                e.wait_ge(s, v)
            bi = ins.fn(e)
            if ins.dma is not None:
                bi.then_inc(ins.dma.dsem, 16)
            elif ins.sig:
                bi.then_inc(self.esem[ename][ins.val[0]], 1)

    def build(self):
        fin = {}
        for t in self.toks:
            if t.dsem is not None:
                fin[t] = 16 * t.dcnt
        self.finalize()
        self.simulate()
        nc = self.nc
        with nc.Block() as block:
            @block.tensor
            def _(e):
                self.run_engine("pe", e)

            @block.scalar
            def _(e):
                self.run_engine("act", e)

            @block.vector
            def _(e):
                self.run_engine("dve", e)

            @block.gpsimd
            def _(e):
                self.run_engine("pool", e)

            @block.sync
            def _(e):
                self.run_engine("sp", e)
                for t, v in fin.items():
                    e.wait_ge(t.dsem, v)


NCOLS = 80
C_GPRE, C_GMEM, C_BM, C_CB, C_LNG, C_LNB, C_GSUB, C_CFAR = 0, 8, 16, 40, 48, 56, 64, 65


def build_program(NSEQ=2, debug=None, phases="0ABCD"):
    nc = bass.Bass("TRN2", target_bir_lowering=False)
    dram = lambda n, shp, kind="ExternalInput": nc.dram_tensor(n, shp, F32, kind=kind).ap()
    x_d = dram("x", [NSEQ * SEQ, DM])
    mem_d = dram("mem", [NSEQ * 256, DM])
    w_in = dram("w_in", [DM, 12288])
    w_oa = dram("w_oa", [DM, DM])
    w_ob = dram("w_ob", [DM, DM])
    w_oc = dram("w_oc", [DM, DM])
    w_out = dram("w_out", [DM, DM])
    w_kv = dram("w_mem_kv", [DM, 2048])
    cols_d = dram("cols", [128, NCOLS])
    convw_d = dram("convw", [128, 8 * 31])
    gpost_d = dram("g_post", [DM])
    lamv_d = dram("lamv", [256])
    bias_d = dram("biasdu", [128, 8 * 256])
    out_d = dram("out", [NSEQ * SEQ, DM], kind="ExternalOutput")
    dbg_d = {}
    if debug:
        for name, shp in debug.items():
            dbg_d[name] = dram("dbg_" + name, shp, kind="ExternalOutput")

    with ExitStack() as st:
        S = Sched(nc, st)
        sb = lambda name, shape, dt: st.enter_context(nc.sbuf_tensor("s_" + name, shape, dt))
        psb = lambda name, shape, dt: st.enter_context(nc.psum_tensor("p_" + name, shape, dt))

        cols = sb("cols", [128, NCOLS], F32)
        convw = sb("convw", [128, 8, 31], F32)
        gpost = sb("gpost", [128, DM], F32)
        lamv = sb("lamv", [128, 256], F32)
        biasdu = sb("biasdu", [128, 8, 256], F32)
        identf = sb("identf", [128, 128], F32)
        identb = sb("identb", [128, 128], BF16)
        ones_b = sb("ones_b", [128, 128], BF16)
        ones128 = sb("ones128", [128, 128], BF16)
        ones1024 = sb("ones1024", [128, 128], BF16)
        mhalf = sb("mhalf", [128, 512], F32)
        small = sb("small", [128, 16], F32)
        hbm = sb("hbm", [128, 24], F32)
        hlng = sb("hlng", [128, 8], F32)
        hlnb = sb("hlnb", [128, 8], F32)
        t_const = S.tok("const")
        S.dma_load("sp", t_const, cols[:], cols_d)
        S.dma_load("sp", t_const, convw[:], convw_d.rearrange("p (c j) -> p c j", j=31))
        S.dma_load("sp", t_const, gpost[:], gpost_d.partition_broadcast(128))
        S.dma_load("sp", t_const, lamv[:], lamv_d.partition_broadcast(128))
        S.dma_load("sp", t_const, biasdu[:], bias_d.rearrange("p (h k) -> p h k", k=256))
        t_c2 = S.tok("const2")
        S.emit("pool", lambda e: e.memset(identf[:], 0.0), writes=[t_c2])
        S.emit("pool", lambda e: e.affine_select(out=identf[:], in_=identf[:], pattern=[[-1, 128]],
                                                 compare_op=ALU.not_equal, fill=1.0, base=0,
                                                 channel_multiplier=1), reads=[t_c2], writes=[t_c2])
        S.emit("pool", lambda e: e.memset(mhalf[:], -0.5), writes=[t_c2])
        S.emit("dve", lambda e: e.tensor_copy(out=identb[:], in_=identf[:]), reads=[t_c2], writes=[t_c2])
        S.emit("dve", lambda e: e.memset(ones_b[:], 1.0), writes=[t_c2])
        S.emit("dve", lambda e: e.memset(ones128[:], 1.0 / 128), writes=[t_c2])
        S.emit("dve", lambda e: e.memset(ones1024[:], 1.0 / 1024), writes=[t_c2])
        lam_init = 0.8 - 0.6 * math.exp(-0.3 * 0)
        S.emit("dve", lambda e: e.tensor_tensor(out=lamv[:, 0:64], in0=lamv[:, 0:64], in1=lamv[:, 64:128], op=ALU.mult),
               reads=[t_const], writes=[t_c2])
        S.emit("dve", lambda e: e.tensor_tensor(out=lamv[:, 128:192], in0=lamv[:, 128:192], in1=lamv[:, 192:256], op=ALU.mult),
               reads=[t_const], writes=[t_c2])
        S.emit("dve", lambda e: e.reduce_sum(out=small[:, 2:3], in_=lamv[:, 0:64], axis=AX.X), reads=[t_c2], writes=[t_c2])
        S.emit("dve", lambda e: e.reduce_sum(out=small[:, 3:4], in_=lamv[:, 128:192], axis=AX.X), reads=[t_c2], writes=[t_c2])
        S.emit("act", lambda e: e.activation(out=small[:, 4:6], in_=small[:, 2:4], func=AF.Exp), reads=[t_c2], writes=[t_c2])
        S.emit("dve", lambda e: e.tensor_tensor(out=small[:, 0:1], in0=small[:, 5:6], in1=small[:, 4:5], op=ALU.subtract),
               reads=[t_c2], writes=[t_c2])
        S.emit("dve", lambda e: e.tensor_scalar(out=small[:, 0:1], in0=small[:, 0:1], scalar1=-lam_init, scalar2=None, op0=ALU.add),
               reads=[t_c2], writes=[t_c2])
        S.emit("dve", lambda e: e.tensor_scalar(out=small[:, 1:2], in0=cols[:, C_GSUB:C_GSUB + 1], scalar1=0.5 * (1.0 - lam_init),
                                                scalar2=None, op0=ALU.mult), reads=[t_const], writes=[t_c2])
        S.emit("dve", lambda e: e.tensor_scalar(out=hbm[:], in0=cols[:, C_BM:C_BM + 24], scalar1=0.5, scalar2=None, op0=ALU.mult),
               reads=[t_const], writes=[t_c2])
        S.emit("dve", lambda e: e.tensor_scalar(out=hlng[:], in0=cols[:, C_LNG:C_LNG + 8], scalar1=0.5, scalar2=None, op0=ALU.mult),
               reads=[t_const], writes=[t_c2])
        S.emit("dve", lambda e: e.tensor_scalar(out=hlnb[:], in0=cols[:, C_LNB:C_LNB + 8], scalar1=0.5, scalar2=None, op0=ALU.mult),
               reads=[t_const], writes=[t_c2])
        S.emit("dve", lambda e: e.tensor_scalar(out=convw[:], in0=convw[:], scalar1=0.5, scalar2=None, op0=ALU.mult),
               reads=[t_const], writes=[t_c2])
        CT = [t_const, t_c2]
        nlam = small[:, 0:1]
        gsub = small[:, 1:2]

        hT = sb("hT", [128, 8, SEQ], BF16)
        m2T = sb("m2T", [128, 8, SEQ], BF16)
        big = sb("big", [128, 8192], F32)
        oagT = big[:].bitcast(BF16).rearrange("p (c t) -> p c t", c=8)
        xt = [big[:, i * 1024:(i + 1) * 1024] for i in range(2)]
        xo = [big[:, 2048 + i * 1024:2048 + (i + 1) * 1024] for i in range(2)]
        scr = sb("scr", [128, 16384], BF16)
        junk = sb("junk", [128, 1024], BF16)
        et = [sb("et%d" % i, [128, 512], BF16) for i in range(3)]
        tf = [sb("tf%d" % i, [128, 512], F32) for i in range(6)]
        diag = sb("diag", [128, 31, 128], BF16)
        ssum = sb("ssum", [128, 8], F32)
        NB = 3
        wb = [sb("wb%d" % i, [128, 8, 512], BF16) for i in range(NB)]
        t_wb = [S.tok("wb%d" % i) for i in range(NB)]
        pbank = [psb("pb%d" % i, [128, 512], F32) for i in range(8)]
        t_pb = [S.tok("pb%d" % i) for i in range(8)]
        PJ, PS_, PO, PL = (0, 1), (2, 3), (4, 5), (6, 7)

        t_hT = [[S.tok("hT%d_%d" % (kc, tt)) for tt in range(4)] for kc in range(8)]
        t_m2 = [[S.tok("m2%d_%d" % (kc, tt)) for tt in range(4)] for kc in range(8)]
        t_oag = [[S.tok("oag%d_%d" % (kc, tt)) for tt in range(4)] for kc in range(8)]
        t_xt = [S.tok("xt%d" % i) for i in range(2)]
        t_xo = [S.tok("xo%d" % i) for i in range(2)]
        t_junk = S.tok("junk")
        t_et = [S.tok("et%d" % i) for i in range(3)]
        t_tf = [S.tok("tf%d" % i) for i in range(6)]
        t_diag = [S.tok("diag0"), S.tok("diag1")]
        t_ssum = S.tok("ssum")
        hT_all = [t for r in t_hT for t in r]
        print("sbuf bytes remaining:", nc.sbuf_bytes_remaining)

        def MM(out, lhsT, rhs, start, stop, reads, writes):
            S.emit("pe", lambda e: e.matmul(out, lhsT=lhsT, rhs=rhs, start=start, stop=stop), reads, writes)

        def ACT(out, in_, func, reads, writes, eng="act", **kw):
            S.emit(eng, lambda e: e.activation(out=out, in_=in_, func=func, **kw), reads, writes)

        def TS(eng, out, in0, s1, s2, op0, op1, reads, writes):
            if s2 is None:
                S.emit(eng, lambda e: e.tensor_scalar(out=out, in0=in0, scalar1=s1, scalar2=None, op0=op0), reads, writes)
            else:
                S.emit(eng, lambda e: e.tensor_scalar(out=out, in0=in0, scalar1=s1, scalar2=s2, op0=op0, op1=op1), reads, writes)

        def STT(eng, out, in0, scalar, in1, op0, op1, reads, writes):
            S.emit(eng, lambda e: e.scalar_tensor_tensor(out=out, in0=in0, scalar=scalar, in1=in1, op0=op0, op1=op1), reads, writes)

        def TT(eng, out, in0, in1, op, reads, writes):
            S.emit(eng, lambda e: e.tensor_tensor(out=out, in0=in0, in1=in1, op=op), reads, writes)

        def RSQRT(ap, n, reads, writes):
            S.emit("pool", lambda e: e.tensor_tensor(out=ap, in0=ap, in1=mhalf[:, 0:n], op=ALU.pow), reads, writes)

        class Rot:
            def __init__(self, items):
                self.items = items
                self.i = 0

            def next(self):
                v = self.items[self.i]
                self.i = (self.i + 1) % len(self.items)
                return v

        pj = Rot(list(PJ))
        ps_r = Rot(list(PS_))
        et_r = Rot([0, 1, 2])

        wstate = {"next": 0}

        def wload(slices):
            i = wstate["next"]
            wstate["next"] = (i + 1) % NB
            off = 0
            for ap in slices:
                n = ap.shape[1]
                S.dma_load("pool", t_wb[i], wb[i][:, :, off:off + n], ap.rearrange("(kc p) n -> p kc n", p=128))
                off += n
            return i

        steps = []

        def add_step(panels, fn):
            steps.append((panels, fn))

        def run_steps():
            loaded = {}
            for i, (panels, fn) in enumerate(steps):
                cur = loaded.setdefault(i, [])
                while len(cur) < len(panels):
                    cur.append(wload(panels[len(cur)]))
                if i + 1 < len(steps):
                    nxt_panels = steps[i + 1][0]
                    nxt = loaded.setdefault(i + 1, [])
                    free = NB - len(cur)
                    while len(nxt) < len(nxt_panels) and free > 0:
                        nxt.append(wload(nxt_panels[len(nxt)]))
                        free -= 1
                fn(*cur)

        def rmsnorm_T(src_rows_ap, ntb, gcol0, dst, dst_toks_fn, width):
            for tb in range(ntb):
                bi = tb % 2
                S.dma_load("sp", t_xt[bi], xt[bi], src_rows_ap[tb * 128:(tb + 1) * 128, :])
                S.emit("dve", lambda e: e.memset(ssum[:, 0:1], 0.0), [], [t_ssum])
                ACT(junk[:], xt[bi], AF.Square, [t_xt[bi], t_ssum], [t_junk, t_ssum], accum_out=ssum[:, 0:1])
                TS("dve", ssum[:, 1:2], ssum[:, 0:1], 1.0 / DM, 1e-6, ALU.mult, ALU.add, [t_ssum], [t_ssum])
                RSQRT(ssum[:, 1:2], 1, [t_ssum] + CT, [t_ssum])
                TS("dve", xo[bi], xt[bi], ssum[:, 1:2], None, ALU.mult, None, [t_xt[bi], t_ssum], [t_xo[bi]])
                for half in range(2):
                    b = ps_r.next()
                    for i in range(4):
                        kc = half * 4 + i
                        S.emit("pe", lambda e, b=b, i=i, kc=kc, bi=bi: e.transpose(
                            out=pbank[b][:, i * 128:(i + 1) * 128], in_=xo[bi][:, kc * 128:(kc + 1) * 128],
                            identity=identf[:]), [t_xo[bi]] + CT, [t_pb[b]])
                    for i in range(4):
                        kc = half * 4 + i
                        dt = dst_toks_fn(kc, tb)
                        ACT(dst[:, kc, tb * 128:(tb + 1) * 128], pbank[b][:, i * 128:(i + 1) * 128], AF.Copy,
                            [t_pb[b]] + CT, [dt], scale=cols[:, gcol0 + kc:gcol0 + kc + 1])

        qT = scr[:, 0:4096].rearrange("p (j t) -> p j t", j=2)
        kT = scr[:, 4096:8192].rearrange("p (j t) -> p j t", j=2)
        vv = scr[:, 8192:12288].rearrange("p (b n) -> p b n", b=16)
        sza = scr[:, 12288:16384].rearrange("p (j t) -> p j t", j=2)
        t_q, t_k, t_v, t_sz = S.tok("qT"), S.tok("kT"), S.tok("vv"), S.tok("sza")
        CW = SEQ + 30
        cT = [scr[:, i * 2 * CW:(i + 1) * 2 * CW].rearrange("p (j t) -> p j t", j=2) for i in range(2)]
        t_cT = [S.tok("cT0"), S.tok("cT1")]
        memT = scr[:, 0:2048].rearrange("p (c t) -> p c t", c=8)
        kmT = scr[:, 2048:4096].rearrange("p (c t) -> p c t", c=8)
        vm = scr[:, 4096:6144].rearrange("p (b n) -> p b n", b=2)
        qcT = scr[:, 6144:10240].rearrange("p (j t) -> p j t", j=2)
        szc = scr[:, 10240:14336].rearrange("p (j t) -> p j t", j=2)
        t_memT, t_kmT, t_vm, t_qc, t_szc = S.tok("memT"), S.tok("kmT"), S.tok("vm"), S.tok("qcT"), S.tok("szc")

        def proj_fm(widx, col0, tt, bank):
            for kc in range(8):
                MM(pbank[bank][:], wb[widx][:, kc, col0:col0 + 128], hT[:, kc, tt * 512:(tt + 1) * 512],
                   kc == 0, kc == 7, [t_wb[widx], t_hT[kc][tt]], [t_pb[bank]])

        def silu2_from_psum(bank, dst, dst_tok, tfi):
            ACT(tf[tfi][:], pbank[bank][:], AF.Tanh, [t_pb[bank]], [t_tf[tfi]], scale=0.5)
            STT("dve", dst, tf[tfi][:], 1.0, pbank[bank][:], ALU.add, ALU.mult, [t_tf[tfi], t_pb[bank]], [dst_tok])

        first_b = min(i for i, p in enumerate("ABC") if p in phases) if any(p in phases for p in "ABC") else 0
        def merge_step(seq, b, qd, w_o):
            def fn(widx):
                for dci in range(2):
                    dc = qd * 2 + dci
                    for tt in range(4):
                        by = pj.next()
                        for kc in range(8):
                            MM(pbank[by][:], wb[widx][:, kc, dci * 128:(dci + 1) * 128], oagT[:, kc, tt * 512:(tt + 1) * 512],
                               kc == 0, kc == 7, [t_wb[widx], t_oag[kc][tt]], [t_pb[by]])
                        bg = pj.next()
                        for kc in range(8):
                            MM(pbank[bg][:], wb[widx][:, kc, 256 + dci * 128:256 + (dci + 1) * 128], hT[:, kc, tt * 512:(tt + 1) * 512],
                               kc == 0, kc == 7, [t_wb[widx], t_hT[kc][tt]], [t_pb[bg]])
                        ACT(tf[4][:], pbank[bg][:], AF.Tanh, [t_pb[bg]] + CT, [t_tf[4]], scale=0.5,
                            bias=hbm[:, b * 8 + dc:b * 8 + dc + 1])
                        dst = m2T[:, dc, tt * 512:(tt + 1) * 512]
                        if b == first_b:
                            STT("dve", dst, tf[4][:], 1.0, pbank[by][:], ALU.add, ALU.mult, [t_tf[4], t_pb[by]], [t_m2[dc][tt]])
                        else:
                            STT("dve", tf[5][:], tf[4][:], 1.0, pbank[by][:], ALU.add, ALU.mult, [t_tf[4], t_pb[by]], [t_tf[5]])
                            TT("dve", dst, tf[5][:], dst, ALU.add, [t_tf[5], t_m2[dc][tt]], [t_m2[dc][tt]])
            gcol = 9216 + b * 1024 + qd * 256
            add_step([[w_o[:, qd * 256:(qd + 1) * 256], w_in[:, gcol:gcol + 256]]], fn)

        def attn_head(hg, j):
            h = 2 * hg + j
            for G in range(4):
                nkb = 4 * G + 4
                for c in range(2):
                    r0, r1 = c * 64, (c + 1) * 64
                    po, pl = PO[c], PL[c]
                    for kb in range(nkb):
                        jj = kb - 4 * G
                        c0 = max(jj, 0) * 128
                        W = 512 - c0
                        if jj >= 0:
                            boff, nw = 0, min(256, W)
                        elif jj == -1:
                            boff, nw = 128, 128
                        else:
                            boff, nw = 0, 0
                        bs = ps_r.next()
                        MM(pbank[bs][:, 0:W], kT[r0:r1, j, kb * 128:(kb + 1) * 128],
                           qT[r0:r1, j, G * 512 + c0:(G + 1) * 512], True, True, [t_k, t_q], [t_pb[bs]])
                        ei = et_r.next()
                        if nw > 0:
                            ti = 2 + (kb % 2)
                            TT("dve", tf[ti][:, 0:nw], pbank[bs][:, 0:nw], biasdu[:, h, boff:boff + nw], ALU.add,
                               [t_pb[bs]] + CT, [t_tf[ti]])
                            ACT(et[ei][:, 0:nw], tf[ti][:, 0:nw], AF.Exp, [t_tf[ti]], [t_et[ei]])
                        if nw < W:
                            ACT(et[ei][:, nw:W], pbank[bs][:, nw:W], AF.Exp, [t_pb[bs]] + CT, [t_et[ei]],
                                bias=cols[:, C_CFAR + h:C_CFAR + h + 1])
                        MM(pbank[po][:, c0:512], vv[:, kb, j * 128:(j + 1) * 128], et[ei][:, 0:W],
                           kb == 0, kb == nkb - 1, [t_v, t_et[ei]], [t_pb[po]])
                        MM(pbank[pl][:, c0:512], ones_b[:], et[ei][:, 0:W],
                           kb == 0, kb == nkb - 1, [t_et[ei]] + CT, [t_pb[pl]])
                for c in range(2):
                    S.emit("dve", lambda e, c=c: e.reciprocal(out=tf[c][:], in_=pbank[PL[c]][:]), [t_pb[PL[c]]], [t_tf[c]])
                    TT("dve", tf[c][:], pbank[PO[c]][:], tf[c][:], ALU.mult, [t_pb[PO[c]], t_tf[c]], [t_tf[c]])
                STT("dve", tf[0][:], tf[1][:], nlam, tf[0][:], ALU.mult, ALU.add, [t_tf[0], t_tf[1]] + CT, [t_tf[0]])
                ACT(et[0][:], tf[0][:], AF.Square, [t_tf[0]], [t_et[0]])
                bq = pj.next()
                MM(pbank[bq][:], ones128[:], et[0][:], True, True, [t_et[0]] + CT, [t_pb[bq]])
                TS("dve", tf[1][:], pbank[bq][:], 1e-6, None, ALU.add, None, [t_pb[bq]], [t_tf[1]])
                RSQRT(tf[1][:], 512, [t_tf[1]] + CT, [t_tf[1]])
                STT("dve", tf[0][:], tf[0][:], gsub, tf[1][:], ALU.mult, ALU.mult, [t_tf[0], t_tf[1]] + CT, [t_tf[0]])
                TT("dve", oagT[:, h, G * 512:(G + 1) * 512], tf[0][:], sza[:, j, G * 512:(G + 1) * 512], ALU.mult,
                   [t_tf[0], t_sz], [t_oag[h][G]])

        def phase_A(seq):
            for hg in range(4):
                def fq(widx, hg=hg):
                    for j in range(2):
                        for tt in range(4):
                            b = pj.next()
                            proj_fm(widx, j * 128, tt, b)
                            ACT(qT[:, j, tt * 512:(tt + 1) * 512], pbank[b][:], AF.Copy, [t_pb[b]], [t_q], scale=0.125)
                add_step([[w_in[:, hg * 256:(hg + 1) * 256]]], fq)

                def fk(widx, hg=hg):
                    for j in range(2):
                        for tt in range(4):
                            b = pj.next()
                            proj_fm(widx, j * 128, tt, b)
                            S.emit("dve", lambda e, b=b, j=j, tt=tt: e.tensor_copy(out=kT[:, j, tt * 512:(tt + 1) * 512], in_=pbank[b][:]),
                                   [t_pb[b]], [t_k])
                add_step([[w_in[:, 1024 + hg * 256:1024 + (hg + 1) * 256]]], fk)

                def fv(widx, hg=hg):
                    for tb in range(16):
                        b = pj.next()
                        tt = tb // 4
                        for kc in range(8):
                            MM(pbank[b][:, 0:256], hT[:, kc, tb * 128:(tb + 1) * 128], wb[widx][:, kc, 0:256],
                               kc == 0, kc == 7, [t_wb[widx], t_hT[kc][tt]], [t_pb[b]])
                        if tb % 2 == 0:
                            ACT(vv[:, tb, :], pbank[b][:, 0:256], AF.Copy, [t_pb[b]], [t_v])
                        else:
                            S.emit("dve", lambda e, b=b, tb=tb: e.tensor_copy(out=vv[:, tb, :], in_=pbank[b][:, 0:256]), [t_pb[b]], [t_v])
                add_step([[w_in[:, 2048 + hg * 256:2048 + (hg + 1) * 256]]], fv)

                def fz(widx, hg=hg):
                    for j in range(2):
                        for tt in range(4):
                            b = pj.next()
                            proj_fm(widx, j * 128, tt, b)
                            silu2_from_psum(b, sza[:, j, tt * 512:(tt + 1) * 512], t_sz, 4 + (tt % 2))
                add_step([[w_in[:, 3072 + hg * 256:3072 + (hg + 1) * 256]]], fz)

                def fa(hg=hg):
                    for j in range(2):
                        attn_head(hg, j)
                add_step([], fa)
            for qd in range(4):
                merge_step(seq, 0, qd, w_oa)

        def phase_B(seq):
            def bar():
                S.barrier(S.COMPUTE)
            add_step([], bar)
            for qd in range(4):
                def fc(widx, qd=qd):
                    ci = qd % 2
                    S.emit("dve", lambda e, ci=ci: e.memset(cT[ci][:, :, 0:30], 0.0), [], [t_cT[ci]])
                    for dci in range(2):
                        dc = qd * 2 + dci
                        for tt in range(4):
                            ba = pj.next()
                            proj_fm(widx, dci * 128, tt, ba)
                            bg = pj.next()
                            proj_fm(widx, 256 + dci * 128, tt, bg)
                            ACT(tf[4][:], pbank[bg][:], AF.Tanh, [t_pb[bg]], [t_tf[4]], scale=0.5)
                            STT("dve", cT[ci][:, dci, 30 + tt * 512:30 + (tt + 1) * 512], tf[4][:], 1.0, pbank[ba][:],
                                ALU.add, ALU.mult, [t_tf[4], t_pb[ba]], [t_cT[ci]])
                    for dci in range(2):
                        dc = qd * 2 + dci
                        for jt in range(31):
                            if jt % 2 == 0:
                                ACT(diag[:, jt, :], identf[:], AF.Copy, CT, [t_diag[0]], scale=convw[:, dc, jt:jt + 1])
                            else:
                                TS("dve", diag[:, jt, :], identf[:], convw[:, dc, jt:jt + 1], None, ALU.mult, None, CT, [t_diag[1]])
                        for tt in range(4):
                            b = pj.next()
                            for jt in range(31):
                                MM(pbank[b][:], diag[:, jt, :], cT[ci][:, dci, tt * 512 + jt:tt * 512 + jt + 512],
                                   jt == 0, jt == 30, [t_diag[jt % 2], t_cT[ci]], [t_pb[b]])
                            ACT(oagT[:, dc, tt * 512:(tt + 1) * 512], pbank[b][:], AF.Identity, [t_pb[b]] + CT, [t_oag[dc][tt]],
                                bias=cols[:, C_CB + dc:C_CB + dc + 1])
                add_step([[w_in[:, 4096 + qd * 256:4096 + (qd + 1) * 256], w_in[:, 5120 + qd * 256:5120 + (qd + 1) * 256]]], fc)

            def f2(w0, w1):
                wz = (w0, w1)
                for tt in range(4):
                    sl = slice(tt * 512, (tt + 1) * 512)
                    bmu, bms = PO[0], PO[1]
                    for dc in range(8):
                        MM(pbank[bmu][:], ones1024[:], oagT[:, dc, sl], dc == 0, dc == 7, [t_oag[dc][tt]] + CT, [t_pb[bmu]])
                    for dc in range(8):
                        ei = et_r.next()
                        ACT(et[ei][:], oagT[:, dc, sl], AF.Square, [t_oag[dc][tt]], [t_et[ei]])
                        MM(pbank[bms][:], ones1024[:], et[ei][:], dc == 0, dc == 7, [t_et[ei]] + CT, [t_pb[bms]])
                    ACT(tf[0][:], pbank[bmu][:], AF.Copy, [t_pb[bmu]], [t_tf[0]])
                    TT("dve", tf[1][:], tf[0][:], tf[0][:], ALU.mult, [t_tf[0]], [t_tf[1]])
                    TT("dve", tf[1][:], pbank[bms][:], tf[1][:], ALU.subtract, [t_pb[bms], t_tf[1]], [t_tf[1]])
                    TS("dve", tf[1][:], tf[1][:], 1e-6, None, ALU.add, None, [t_tf[1]], [t_tf[1]])
                    RSQRT(tf[1][:], 512, [t_tf[1]] + CT, [t_tf[1]])
                    STT("dve", tf[0][:], tf[0][:], -1.0, tf[1][:], ALU.mult, ALU.mult, [t_tf[0], t_tf[1]], [t_tf[0]])
                    for dc in range(8):
                        TT("dve", tf[2][:], oagT[:, dc, sl], tf[1][:], ALU.mult, [t_oag[dc][tt], t_tf[1]], [t_tf[2]])
                        TT("dve", tf[2][:], tf[2][:], tf[0][:], ALU.add, [t_tf[2], t_tf[0]], [t_tf[2]])
                        ACT(tf[3][:], tf[2][:], AF.Tanh, [t_tf[2]] + CT, [t_tf[3]], scale=hlng[:, dc:dc + 1], bias=hlnb[:, dc:dc + 1])
                        ACT(tf[2][:], tf[2][:], AF.Identity, [t_tf[2]] + CT, [t_tf[2]], scale=cols[:, C_LNG + dc:C_LNG + dc + 1],
                            bias=cols[:, C_LNB + dc:C_LNB + dc + 1])
                        STT("dve", tf[2][:], tf[3][:], 1.0, tf[2][:], ALU.add, ALU.mult, [t_tf[3], t_tf[2]], [t_tf[2]])
                        bz = pj.next()
                        proj_fm(wz[dc // 4], (dc % 4) * 128, tt, bz)
                        ACT(tf[4][:], pbank[bz][:], AF.Tanh, [t_pb[bz]], [t_tf[4]], scale=0.5)
                        STT("dve", tf[5][:], tf[4][:], 1.0, pbank[bz][:], ALU.add, ALU.mult, [t_tf[4], t_pb[bz]], [t_tf[5]])
                        STT("dve", oagT[:, dc, sl], tf[2][:], 0.25, tf[5][:], ALU.mult, ALU.mult, [t_tf[2], t_tf[5]], [t_oag[dc][tt]])
            add_step([[w_in[:, 6144:6656]], [w_in[:, 6656:7168]]], f2)
            for qd in range(4):
                merge_step(seq, 1, qd, w_ob)

        def phase_C(seq):
            def fmem():
                S.barrier(S.ENGS, dma_toks=t_xt + t_xo)
                rmsnorm_T(mem_d[seq * 256:(seq + 1) * 256, :], 2, C_GMEM, memT, lambda kc, tb: t_memT, 256)
                S.barrier(S.COMPUTE, dma_toks=t_xt + t_xo)
            add_step([], fmem)
            for p in range(2):
                def fkk(widx, p=p):
                    for ci in range(4):
                        b = pj.next()
                        for kc in range(8):
                            MM(pbank[b][:, 0:256], wb[widx][:, kc, ci * 128:(ci + 1) * 128], memT[:, kc, :], kc == 0, kc == 7,
                               [t_wb[widx], t_memT], [t_pb[b]])
                        ACT(kmT[:, p * 4 + ci, :], pbank[b][:, 0:256], AF.Copy, [t_pb[b]], [t_kmT], scale=1.0 / 16)
                add_step([[w_kv[:, p * 512:(p + 1) * 512]]], fkk)
            for p in range(2):
                def fvv(widx, p=p):
                    for kb in range(2):
                        b = pj.next()
                        for kc in range(8):
                            MM(pbank[b][:], memT[:, kc, kb * 128:(kb + 1) * 128], wb[widx][:, kc, :], kc == 0, kc == 7,
                               [t_wb[widx], t_memT], [t_pb[b]])
                        ACT(vm[:, kb, p * 512:(p + 1) * 512], pbank[b][:], AF.Copy, [t_pb[b]], [t_vm])
                add_step([[w_kv[:, 1024 + p * 512:1024 + (p + 1) * 512]]], fvv)
            for hx in range(4):
                def fx(widx, hx=hx):
                    for dci in range(2):
                        for tt in range(4):
                            b = pj.next()
                            proj_fm(widx, dci * 128, tt, b)
                            ACT(qcT[:, dci, tt * 512:(tt + 1) * 512], pbank[b][:], AF.Copy, [t_pb[b]], [t_qc])
                            b2 = pj.next()
                            proj_fm(widx, 256 + dci * 128, tt, b2)
                            silu2_from_psum(b2, szc[:, dci, tt * 512:(tt + 1) * 512], t_szc, 4 + (tt % 2))
                    for G in range(4):
                        sl = slice(G * 512, (G + 1) * 512)
                        eis = []
                        for kb in range(2):
                            bs = ps_r.next()
                            for dci in range(2):
                                MM(pbank[bs][:], kmT[:, hx * 2 + dci, kb * 128:(kb + 1) * 128], qcT[:, dci, sl], dci == 0, dci == 1,
                                   [t_kmT, t_qc], [t_pb[bs]])
                            ei = et_r.next()
                            ACT(et[ei][:], pbank[bs][:], AF.Exp, [t_pb[bs]], [t_et[ei]])
                            eis.append(ei)
                        for ec in range(2):
                            for kb in range(2):
                                MM(pbank[PO[ec]][:], vm[:, kb, hx * 256 + ec * 128:hx * 256 + (ec + 1) * 128], et[eis[kb]][:],
                                   kb == 0, kb == 1, [t_vm, t_et[eis[kb]]], [t_pb[PO[ec]]])
                        for kb in range(2):
                            MM(pbank[PL[0]][:], ones_b[:], et[eis[kb]][:], kb == 0, kb == 1, [t_et[eis[kb]]] + CT, [t_pb[PL[0]]])
                        S.emit("dve", lambda e: e.reciprocal(out=tf[0][:], in_=pbank[PL[0]][:]), [t_pb[PL[0]]], [t_tf[0]])
                        for ec in range(2):
                            TT("dve", tf[1 + ec][:], pbank[PO[ec]][:], tf[0][:], ALU.mult, [t_pb[PO[ec]], t_tf[0]], [t_tf[1 + ec]])
                            STT("dve", oagT[:, hx * 2 + ec, sl], tf[1 + ec][:], 0.5, szc[:, ec, sl], ALU.mult, ALU.mult,
                                [t_tf[1 + ec], t_szc], [t_oag[hx * 2 + ec][G]])
                add_step([[w_in[:, 7168 + hx * 256:7168 + (hx + 1) * 256], w_in[:, 8192 + hx * 256:8192 + (hx + 1) * 256]]], fx)
            for qd in range(4):
                merge_step(seq, 2, qd, w_oc)

        def phase_D(seq):
            def fd(w0, w1):
                S.barrier(S.ENGS, dma_toks=[])
                wz = (w0, w1)
                for tb in range(16):
                    bi = tb % 2
                    tt = tb // 4
                    S.dma_load("sp", t_xt[bi], xt[bi], x_d[seq * SEQ + tb * 128:seq * SEQ + (tb + 1) * 128, :])
                    banks = (PO[bi], PL[bi])
                    for hf in range(2):
                        for kc in range(8):
                            MM(pbank[banks[hf]][:], m2T[:, kc, tb * 128:(tb + 1) * 128], wb[wz[hf]][:, kc, :], kc == 0, kc == 7,
                               [t_wb[wz[hf]], t_m2[kc][tt]], [t_pb[banks[hf]]])
                    S.emit("dve", lambda e: e.memset(ssum[:, 2:4], 0.0), [], [t_ssum])
                    for hf in range(2):
                        ACT(junk[:, 0:512], pbank[banks[hf]][:], AF.Square, [t_pb[banks[hf]], t_ssum], [t_junk, t_ssum],
                            accum_out=ssum[:, 2 + hf:3 + hf])
                    TT("dve", ssum[:, 4:5], ssum[:, 2:3], ssum[:, 3:4], ALU.add, [t_ssum], [t_ssum])
                    TS("dve", ssum[:, 4:5], ssum[:, 4:5], 0.25 / DM, 1e-6, ALU.mult, ALU.add, [t_ssum], [t_ssum])
                    RSQRT(ssum[:, 4:5], 1, [t_ssum] + CT, [t_ssum])
                    TS("dve", ssum[:, 5:6], ssum[:, 4:5], 0.5, None, ALU.mult, None, [t_ssum], [t_ssum])
                    for hf in range(2):
                        cs = slice(hf * 512, (hf + 1) * 512)
                        STT("dve", xo[bi][:, cs], pbank[banks[hf]][:], ssum[:, 5:6], gpost[:, cs], ALU.mult, ALU.mult,
                            [t_pb[banks[hf]], t_ssum] + CT, [t_xo[bi]])
                        TT("dve", xo[bi][:, cs], xo[bi][:, cs], xt[bi][:, cs], ALU.add, [t_xo[bi], t_xt[bi]], [t_xo[bi]])
                    S.dma_store("sp", t_xo[bi], out_d[seq * SEQ + tb * 128:seq * SEQ + (tb + 1) * 128, :], xo[bi])
            add_step([[w_out[:, 0:512]], [w_out[:, 512:1024]]], fd)

        for seq in range(NSEQ):
            def f0(seq=seq):
                S.barrier(S.ENGS, dma_toks=t_xt + t_xo)
                rmsnorm_T(x_d[seq * SEQ:(seq + 1) * SEQ, :], 16, C_GPRE, hT, lambda kc, tb: t_hT[kc][tb // 4], SEQ)
                S.barrier(S.COMPUTE, dma_toks=t_xt + t_xo)
            add_step([], f0)
            if "A" in phases:
                phase_A(seq)
            if "B" in phases:
                phase_B(seq)
            if "C" in phases:
                phase_C(seq)
            if debug and seq == 0:
                def fdbg():
                    t_dbg = S.tok("dbg")
                    for name in debug:
                        src = {"hT": hT, "m2T": m2T, "oagT": oagT}[name]
                        rd = hT_all if name == "hT" else [t for r in (t_m2 if name == "m2T" else t_oag) for t in r]
                        for kc in range(8):
                            for tt in range(4):
                                S.emit("dve", lambda e, kc=kc, tt=tt, src=src: e.tensor_copy(out=tf[0][:], in_=src[:, kc, tt * 512:(tt + 1) * 512]),
                                       rd, [t_tf[0]])
                                S.dma_store("sp", t_tf[0], dbg_d[name][kc * 128:(kc + 1) * 128, tt * 512:(tt + 1) * 512], tf[0][:])
                add_step([], fdbg)
            if "D" in phases:
                phase_D(seq)
        if phases != "0ABCD":
            def ftouch():
                t_touch = S.tok("touch")
                for ap in (mem_d, w_in, w_oa, w_ob, w_oc, w_out, w_kv):
                    S.dma_load("sp", t_touch, ssum[0:1, 0:8], ap[0:1, 0:8])
                S.emit("dve", lambda e: e.memset(tf[1][:], 0.0), [], [t_tf[1]])
                S.dma_store("sp", t_tf[1], out_d[0:128, 0:512], tf[1][:])
            add_step([], ftouch)
        run_steps()
        S.build()
        print("instr counts", {e: len(S.prog[e]) for e in S.ENGS}, "signals", S.sigcount, "waits", S.nwaits)
    return nc


def _t5_bucket_host(rel):
    nb, max_exact = 16, 8
    try:
        import jax
        import jax.numpy as jnp
        cpu = jax.devices("cpu")[0]
        with jax.default_device(cpu):
            r = jnp.asarray(rel, dtype=jnp.int32)
            ret = (r > 0).astype(jnp.int32) * nb
            n = jnp.abs(r)
            nf = jnp.maximum(n, 1).astype(jnp.float32)
            large = max_exact + (jnp.log(nf / max_exact) / math.log(128 / max_exact) * (nb - max_exact)).astype(jnp.int32)
            large = jnp.minimum(large, nb - 1)
            return np.asarray(ret + jnp.where(n < max_exact, n, large))
    except Exception:
        ret = (rel > 0).astype(np.int32) * nb
        n = np.abs(rel)
        nf = np.maximum(n, 1).astype(np.float32)
        large = max_exact + (np.log(nf / np.float32(max_exact)) / np.float32(math.log(128 / max_exact))
                             * np.float32(nb - max_exact)).astype(np.int32)
        large = np.minimum(large, nb - 1)
        return ret + np.where(n < max_exact, n, large)


def _colvec(v):
    v = np.asarray(v, np.float32).reshape(-1)
    return np.ascontiguousarray(v.reshape(-1, 128).T)


def prep_shared(rel_bias, g_pre, g_mem, w_in, b_merge, lam_q1, lam_k1, lam_q2, lam_k2, g_subln, w_oa, conv_w,
                conv_b, ln_g, ln_b, w_ob, w_mem_kv, w_oc, w_out, g_post):
    f = lambda a: np.ascontiguousarray(np.asarray(a, np.float32))
    rel_bias = f(rel_bias)
    cols = np.zeros((128, NCOLS), np.float32)
    cols[:, C_GPRE:C_GPRE + 8] = _colvec(g_pre[0])
    cols[:, C_GMEM:C_GMEM + 8] = _colvec(g_mem[0])
    cols[:, C_BM:C_BM + 24] = _colvec(b_merge[0])
    cols[:, C_CB:C_CB + 8] = _colvec(conv_b[0])
    cols[:, C_LNG:C_LNG + 8] = _colvec(ln_g[0])
    cols[:, C_LNB:C_LNB + 8] = _colvec(ln_b[0])
    cols[:, C_GSUB:C_GSUB + 1] = _colvec(g_subln[0])
    far_b = int(_t5_bucket_host(np.array([-1000], np.int32))[0])
    cols[:, C_CFAR:C_CFAR + 8] = np.broadcast_to(rel_bias[far_b][None, :], (128, 8))
    cw = f(conv_w)[0]
    convw = np.ascontiguousarray(cw.reshape(31, 8, 128).transpose(2, 1, 0)).reshape(128, 8 * 31)
    kk = np.arange(128, dtype=np.int32)[:, None]
    qq = np.arange(128, dtype=np.int32)[None, :]
    relD = kk - qq
    relU = kk - qq - 128
    bD = _t5_bucket_host(relD)
    bU = _t5_bucket_host(relU)
    allowed = (kk // 64) <= (qq // 64)
    ext = np.concatenate([rel_bias, np.full((1, 8), NEG, np.float32)], axis=0)
    idxD = np.where(allowed, bD, 32)
    D = ext[idxD]
    U = ext[bU]
    biasdu = np.ascontiguousarray(np.concatenate([D, U], axis=1).transpose(0, 2, 1)).reshape(128, 8 * 256)
    lamv = np.concatenate([f(lam_q1)[0], f(lam_k1)[0], f(lam_q2)[0], f(lam_k2)[0]]).astype(np.float32)
    return {
        "w_in": f(w_in)[0], "w_oa": f(w_oa)[0], "w_ob": f(w_ob)[0], "w_oc": f(w_oc)[0], "w_out": f(w_out)[0],
        "w_mem_kv": f(w_mem_kv)[0], "cols": cols, "convw": convw, "g_post": f(g_post)[0], "lamv": lamv,
        "biasdu": biasdu.astype(np.float32),
    }


_PROG = {}


def kernel(x, mem, rel_bias, g_pre, g_mem, w_in, b_merge, lam_q1, lam_k1, lam_q2, lam_k2, g_subln, w_oa, conv_w,
           conv_b, ln_g, ln_b, w_ob, w_mem_kv, w_oc, w_out, g_post):
    shared = prep_shared(rel_bias, g_pre, g_mem, w_in, b_merge, lam_q1, lam_k1, lam_q2, lam_k2, g_subln, w_oa,
                         conv_w, conv_b, ln_g, ln_b, w_ob, w_mem_kv, w_oc, w_out, g_post)
    x = np.asarray(x, np.float32)
    mem = np.asarray(mem, np.float32)
    B = x.shape[0]
    per = B // NCORES
    if "p" not in _PROG:
        _PROG["p"] = build_program(NSEQ=per)
    nc = _PROG["p"]
    in_maps = []
    for c in range(NCORES):
        m = dict(shared)
        m["x"] = np.ascontiguousarray(x[c * per:(c + 1) * per].reshape(per * SEQ, DM))
        m["mem"] = np.ascontiguousarray(mem[c * per:(c + 1) * per].reshape(per * 256, DM))
        in_maps.append(m)
    res = run_bass_kernel_spmd(nc, in_maps, core_ids=list(range(NCORES)))
    outs = [np.asarray(r["out"], np.float32).reshape(per, SEQ, DM) for r in res.results]
    return np.concatenate(outs, axis=0)
```

```python
import math
from contextlib import ExitStack
import numpy as np
import concourse.bass as bass
import concourse.mybir as mybir
from concourse.bass_utils import run_bass_kernel_spmd

F32 = mybir.dt.float32
BF16 = mybir.dt.bfloat16
AF = mybir.ActivationFunctionType
ALU = mybir.AluOpType
AX = mybir.AxisListType

SAME_ENGINE_SYNC = True
EPOCH = 20000
NEG = -30000.0
SEQ = 2048
DM = 1024
NCORES = 8


class Tk:
    __slots__ = ("name", "lw", "rd", "dsem", "dcnt")

    def __init__(self, name):
        self.name = name
        self.lw = None
        self.rd = {}
        self.dsem = None
        self.dcnt = 0


class Ins:
    __slots__ = ("eng", "fn", "cdeps", "raw", "dwaits", "sig", "val", "dma", "waits")

    def __init__(self, eng, fn):
        self.eng = eng
        self.fn = fn
        self.cdeps = set()
        self.raw = set()
        self.dwaits = {}
        self.sig = False
        self.val = None
        self.dma = None
        self.waits = []


class Sched:
    ENGS = ("pe", "act", "dve", "pool", "sp")
    COMPUTE = ("pe", "act", "dve", "pool")

    def __init__(self, nc, stack):
        self.nc = nc
        self.stack = stack
        self.prog = {e: [] for e in self.ENGS}
        self.esem = {e: [] for e in self.ENGS}
        self.toks = []
        self.last_compute = {e: None for e in self.ENGS}
        self.pending = {e: (set(), {}) for e in self.ENGS}

    def tok(self, name):
        t = Tk(name)
        self.toks.append(t)
        return t

    def _dep(self, ins, p, raw):
        if p is None or p is ins:
            return
        if p.dma is not None:
            t = p.dma
            ins.dwaits[t] = max(ins.dwaits.get(t, 0), 16 * t.dcnt)
        else:
            ins.cdeps.add(p)
            if raw:
                ins.raw.add(p)

    def emit(self, eng, fn, reads=(), writes=(), dma=None):
        ins = Ins(eng, fn)
        pc, pd = self.pending[eng]
        if pc or pd:
            for p in pc:
                if p.eng != eng:
                    ins.cdeps.add(p)
            for t, v in pd.items():
                ins.dwaits[t] = max(ins.dwaits.get(t, 0), v)
            self.pending[eng] = (set(), {})
        for t in reads:
            self._dep(ins, t.lw, True)
        for t in writes:
            self._dep(ins, t.lw, False)
            for r in t.rd.values():
                self._dep(ins, r, False)
        if dma is not None:
            if dma.dsem is None:
                dma.dsem = self.stack.enter_context(self.nc.semaphore("ds_" + dma.name))
            dma.dcnt += 1
            ins.dma = dma
        else:
            self.last_compute[eng] = ins
        for t in reads:
            key = eng if dma is None else ("dma", id(ins))
            t.rd[key] = ins
        for t in writes:
            t.lw = ins
            t.rd = {}
        self.prog[eng].append(ins)
        return ins

    def barrier(self, engs, dma_toks=()):
        lasts = set(self.last_compute[e] for e in self.COMPUTE if self.last_compute[e] is not None)
        for e in engs:
            pc, pd = self.pending[e]
            pc |= lasts
            for t in dma_toks:
                if t.dsem is not None:
                    pd[t] = max(pd.get(t, 0), 16 * t.dcnt)

    def dma_load(self, q, tok, out_ap, in_ap, **kw):
        return self.emit(q, lambda e: e.dma_start(out=out_ap, in_=in_ap, **kw), writes=[tok], dma=tok)

    def dma_store(self, q, tok, out_ap, in_ap, **kw):
        return self.emit(q, lambda e: e.dma_start(out=out_ap, in_=in_ap, **kw), reads=[tok], dma=tok)

    def _needs(self, ins, p):
        if p.eng != ins.eng:
            return True
        return SAME_ENGINE_SYNC and p.eng != "pe"

    def finalize(self):
        for e in self.ENGS:
            for ins in self.prog[e]:
                for p in ins.cdeps:
                    if self._needs(ins, p):
                        p.sig = True
        for e in self.ENGS:
            c = 0
            for ins in self.prog[e]:
                if ins.sig:
                    ep, v = divmod(c, EPOCH)
                    if ep >= len(self.esem[e]):
                        self.esem[e].append(self.stack.enter_context(self.nc.semaphore("es_%s%d" % (e, ep))))
                    ins.val = (ep, v + 1)
                    c += 1
        self.sigcount = {e: sum(1 for i in self.prog[e] if i.sig) for e in self.ENGS}
        self.nwaits = 0
        for e in self.ENGS:
            known = {}
            for ins in self.prog[e]:
                need = {}
                for p in ins.cdeps:
                    if not self._needs(ins, p):
                        continue
                    need[p.eng] = max(need.get(p.eng, (0, 0)), p.val)
                for pe_, v in need.items():
                    if known.get(pe_, (0, 0)) < v:
                        ins.waits.append((self.esem[pe_][v[0]], v[1]))
                        known[pe_] = v
                for t, v in ins.dwaits.items():
                    k = ("d", id(t))
                    if known.get(k, 0) < v:
                        ins.waits.append((t.dsem, v))
                        known[k] = v
                self.nwaits += len(ins.waits)

    def simulate(self):
        sem = {}
        pc = {e: 0 for e in self.ENGS}
        progress = True
        while progress:
            progress = False
            for e in self.ENGS:
                while pc[e] < len(self.prog[e]):
                    ins = self.prog[e][pc[e]]
                    if all(sem.get(id(s), 0) >= v for (s, v) in ins.waits):
                        if ins.dma is not None:
                            sem[id(ins.dma.dsem)] = sem.get(id(ins.dma.dsem), 0) + 16
                        elif ins.sig:
                            k = id(self.esem[e][ins.val[0]])
                            sem[k] = sem.get(k, 0) + 1
                        pc[e] += 1
                        progress = True
                    else:
                        break
        stuck = {e: (pc[e], len(self.prog[e])) for e in self.ENGS if pc[e] < len(self.prog[e])}
        if stuck:
            for e in stuck:
                ins = self.prog[e][pc[e]]
                print("STUCK", e, pc[e], [(s.name, v, sem.get(id(s), 0)) for (s, v) in ins.waits])
            raise RuntimeError("deadlock in generated program: %s" % stuck)

    def run_engine(self, ename, e):
        for ins in self.prog[ename]:
            for (s, v) in ins.waits:
                e.wait_ge(s, v)
            bi = ins.fn(e)
            if ins.dma is not None:
                bi.then_inc(ins.dma.dsem, 16)
            elif ins.sig:
                bi.then_inc(self.esem[ename][ins.val[0]], 1)

    def build(self):
        fin = {}
        for t in self.toks:
            if t.dsem is not None:
                fin[t] = 16 * t.dcnt
        self.finalize()
        self.simulate()
        nc = self.nc
        with nc.Block() as block:
            @block.tensor
            def _(e):
                self.run_engine("pe", e)

            @block.scalar
            def _(e):
                self.run_engine("act", e)

            @block.vector
            def _(e):
                self.run_engine("dve", e)

            @block.gpsimd
            def _(e):
                self.run_engine("pool", e)

            @block.sync
            def _(e):
                self.run_engine("sp", e)
                for t, v in fin.items():
                    e.wait_ge(t.dsem, v)


NCOLS = 80
C_GPRE, C_GMEM, C_BM, C_CB, C_LNG, C_LNB, C_GSUB, C_CFAR = 0, 8, 16, 40, 48, 56, 64, 65


def build_program(NSEQ=2, debug=None, phases="0ABCD"):
    nc = bass.Bass("TRN2", target_bir_lowering=False)
    dram = lambda n, shp, kind="ExternalInput": nc.dram_tensor(n, shp, F32, kind=kind).ap()
    x_d = dram("x", [NSEQ * SEQ, DM])
    mem_d = dram("mem", [NSEQ * 256, DM])
    w_in = dram("w_in", [DM, 12288])
    w_oa = dram("w_oa", [DM, DM])
    w_ob = dram("w_ob", [DM, DM])
    w_oc = dram("w_oc", [DM, DM])
    w_out = dram("w_out", [DM, DM])
    w_kv = dram("w_mem_kv", [DM, 2048])
    cols_d = dram("cols", [128, NCOLS])
    convw_d = dram("convw", [128, 8 * 31])
    gpost_d = dram("g_post", [DM])
    lamv_d = dram("lamv", [256])
    bias_d = dram("biasdu", [128, 8 * 256])
    out_d = dram("out", [NSEQ * SEQ, DM], kind="ExternalOutput")
    dbg_d = {}
    if debug:
        for name, shp in debug.items():
            dbg_d[name] = dram("dbg_" + name, shp, kind="ExternalOutput")

    with ExitStack() as st:
        S = Sched(nc, st)
        sb = lambda name, shape, dt: st.enter_context(nc.sbuf_tensor("s_" + name, shape, dt))
        psb = lambda name, shape, dt: st.enter_context(nc.psum_tensor("p_" + name, shape, dt))

        cols = sb("cols", [128, NCOLS], F32)
        convw = sb("convw", [128, 8, 31], F32)
        gpost = sb("gpost", [128, DM], F32)
        lamv = sb("lamv", [128, 256], F32)
        biasdu = sb("biasdu", [128, 8, 256], F32)
        identf = sb("identf", [128, 128], F32)
        identb = sb("identb", [128, 128], BF16)
        ones_b = sb("ones_b", [128, 128], BF16)
        ones128 = sb("ones128", [128, 128], BF16)
        ones1024 = sb("ones1024", [128, 128], BF16)
        mhalf = sb("mhalf", [128, 512], F32)
        small = sb("small", [128, 16], F32)
        hbm = sb("hbm", [128, 24], F32)
        hlng = sb("hlng", [128, 8], F32)
        hlnb = sb("hlnb", [128, 8], F32)
        t_const = S.tok("const")
        S.dma_load("sp", t_const, cols[:], cols_d)
        S.dma_load("sp", t_const, convw[:], convw_d.rearrange("p (c j) -> p c j", j=31))
        S.dma_load("sp", t_const, gpost[:], gpost_d.partition_broadcast(128))
        S.dma_load("sp", t_const, lamv[:], lamv_d.partition_broadcast(128))
        S.dma_load("sp", t_const, biasdu[:], bias_d.rearrange("p (h k) -> p h k", k=256))
        t_c2 = S.tok("const2")
        S.emit("pool", lambda e: e.memset(identf[:], 0.0), writes=[t_c2])
        S.emit("pool", lambda e: e.affine_select(out=identf[:], in_=identf[:], pattern=[[-1, 128]],
                                                 compare_op=ALU.not_equal, fill=1.0, base=0,
                                                 channel_multiplier=1), reads=[t_c2], writes=[t_c2])
        S.emit("pool", lambda e: e.memset(mhalf[:], -0.5), writes=[t_c2])
        S.emit("dve", lambda e: e.tensor_copy(out=identb[:], in_=identf[:]), reads=[t_c2], writes=[t_c2])
        S.emit("dve", lambda e: e.memset(ones_b[:], 1.0), writes=[t_c2])
        S.emit("dve", lambda e: e.memset(ones128[:], 1.0 / 128), writes=[t_c2])
        S.emit("dve", lambda e: e.memset(ones1024[:], 1.0 / 1024), writes=[t_c2])
        lam_init = 0.8 - 0.6 * math.exp(-0.3 * 0)
        S.emit("dve", lambda e: e.tensor_tensor(out=lamv[:, 0:64], in0=lamv[:, 0:64], in1=lamv[:, 64:128], op=ALU.mult),
               reads=[t_const], writes=[t_c2])
        S.emit("dve", lambda e: e.tensor_tensor(out=lamv[:, 128:192], in0=lamv[:, 128:192], in1=lamv[:, 192:256], op=ALU.mult),
               reads=[t_const], writes=[t_c2])
        S.emit("dve", lambda e: e.reduce_sum(out=small[:, 2:3], in_=lamv[:, 0:64], axis=AX.X), reads=[t_c2], writes=[t_c2])
        S.emit("dve", lambda e: e.reduce_sum(out=small[:, 3:4], in_=lamv[:, 128:192], axis=AX.X), reads=[t_c2], writes=[t_c2])
        S.emit("act", lambda e: e.activation(out=small[:, 4:6], in_=small[:, 2:4], func=AF.Exp), reads=[t_c2], writes=[t_c2])
        S.emit("dve", lambda e: e.tensor_tensor(out=small[:, 0:1], in0=small[:, 5:6], in1=small[:, 4:5], op=ALU.subtract),
               reads=[t_c2], writes=[t_c2])
        S.emit("dve", lambda e: e.tensor_scalar(out=small[:, 0:1], in0=small[:, 0:1], scalar1=-lam_init, scalar2=None, op0=ALU.add),
               reads=[t_c2], writes=[t_c2])
        S.emit("dve", lambda e: e.tensor_scalar(out=small[:, 1:2], in0=cols[:, C_GSUB:C_GSUB + 1], scalar1=0.5 * (1.0 - lam_init),
                                                scalar2=None, op0=ALU.mult), reads=[t_const], writes=[t_c2])
        S.emit("dve", lambda e: e.tensor_scalar(out=hbm[:], in0=cols[:, C_BM:C_BM + 24], scalar1=0.5, scalar2=None, op0=ALU.mult),
               reads=[t_const], writes=[t_c2])
        S.emit("dve", lambda e: e.tensor_scalar(out=hlng[:], in0=cols[:, C_LNG:C_LNG + 8], scalar1=0.5, scalar2=None, op0=ALU.mult),
               reads=[t_const], writes=[t_c2])
        S.emit("dve", lambda e: e.tensor_scalar(out=hlnb[:], in0=cols[:, C_LNB:C_LNB + 8], scalar1=0.5, scalar2=None, op0=ALU.mult),
               reads=[t_const], writes=[t_c2])
        S.emit("dve", lambda e: e.tensor_scalar(out=convw[:], in0=convw[:], scalar1=0.5, scalar2=None, op0=ALU.mult),
               reads=[t_const], writes=[t_c2])
        CT = [t_const, t_c2]
        nlam = small[:, 0:1]
        gsub = small[:, 1:2]

        hT = sb("hT", [128, 8, SEQ], BF16)
        m2T = sb("m2T", [128, 8, SEQ], BF16)
        big = sb("big", [128, 8192], F32)
        oagT = big[:].bitcast(BF16).rearrange("p (c t) -> p c t", c=8)
        xt = [big[:, i * 1024:(i + 1) * 1024] for i in range(2)]
        xo = [big[:, 2048 + i * 1024:2048 + (i + 1) * 1024] for i in range(2)]
        scr = sb("scr", [128, 16384], BF16)
        junk = sb("junk", [128, 1024], BF16)
        et = [sb("et%d" % i, [128, 512], BF16) for i in range(3)]
        tf = [sb("tf%d" % i, [128, 512], F32) for i in range(6)]
        diag = sb("diag", [128, 31, 128], BF16)
        ssum = sb("ssum", [128, 8], F32)
        NB = 3
        wb = [sb("wb%d" % i, [128, 8, 512], BF16) for i in range(NB)]
        t_wb = [S.tok("wb%d" % i) for i in range(NB)]
        pbank = [psb("pb%d" % i, [128, 512], F32) for i in range(8)]
        t_pb = [S.tok("pb%d" % i) for i in range(8)]
        PJ, PS_, PO, PL = (0, 1), (2, 3), (4, 5), (6, 7)

        t_hT = [[S.tok("hT%d_%d" % (kc, tt)) for tt in range(4)] for kc in range(8)]
        t_m2 = [[S.tok("m2%d_%d" % (kc, tt)) for tt in range(4)] for kc in range(8)]
        t_oag = [[S.tok("oag%d_%d" % (kc, tt)) for tt in range(4)] for kc in range(8)]
        t_xt = [S.tok("xt%d" % i) for i in range(2)]
        t_xo = [S.tok("xo%d" % i) for i in range(2)]
        t_junk = S.tok("junk")
        t_et = [S.tok("et%d" % i) for i in range(3)]
        t_tf = [S.tok("tf%d" % i) for i in range(6)]
        t_diag = [S.tok("diag0"), S.tok("diag1")]
        t_ssum = S.tok("ssum")
        hT_all = [t for r in t_hT for t in r]
        print("sbuf bytes remaining:", nc.sbuf_bytes_remaining)

        def MM(out, lhsT, rhs, start, stop, reads, writes):
            S.emit("pe", lambda e: e.matmul(out, lhsT=lhsT, rhs=rhs, start=start, stop=stop), reads, writes)

        def ACT(out, in_, func, reads, writes, eng="act", **kw):
            S.emit(eng, lambda e: e.activation(out=out, in_=in_, func=func, **kw), reads, writes)

        def TS(eng, out, in0, s1, s2, op0, op1, reads, writes):
            if s2 is None:
                S.emit(eng, lambda e: e.tensor_scalar(out=out, in0=in0, scalar1=s1, scalar2=None, op0=op0), reads, writes)
            else:
                S.emit(eng, lambda e: e.tensor_scalar(out=out, in0=in0, scalar1=s1, scalar2=s2, op0=op0, op1=op1), reads, writes)

        def STT(eng, out, in0, scalar, in1, op0, op1, reads, writes):
            S.emit(eng, lambda e: e.scalar_tensor_tensor(out=out, in0=in0, scalar=scalar, in1=in1, op0=op0, op1=op1), reads, writes)

        def TT(eng, out, in0, in1, op, reads, writes):
            S.emit(eng, lambda e: e.tensor_tensor(out=out, in0=in0, in1=in1, op=op), reads, writes)

        def RSQRT(ap, n, reads, writes):
            S.emit("act", lambda e: e.activation(out=ap, in_=ap, func=AF.Sqrt), reads, writes)
            S.emit("dve", lambda e: e.reciprocal(out=ap, in_=ap), writes, writes)

        class Rot:
            def __init__(self, items):
                self.items = items
                self.i = 0

            def next(self):
                v = self.items[self.i]
                self.i = (self.i + 1) % len(self.items)
                return v

        pj = Rot(list(PJ))
        ps_r = Rot(list(PS_))
        et_r = Rot([0, 1, 2])

        wstate = {"next": 0}

        def wload(slices):
            i = wstate["next"]
            wstate["next"] = (i + 1) % NB
            off = 0
            for ap in slices:
                n = ap.shape[1]
                S.dma_load("pool", t_wb[i], wb[i][:, :, off:off + n], ap.rearrange("(kc p) n -> p kc n", p=128))
                off += n
            return i

        steps = []

        def add_step(panels, fn):
            steps.append((panels, fn))

        def run_steps():
            loaded = {}
            for i, (panels, fn) in enumerate(steps):
                cur = loaded.setdefault(i, [])
                while len(cur) < len(panels):
                    cur.append(wload(panels[len(cur)]))
                if i + 1 < len(steps):
                    nxt_panels = steps[i + 1][0]
                    nxt = loaded.setdefault(i + 1, [])
                    free = NB - len(cur)
                    while len(nxt) < len(nxt_panels) and free > 0:
                        nxt.append(wload(nxt_panels[len(nxt)]))
                        free -= 1
                fn(*cur)

        def rmsnorm_T(src_rows_ap, ntb, gcol0, dst, dst_toks_fn, width):
            for tb in range(ntb):
                bi = tb % 2
                S.dma_load("sp", t_xt[bi], xt[bi], src_rows_ap[tb * 128:(tb + 1) * 128, :])
                S.emit("dve", lambda e: e.memset(ssum[:, 0:1], 0.0), [], [t_ssum])
                ACT(junk[:], xt[bi], AF.Square, [t_xt[bi], t_ssum], [t_junk, t_ssum], accum_out=ssum[:, 0:1])
                TS("dve", ssum[:, 1:2], ssum[:, 0:1], 1.0 / DM, 1e-6, ALU.mult, ALU.add, [t_ssum], [t_ssum])
                RSQRT(ssum[:, 1:2], 1, [t_ssum] + CT, [t_ssum])
                TS("dve", xo[bi], xt[bi], ssum[:, 1:2], None, ALU.mult, None, [t_xt[bi], t_ssum], [t_xo[bi]])
                for half in range(2):
                    b = ps_r.next()
                    for i in range(4):
                        kc = half * 4 + i
                        S.emit("pe", lambda e, b=b, i=i, kc=kc, bi=bi: e.transpose(
                            out=pbank[b][:, i * 128:(i + 1) * 128], in_=xo[bi][:, kc * 128:(kc + 1) * 128],
                            identity=identf[:]), [t_xo[bi]] + CT, [t_pb[b]])
                    for i in range(4):
                        kc = half * 4 + i
                        dt = dst_toks_fn(kc, tb)
                        ACT(dst[:, kc, tb * 128:(tb + 1) * 128], pbank[b][:, i * 128:(i + 1) * 128], AF.Copy,
                            [t_pb[b]] + CT, [dt], scale=cols[:, gcol0 + kc:gcol0 + kc + 1])

        qT = scr[:, 0:4096].rearrange("p (j t) -> p j t", j=2)
        kT = scr[:, 4096:8192].rearrange("p (j t) -> p j t", j=2)
        vv = scr[:, 8192:12288].rearrange("p (b n) -> p b n", b=16)
        sza = scr[:, 12288:16384].rearrange("p (j t) -> p j t", j=2)
        t_q, t_k, t_v, t_sz = S.tok("qT"), S.tok("kT"), S.tok("vv"), S.tok("sza")
        CW = SEQ + 30
        cT = [scr[:, i * 2 * CW:(i + 1) * 2 * CW].rearrange("p (j t) -> p j t", j=2) for i in range(2)]
        t_cT = [S.tok("cT0"), S.tok("cT1")]
        memT = scr[:, 0:2048].rearrange("p (c t) -> p c t", c=8)
        kmT = scr[:, 2048:4096].rearrange("p (c t) -> p c t", c=8)
        vm = scr[:, 4096:6144].rearrange("p (b n) -> p b n", b=2)
        qcT = scr[:, 6144:10240].rearrange("p (j t) -> p j t", j=2)
        szc = scr[:, 10240:14336].rearrange("p (j t) -> p j t", j=2)
        t_memT, t_kmT, t_vm, t_qc, t_szc = S.tok("memT"), S.tok("kmT"), S.tok("vm"), S.tok("qcT"), S.tok("szc")

        def proj_fm(widx, col0, tt, bank):
            for kc in range(8):
                MM(pbank[bank][:], wb[widx][:, kc, col0:col0 + 128], hT[:, kc, tt * 512:(tt + 1) * 512],
                   kc == 0, kc == 7, [t_wb[widx], t_hT[kc][tt]], [t_pb[bank]])

        def silu2_from_psum(bank, dst, dst_tok, tfi):
            ACT(tf[tfi][:], pbank[bank][:], AF.Tanh, [t_pb[bank]], [t_tf[tfi]], scale=0.5)
            STT("dve", dst, tf[tfi][:], 1.0, pbank[bank][:], ALU.add, ALU.mult, [t_tf[tfi], t_pb[bank]], [dst_tok])

        first_b = min(i for i, p in enumerate("ABC") if p in phases) if any(p in phases for p in "ABC") else 0
        def merge_step(seq, b, qd, w_o):
            def fn(widx):
                for dci in range(2):
                    dc = qd * 2 + dci
                    for tt in range(4):
                        by = pj.next()
                        for kc in range(8):
                            MM(pbank[by][:], wb[widx][:, kc, dci * 128:(dci + 1) * 128], oagT[:, kc, tt * 512:(tt + 1) * 512],
                               kc == 0, kc == 7, [t_wb[widx], t_oag[kc][tt]], [t_pb[by]])
                        bg = pj.next()
                        for kc in range(8):
                            MM(pbank[bg][:], wb[widx][:, kc, 256 + dci * 128:256 + (dci + 1) * 128], hT[:, kc, tt * 512:(tt + 1) * 512],
                               kc == 0, kc == 7, [t_wb[widx], t_hT[kc][tt]], [t_pb[bg]])
                        ACT(tf[4][:], pbank[bg][:], AF.Tanh, [t_pb[bg]] + CT, [t_tf[4]], scale=0.5,
                            bias=hbm[:, b * 8 + dc:b * 8 + dc + 1])
                        dst = m2T[:, dc, tt * 512:(tt + 1) * 512]
                        if b == first_b:
                            STT("dve", dst, tf[4][:], 1.0, pbank[by][:], ALU.add, ALU.mult, [t_tf[4], t_pb[by]], [t_m2[dc][tt]])
                        else:
                            STT("dve", tf[5][:], tf[4][:], 1.0, pbank[by][:], ALU.add, ALU.mult, [t_tf[4], t_pb[by]], [t_tf[5]])
                            TT("dve", dst, tf[5][:], dst, ALU.add, [t_tf[5], t_m2[dc][tt]], [t_m2[dc][tt]])
            gcol = 9216 + b * 1024 + qd * 256
            add_step([[w_o[:, qd * 256:(qd + 1) * 256], w_in[:, gcol:gcol + 256]]], fn)

        def attn_head(hg, j):
            h = 2 * hg + j
            for G in range(4):
                nkb = 4 * G + 4
                for c in range(2):
                    r0, r1 = c * 64, (c + 1) * 64
                    po, pl = PO[c], PL[c]
                    for kb in range(nkb):
                        jj = kb - 4 * G
                        c0 = max(jj, 0) * 128
                        W = 512 - c0
                        if jj >= 0:
                            boff, nw = 0, min(256, W)
                        elif jj == -1:
                            boff, nw = 128, 128
                        else:
                            boff, nw = 0, 0
                        bs = ps_r.next()
                        MM(pbank[bs][:, 0:W], kT[r0:r1, j, kb * 128:(kb + 1) * 128],
                           qT[r0:r1, j, G * 512 + c0:(G + 1) * 512], True, True, [t_k, t_q], [t_pb[bs]])
                        ei = et_r.next()
                        if nw > 0:
                            ti = 2 + (kb % 2)
                            TT("dve", tf[ti][:, 0:nw], pbank[bs][:, 0:nw], biasdu[:, h, boff:boff + nw], ALU.add,
                               [t_pb[bs]] + CT, [t_tf[ti]])
                            ACT(et[ei][:, 0:nw], tf[ti][:, 0:nw], AF.Exp, [t_tf[ti]], [t_et[ei]])
                        if nw < W:
                            ACT(et[ei][:, nw:W], pbank[bs][:, nw:W], AF.Exp, [t_pb[bs]] + CT, [t_et[ei]],
                                bias=cols[:, C_CFAR + h:C_CFAR + h + 1])
                        MM(pbank[po][:, c0:512], vv[:, kb, j * 128:(j + 1) * 128], et[ei][:, 0:W],
                           kb == 0, kb == nkb - 1, [t_v, t_et[ei]], [t_pb[po]])
                        MM(pbank[pl][:, c0:512], ones_b[:], et[ei][:, 0:W],
                           kb == 0, kb == nkb - 1, [t_et[ei]] + CT, [t_pb[pl]])
                for c in range(2):
                    S.emit("dve", lambda e, c=c: e.reciprocal(out=tf[c][:], in_=pbank[PL[c]][:]), [t_pb[PL[c]]], [t_tf[c]])
                    TT("dve", tf[c][:], pbank[PO[c]][:], tf[c][:], ALU.mult, [t_pb[PO[c]], t_tf[c]], [t_tf[c]])
                STT("dve", tf[0][:], tf[1][:], nlam, tf[0][:], ALU.mult, ALU.add, [t_tf[0], t_tf[1]] + CT, [t_tf[0]])
                ACT(et[0][:], tf[0][:], AF.Square, [t_tf[0]], [t_et[0]])
                bq = pj.next()
                MM(pbank[bq][:], ones128[:], et[0][:], True, True, [t_et[0]] + CT, [t_pb[bq]])
                TS("dve", tf[1][:], pbank[bq][:], 1e-6, None, ALU.add, None, [t_pb[bq]], [t_tf[1]])
                RSQRT(tf[1][:], 512, [t_tf[1]] + CT, [t_tf[1]])
                STT("dve", tf[0][:], tf[0][:], gsub, tf[1][:], ALU.mult, ALU.mult, [t_tf[0], t_tf[1]] + CT, [t_tf[0]])
                TT("dve", oagT[:, h, G * 512:(G + 1) * 512], tf[0][:], sza[:, j, G * 512:(G + 1) * 512], ALU.mult,
                   [t_tf[0], t_sz], [t_oag[h][G]])

        def phase_A(seq):
            for hg in range(4):
                def fq(widx, hg=hg):
                    for j in range(2):
                        for tt in range(4):
                            b = pj.next()
                            proj_fm(widx, j * 128, tt, b)
                            ACT(qT[:, j, tt * 512:(tt + 1) * 512], pbank[b][:], AF.Copy, [t_pb[b]], [t_q], scale=0.125)
                add_step([[w_in[:, hg * 256:(hg + 1) * 256]]], fq)

                def fk(widx, hg=hg):
                    for j in range(2):
                        for tt in range(4):
                            b = pj.next()
                            proj_fm(widx, j * 128, tt, b)
                            S.emit("dve", lambda e, b=b, j=j, tt=tt: e.tensor_copy(out=kT[:, j, tt * 512:(tt + 1) * 512], in_=pbank[b][:]),
                                   [t_pb[b]], [t_k])
                add_step([[w_in[:, 1024 + hg * 256:1024 + (hg + 1) * 256]]], fk)

                def fv(widx, hg=hg):
                    for tb in range(16):
                        b = pj.next()
                        tt = tb // 4
                        for kc in range(8):
                            MM(pbank[b][:, 0:256], hT[:, kc, tb * 128:(tb + 1) * 128], wb[widx][:, kc, 0:256],
                               kc == 0, kc == 7, [t_wb[widx], t_hT[kc][tt]], [t_pb[b]])
                        if tb % 2 == 0:
                            ACT(vv[:, tb, :], pbank[b][:, 0:256], AF.Copy, [t_pb[b]], [t_v])
                        else:
                            S.emit("dve", lambda e, b=b, tb=tb: e.tensor_copy(out=vv[:, tb, :], in_=pbank[b][:, 0:256]), [t_pb[b]], [t_v])
                add_step([[w_in[:, 2048 + hg * 256:2048 + (hg + 1) * 256]]], fv)

                def fz(widx, hg=hg):
                    for j in range(2):
                        for tt in range(4):
                            b = pj.next()
                            proj_fm(widx, j * 128, tt, b)
                            silu2_from_psum(b, sza[:, j, tt * 512:(tt + 1) * 512], t_sz, 4 + (tt % 2))
                add_step([[w_in[:, 3072 + hg * 256:3072 + (hg + 1) * 256]]], fz)

                def fa(hg=hg):
                    for j in range(2):
                        attn_head(hg, j)
                add_step([], fa)
            for qd in range(4):
                merge_step(seq, 0, qd, w_oa)

        def phase_B(seq):
            def bar():
                S.barrier(S.COMPUTE)
            add_step([], bar)
            for qd in range(4):
                def fc(widx, qd=qd):
                    ci = qd % 2
                    S.emit("dve", lambda e, ci=ci: e.memset(cT[ci][:, :, 0:30], 0.0), [], [t_cT[ci]])
                    for dci in range(2):
                        dc = qd * 2 + dci
                        for tt in range(4):
                            ba = pj.next()
                            proj_fm(widx, dci * 128, tt, ba)
                            bg = pj.next()
                            proj_fm(widx, 256 + dci * 128, tt, bg)
                            ACT(tf[4][:], pbank[bg][:], AF.Tanh, [t_pb[bg]], [t_tf[4]], scale=0.5)
                            STT("dve", cT[ci][:, dci, 30 + tt * 512:30 + (tt + 1) * 512], tf[4][:], 1.0, pbank[ba][:],
                                ALU.add, ALU.mult, [t_tf[4], t_pb[ba]], [t_cT[ci]])
                    for dci in range(2):
                        dc = qd * 2 + dci
                        for jt in range(31):
                            if jt % 2 == 0:
                                ACT(diag[:, jt, :], identf[:], AF.Copy, CT, [t_diag[0]], scale=convw[:, dc, jt:jt + 1])
                            else:
                                TS("dve", diag[:, jt, :], identf[:], convw[:, dc, jt:jt + 1], None, ALU.mult, None, CT, [t_diag[1]])
                        for tt in range(4):
                            b = pj.next()
                            for jt in range(31):
                                MM(pbank[b][:], diag[:, jt, :], cT[ci][:, dci, tt * 512 + jt:tt * 512 + jt + 512],
                                   jt == 0, jt == 30, [t_diag[jt % 2], t_cT[ci]], [t_pb[b]])
                            ACT(oagT[:, dc, tt * 512:(tt + 1) * 512], pbank[b][:], AF.Identity, [t_pb[b]] + CT, [t_oag[dc][tt]],
                                bias=cols[:, C_CB + dc:C_CB + dc + 1])
                add_step([[w_in[:, 4096 + qd * 256:4096 + (qd + 1) * 256], w_in[:, 5120 + qd * 256:5120 + (qd + 1) * 256]]], fc)

            def f2(w0, w1):
                wz = (w0, w1)
                for tt in range(4):
                    sl = slice(tt * 512, (tt + 1) * 512)
                    bmu, bms = PO[0], PO[1]
                    for dc in range(8):
                        MM(pbank[bmu][:], ones1024[:], oagT[:, dc, sl], dc == 0, dc == 7, [t_oag[dc][tt]] + CT, [t_pb[bmu]])
                    for dc in range(8):
                        ei = et_r.next()
                        ACT(et[ei][:], oagT[:, dc, sl], AF.Square, [t_oag[dc][tt]], [t_et[ei]])
                        MM(pbank[bms][:], ones1024[:], et[ei][:], dc == 0, dc == 7, [t_et[ei]] + CT, [t_pb[bms]])
                    ACT(tf[0][:], pbank[bmu][:], AF.Copy, [t_pb[bmu]], [t_tf[0]])
                    TT("dve", tf[1][:], tf[0][:], tf[0][:], ALU.mult, [t_tf[0]], [t_tf[1]])
                    TT("dve", tf[1][:], pbank[bms][:], tf[1][:], ALU.subtract, [t_pb[bms], t_tf[1]], [t_tf[1]])
                    TS("dve", tf[1][:], tf[1][:], 1e-6, None, ALU.add, None, [t_tf[1]], [t_tf[1]])
                    RSQRT(tf[1][:], 512, [t_tf[1]] + CT, [t_tf[1]])
                    STT("dve", tf[0][:], tf[0][:], -1.0, tf[1][:], ALU.mult, ALU.mult, [t_tf[0], t_tf[1]], [t_tf[0]])
                    for dc in range(8):
                        TT("dve", tf[2][:], oagT[:, dc, sl], tf[1][:], ALU.mult, [t_oag[dc][tt], t_tf[1]], [t_tf[2]])
                        TT("dve", tf[2][:], tf[2][:], tf[0][:], ALU.add, [t_tf[2], t_tf[0]], [t_tf[2]])
                        ACT(tf[3][:], tf[2][:], AF.Tanh, [t_tf[2]] + CT, [t_tf[3]], scale=hlng[:, dc:dc + 1], bias=hlnb[:, dc:dc + 1])
                        ACT(tf[2][:], tf[2][:], AF.Identity, [t_tf[2]] + CT, [t_tf[2]], scale=cols[:, C_LNG + dc:C_LNG + dc + 1],
                            bias=cols[:, C_LNB + dc:C_LNB + dc + 1])
                        STT("dve", tf[2][:], tf[3][:], 1.0, tf[2][:], ALU.add, ALU.mult, [t_tf[3], t_tf[2]], [t_tf[2]])
                        bz = pj.next()
                        proj_fm(wz[dc // 4], (dc % 4) * 128, tt, bz)
                        ACT(tf[4][:], pbank[bz][:], AF.Tanh, [t_pb[bz]], [t_tf[4]], scale=0.5)
                        STT("dve", tf[5][:], tf[4][:], 1.0, pbank[bz][:], ALU.add, ALU.mult, [t_tf[4], t_pb[bz]], [t_tf[5]])
                        STT("dve", oagT[:, dc, sl], tf[2][:], 0.25, tf[5][:], ALU.mult, ALU.mult, [t_tf[2], t_tf[5]], [t_oag[dc][tt]])
            add_step([[w_in[:, 6144:6656]], [w_in[:, 6656:7168]]], f2)
            for qd in range(4):
                merge_step(seq, 1, qd, w_ob)

        def phase_C(seq):
            def fmem():
                S.barrier(S.ENGS, dma_toks=t_xt + t_xo)
                rmsnorm_T(mem_d[seq * 256:(seq + 1) * 256, :], 2, C_GMEM, memT, lambda kc, tb: t_memT, 256)
                S.barrier(S.COMPUTE, dma_toks=t_xt + t_xo)
            add_step([], fmem)
            for p in range(2):
                def fkk(widx, p=p):
                    for ci in range(4):
                        b = pj.next()
                        for kc in range(8):
                            MM(pbank[b][:, 0:256], wb[widx][:, kc, ci * 128:(ci + 1) * 128], memT[:, kc, :], kc == 0, kc == 7,
                               [t_wb[widx], t_memT], [t_pb[b]])
                        ACT(kmT[:, p * 4 + ci, :], pbank[b][:, 0:256], AF.Copy, [t_pb[b]], [t_kmT], scale=1.0 / 16)
                add_step([[w_kv[:, p * 512:(p + 1) * 512]]], fkk)
            for p in range(2):
                def fvv(widx, p=p):
                    for kb in range(2):
                        b = pj.next()
                        for kc in range(8):
                            MM(pbank[b][:], memT[:, kc, kb * 128:(kb + 1) * 128], wb[widx][:, kc, :], kc == 0, kc == 7,
                               [t_wb[widx], t_memT], [t_pb[b]])
                        ACT(vm[:, kb, p * 512:(p + 1) * 512], pbank[b][:], AF.Copy, [t_pb[b]], [t_vm])
                add_step([[w_kv[:, 1024 + p * 512:1024 + (p + 1) * 512]]], fvv)
            for hx in range(4):
                def fx(widx, hx=hx):
                    for dci in range(2):
                        for tt in range(4):
                            b = pj.next()
                            proj_fm(widx, dci * 128, tt, b)
                            ACT(qcT[:, dci, tt * 512:(tt + 1) * 512], pbank[b][:], AF.Copy, [t_pb[b]], [t_qc])
                            b2 = pj.next()
                            proj_fm(widx, 256 + dci * 128, tt, b2)
                            silu2_from_psum(b2, szc[:, dci, tt * 512:(tt + 1) * 512], t_szc, 4 + (tt % 2))
                    for G in range(4):
                        sl = slice(G * 512, (G + 1) * 512)
                        eis = []
                        for kb in range(2):
                            bs = ps_r.next()
                            for dci in range(2):
                                MM(pbank[bs][:], kmT[:, hx * 2 + dci, kb * 128:(kb + 1) * 128], qcT[:, dci, sl], dci == 0, dci == 1,
                                   [t_kmT, t_qc], [t_pb[bs]])
                            ei = et_r.next()
                            ACT(et[ei][:], pbank[bs][:], AF.Exp, [t_pb[bs]], [t_et[ei]])
                            eis.append(ei)
                        for ec in range(2):
                            for kb in range(2):
                                MM(pbank[PO[ec]][:], vm[:, kb, hx * 256 + ec * 128:hx * 256 + (ec + 1) * 128], et[eis[kb]][:],
                                   kb == 0, kb == 1, [t_vm, t_et[eis[kb]]], [t_pb[PO[ec]]])
                        for kb in range(2):
                            MM(pbank[PL[0]][:], ones_b[:], et[eis[kb]][:], kb == 0, kb == 1, [t_et[eis[kb]]] + CT, [t_pb[PL[0]]])
                        S.emit("dve", lambda e: e.reciprocal(out=tf[0][:], in_=pbank[PL[0]][:]), [t_pb[PL[0]]], [t_tf[0]])
                        for ec in range(2):
                            TT("dve", tf[1 + ec][:], pbank[PO[ec]][:], tf[0][:], ALU.mult, [t_pb[PO[ec]], t_tf[0]], [t_tf[1 + ec]])
                            STT("dve", oagT[:, hx * 2 + ec, sl], tf[1 + ec][:], 0.5, szc[:, ec, sl], ALU.mult, ALU.mult,
                                [t_tf[1 + ec], t_szc], [t_oag[hx * 2 + ec][G]])
                add_step([[w_in[:, 7168 + hx * 256:7168 + (hx + 1) * 256], w_in[:, 8192 + hx * 256:8192 + (hx + 1) * 256]]], fx)
            for qd in range(4):
                merge_step(seq, 2, qd, w_oc)

        def phase_D(seq):
            def fd(w0, w1):
                S.barrier(S.ENGS, dma_toks=[])
                wz = (w0, w1)
                for tb in range(16):
                    bi = tb % 2
                    tt = tb // 4
                    S.dma_load("sp", t_xt[bi], xt[bi], x_d[seq * SEQ + tb * 128:seq * SEQ + (tb + 1) * 128, :])
                    banks = (PO[bi], PL[bi])
                    for hf in range(2):
                        for kc in range(8):
                            MM(pbank[banks[hf]][:], m2T[:, kc, tb * 128:(tb + 1) * 128], wb[wz[hf]][:, kc, :], kc == 0, kc == 7,
                               [t_wb[wz[hf]], t_m2[kc][tt]], [t_pb[banks[hf]]])
                    S.emit("dve", lambda e: e.memset(ssum[:, 2:4], 0.0), [], [t_ssum])
                    for hf in range(2):
                        ACT(junk[:, 0:512], pbank[banks[hf]][:], AF.Square, [t_pb[banks[hf]], t_ssum], [t_junk, t_ssum],
                            accum_out=ssum[:, 2 + hf:3 + hf])
                    TT("dve", ssum[:, 4:5], ssum[:, 2:3], ssum[:, 3:4], ALU.add, [t_ssum], [t_ssum])
                    TS("dve", ssum[:, 4:5], ssum[:, 4:5], 0.25 / DM, 1e-6, ALU.mult, ALU.add, [t_ssum], [t_ssum])
                    RSQRT(ssum[:, 4:5], 1, [t_ssum] + CT, [t_ssum])
                    TS("dve", ssum[:, 5:6], ssum[:, 4:5], 0.5, None, ALU.mult, None, [t_ssum], [t_ssum])
                    for hf in range(2):
                        cs = slice(hf * 512, (hf + 1) * 512)
                        STT("dve", xo[bi][:, cs], pbank[banks[hf]][:], ssum[:, 5:6], gpost[:, cs], ALU.mult, ALU.mult,
                            [t_pb[banks[hf]], t_ssum] + CT, [t_xo[bi]])
                        TT("dve", xo[bi][:, cs], xo[bi][:, cs], xt[bi][:, cs], ALU.add, [t_xo[bi], t_xt[bi]], [t_xo[bi]])
                    S.dma_store("sp", t_xo[bi], out_d[seq * SEQ + tb * 128:seq * SEQ + (tb + 1) * 128, :], xo[bi])
            add_step([[w_out[:, 0:512]], [w_out[:, 512:1024]]], fd)

        for seq in range(NSEQ):
            def f0(seq=seq):
                S.barrier(S.ENGS, dma_toks=t_xt + t_xo)
                rmsnorm_T(x_d[seq * SEQ:(seq + 1) * SEQ, :], 16, C_GPRE, hT, lambda kc, tb: t_hT[kc][tb // 4], SEQ)
                S.barrier(S.COMPUTE, dma_toks=t_xt + t_xo)
            add_step([], f0)
            if "A" in phases:
                phase_A(seq)
            if "B" in phases:
                phase_B(seq)
            if "C" in phases:
                phase_C(seq)
            if debug and seq == 0:
                def fdbg():
                    t_dbg = S.tok("dbg")
                    for name in debug:
                        src = {"hT": hT, "m2T": m2T, "oagT": oagT}[name]
                        rd = hT_all if name == "hT" else [t for r in (t_m2 if name == "m2T" else t_oag) for t in r]
                        for kc in range(8):
                            for tt in range(4):
                                S.emit("dve", lambda e, kc=kc, tt=tt, src=src: e.tensor_copy(out=tf[0][:], in_=src[:, kc, tt * 512:(tt + 1) * 512]),
                                       rd, [t_tf[0]])
                                S.dma_store("sp", t_tf[0], dbg_d[name][kc * 128:(kc + 1) * 128, tt * 512:(tt + 1) * 512], tf[0][:])
                add_step([], fdbg)
            if "D" in phases:
                phase_D(seq)
        if phases != "0ABCD":
            def ftouch():
                t_touch = S.tok("touch")
                for ap in (mem_d, w_in, w_oa, w_ob, w_oc, w_out, w_kv):
                    S.dma_load("sp", t_touch, ssum[0:1, 0:8], ap[0:1, 0:8])
                S.emit("dve", lambda e: e.memset(tf[1][:], 0.0), [], [t_tf[1]])
                S.dma_store("sp", t_tf[1], out_d[0:128, 0:512], tf[1][:])
            add_step([], ftouch)
        run_steps()
        S.build()
        print("instr counts", {e: len(S.prog[e]) for e in S.ENGS}, "signals", S.sigcount, "waits", S.nwaits)
    return nc


def _t5_bucket_host(rel):
    nb, max_exact = 16, 8
    try:
        import jax
        import jax.numpy as jnp
        cpu = jax.devices("cpu")[0]
        with jax.default_device(cpu):
            r = jnp.asarray(rel, dtype=jnp.int32)
            ret = (r > 0).astype(jnp.int32) * nb
            n = jnp.abs(r)
            nf = jnp.maximum(n, 1).astype(jnp.float32)
            large = max_exact + (jnp.log(nf / max_exact) / math.log(128 / max_exact) * (nb - max_exact)).astype(jnp.int32)
            large = jnp.minimum(large, nb - 1)
            return np.asarray(ret + jnp.where(n < max_exact, n, large))
    except Exception:
        ret = (rel > 0).astype(np.int32) * nb
        n = np.abs(rel)
        nf = np.maximum(n, 1).astype(np.float32)
        large = max_exact + (np.log(nf / np.float32(max_exact)) / np.float32(math.log(128 / max_exact))
                             * np.float32(nb - max_exact)).astype(np.int32)
        large = np.minimum(large, nb - 1)
        return ret + np.where(n < max_exact, n, large)


def _colvec(v):
    v = np.asarray(v, np.float32).reshape(-1)
    return np.ascontiguousarray(v.reshape(-1, 128).T)


def prep_shared(rel_bias, g_pre, g_mem, w_in, b_merge, lam_q1, lam_k1, lam_q2, lam_k2, g_subln, w_oa, conv_w,
                conv_b, ln_g, ln_b, w_ob, w_mem_kv, w_oc, w_out, g_post):
    f = lambda a: np.ascontiguousarray(np.asarray(a, np.float32))
    rel_bias = f(rel_bias)
    cols = np.zeros((128, NCOLS), np.float32)
    cols[:, C_GPRE:C_GPRE + 8] = _colvec(g_pre[0])
    cols[:, C_GMEM:C_GMEM + 8] = _colvec(g_mem[0])
    cols[:, C_BM:C_BM + 24] = _colvec(b_merge[0])
    cols[:, C_CB:C_CB + 8] = _colvec(conv_b[0])
    cols[:, C_LNG:C_LNG + 8] = _colvec(ln_g[0])
    cols[:, C_LNB:C_LNB + 8] = _colvec(ln_b[0])
    cols[:, C_GSUB:C_GSUB + 1] = _colvec(g_subln[0])
    far_b = int(_t5_bucket_host(np.array([-1000], np.int32))[0])
    cols[:, C_CFAR:C_CFAR + 8] = np.broadcast_to(rel_bias[far_b][None, :], (128, 8))
    cw = f(conv_w)[0]
    convw = np.ascontiguousarray(cw.reshape(31, 8, 128).transpose(2, 1, 0)).reshape(128, 8 * 31)
    kk = np.arange(128, dtype=np.int32)[:, None]
    qq = np.arange(128, dtype=np.int32)[None, :]
    relD = kk - qq
    relU = kk - qq - 128
    bD = _t5_bucket_host(relD)
    bU = _t5_bucket_host(relU)
    allowed = (kk // 64) <= (qq // 64)
    ext = np.concatenate([rel_bias, np.full((1, 8), NEG, np.float32)], axis=0)
    idxD = np.where(allowed, bD, 32)
    D = ext[idxD]
    U = ext[bU]
    biasdu = np.ascontiguousarray(np.concatenate([D, U], axis=1).transpose(0, 2, 1)).reshape(128, 8 * 256)
    lamv = np.concatenate([f(lam_q1)[0], f(lam_k1)[0], f(lam_q2)[0], f(lam_k2)[0]]).astype(np.float32)
    return {
        "w_in": f(w_in)[0], "w_oa": f(w_oa)[0], "w_ob": f(w_ob)[0], "w_oc": f(w_oc)[0], "w_out": f(w_out)[0],
        "w_mem_kv": f(w_mem_kv)[0], "cols": cols, "convw": convw, "g_post": f(g_post)[0], "lamv": lamv,
        "biasdu": biasdu.astype(np.float32),
    }


_PROG = {}


def kernel(x, mem, rel_bias, g_pre, g_mem, w_in, b_merge, lam_q1, lam_k1, lam_q2, lam_k2, g_subln, w_oa, conv_w,
           conv_b, ln_g, ln_b, w_ob, w_mem_kv, w_oc, w_out, g_post):
    shared = prep_shared(rel_bias, g_pre, g_mem, w_in, b_merge, lam_q1, lam_k1, lam_q2, lam_k2, g_subln, w_oa,
                         conv_w, conv_b, ln_g, ln_b, w_ob, w_mem_kv, w_oc, w_out, g_post)
    x = np.asarray(x, np.float32)
    mem = np.asarray(mem, np.float32)
    B = x.shape[0]
    per = B // NCORES
    if "p" not in _PROG:
        _PROG["p"] = build_program(NSEQ=per)
    nc = _PROG["p"]
    in_maps = []
    for c in range(NCORES):
        m = dict(shared)
        m["x"] = np.ascontiguousarray(x[c * per:(c + 1) * per].reshape(per * SEQ, DM))
        m["mem"] = np.ascontiguousarray(mem[c * per:(c + 1) * per].reshape(per * 256, DM))
        in_maps.append(m)
    res = run_bass_kernel_spmd(nc, in_maps, core_ids=list(range(NCORES)))
    outs = [np.asarray(r["out"], np.float32).reshape(per, SEQ, DM) for r in res.results]
    return np.concatenate(outs, axis=0)
```

```python
import math
from contextlib import ExitStack
import numpy as np
import concourse.bass as bass
import concourse.mybir as mybir
from concourse.bass_utils import run_bass_kernel_spmd

F32 = mybir.dt.float32
BF16 = mybir.dt.bfloat16
AF = mybir.ActivationFunctionType
ALU = mybir.AluOpType
AX = mybir.AxisListType

SAME_ENGINE_SYNC = True
EPOCH = 20000
NEG = -30000.0
SEQ = 2048
DM = 1024
NCORES = 8


class Tk:
    __slots__ = ("name", "lw", "rd", "dsem", "dcnt")

    def __init__(self, name):
        self.name = name
        self.lw = None
        self.rd = {}
        self.dsem = None
        self.dcnt = 0


class Ins:
    __slots__ = ("eng", "fn", "cdeps", "raw", "dwaits", "sig", "val", "dma", "waits")

    def __init__(self, eng, fn):
        self.eng = eng
        self.fn = fn
        self.cdeps = set()
        self.raw = set()
        self.dwaits = {}
        self.sig = False
        self.val = None
        self.dma = None
        self.waits = []


class Sched:
    ENGS = ("pe", "act", "dve", "pool", "sp")
    COMPUTE = ("pe", "act", "dve", "pool")

    def __init__(self, nc, stack):
        self.nc = nc
        self.stack = stack
        self.prog = {e: [] for e in self.ENGS}
        self.esem = {e: [] for e in self.ENGS}
        self.toks = []
        self.last_compute = {e: None for e in self.ENGS}
        self.pending = {e: (set(), {}) for e in self.ENGS}

    def tok(self, name):
        t = Tk(name)
        self.toks.append(t)
        return t

    def _dep(self, ins, p, raw):
        if p is None or p is ins:
            return
        if p.dma is not None:
            t = p.dma
            ins.dwaits[t] = max(ins.dwaits.get(t, 0), 16 * t.dcnt)
        else:
            ins.cdeps.add(p)
            if raw:
                ins.raw.add(p)

    def emit(self, eng, fn, reads=(), writes=(), dma=None):
        ins = Ins(eng, fn)
        pc, pd = self.pending[eng]
        if pc or pd:
            for p in pc:
                if p.eng != eng:
                    ins.cdeps.add(p)
            for t, v in pd.items():
                ins.dwaits[t] = max(ins.dwaits.get(t, 0), v)
            self.pending[eng] = (set(), {})
        for t in reads:
            self._dep(ins, t.lw, True)
        for t in writes:
            self._dep(ins, t.lw, False)
            for r in t.rd.values():
                self._dep(ins, r, False)
        if dma is not None:
            if dma.dsem is None:
                dma.dsem = self.stack.enter_context(self.nc.semaphore("ds_" + dma.name))
            dma.dcnt += 1
            ins.dma = dma
        else:
            self.last_compute[eng] = ins
        for t in reads:
            key = eng if dma is None else ("dma", id(ins))
            t.rd[key] = ins
        for t in writes:
            t.lw = ins
            t.rd = {}
        self.prog[eng].append(ins)
        return ins

    def barrier(self, engs, dma_toks=()):
        lasts = set(self.last_compute[e] for e in self.COMPUTE if self.last_compute[e] is not None)
        for e in engs:
            pc, pd = self.pending[e]
            pc |= lasts
            for t in dma_toks:
                if t.dsem is not None:
                    pd[t] = max(pd.get(t, 0), 16 * t.dcnt)

    def dma_load(self, q, tok, out_ap, in_ap, **kw):
        return self.emit(q, lambda e: e.dma_start(out=out_ap, in_=in_ap, **kw), writes=[tok], dma=tok)

    def dma_store(self, q, tok, out_ap, in_ap, **kw):
        return self.emit(q, lambda e: e.dma_start(out=out_ap, in_=in_ap, **kw), reads=[tok], dma=tok)

    def _needs(self, ins, p):
        if p.eng != ins.eng:
            return True
        return SAME_ENGINE_SYNC and p.eng != "pe"

    def finalize(self):
        for e in self.ENGS:
            for ins in self.prog[e]:
                for p in ins.cdeps:
                    if self._needs(ins, p):
                        p.sig = True
        for e in self.ENGS:
            c = 0
            for ins in self.prog[e]:
                if ins.sig:
                    ep, v = divmod(c, EPOCH)
                    if ep >= len(self.esem[e]):
                        self.esem[e].append(self.stack.enter_context(self.nc.semaphore("es_%s%d" % (e, ep))))
                    ins.val = (ep, v + 1)
                    c += 1
        self.sigcount = {e: sum(1 for i in self.prog[e] if i.sig) for e in self.ENGS}
        self.nwaits = 0
        for e in self.ENGS:
            known = {}
            for ins in self.prog[e]:
                need = {}
                for p in ins.cdeps:
                    if not self._needs(ins, p):
                        continue
                    need[p.eng] = max(need.get(p.eng, (0, 0)), p.val)
                for pe_, v in need.items():
                    if known.get(pe_, (0, 0)) < v:
                        ins.waits.append((self.esem[pe_][v[0]], v[1]))
                        known[pe_] = v
                for t, v in ins.dwaits.items():
                    k = ("d", id(t))
                    if known.get(k, 0) < v:
                        ins.waits.append((t.dsem, v))
                        known[k] = v
                self.nwaits += len(ins.waits)

    def simulate(self):
        sem = {}
        pc = {e: 0 for e in self.ENGS}
        progress = True
        while progress:
            progress = False
            for e in self.ENGS:
                while pc[e] < len(self.prog[e]):
                    ins = self.prog[e][pc[e]]
                    if all(sem.get(id(s), 0) >= v for (s, v) in ins.waits):
                        if ins.dma is not None:
                            sem[id(ins.dma.dsem)] = sem.get(id(ins.dma.dsem), 0) + 16
                        elif ins.sig:
                            k = id(self.esem[e][ins.val[0]])
                            sem[k] = sem.get(k, 0) + 1
                        pc[e] += 1
                        progress = True
                    else:
                        break
        stuck = {e: (pc[e], len(self.prog[e])) for e in self.ENGS if pc[e] < len(self.prog[e])}
        if stuck:
            for e in stuck:
                ins = self.prog[e][pc[e]]
                print("STUCK", e, pc[e], [(s.name, v, sem.get(id(s), 0)) for (s, v) in ins.waits])
            raise RuntimeError("deadlock in generated program: %s" % stuck)

    def run_engine(self, ename, e):
        for ins in self.prog[ename]:
            for (s, v) in ins.waits:
                e.wait_ge(s, v)
            bi = ins.fn(e)
            if ins.dma is not None:
                bi.then_inc(ins.dma.dsem, 16)
            elif ins.sig:
                bi.then_inc(self.esem[ename][ins.val[0]], 1)

    def build(self):
        fin = {}
        for t in self.toks:
            if t.dsem is not None:
                fin[t] = 16 * t.dcnt
        self.finalize()
        self.simulate()
        nc = self.nc
        with nc.Block() as block:
            @block.tensor
            def _(e):
                self.run_engine("pe", e)

            @block.scalar
            def _(e):
                self.run_engine("act", e)

            @block.vector
            def _(e):
                self.run_engine("dve", e)

            @block.gpsimd
            def _(e):
                self.run_engine("pool", e)

            @block.sync
            def _(e):
                self.run_engine("sp", e)
                for t, v in fin.items():
                    e.wait_ge(t.dsem, v)


STEP_LOG = []
EPI_D1 = 10
EPI_D2 = 14
NCOLS = 80
C_GPRE, C_GMEM, C_BM, C_CB, C_LNG, C_LNB, C_GSUB, C_CFAR = 0, 8, 16, 40, 48, 56, 64, 65


def build_program(NSEQ=2, debug=None, phases="0ABCD"):
    nc = bass.Bass("TRN2", target_bir_lowering=False)
    dram = lambda n, shp, kind="ExternalInput": nc.dram_tensor(n, shp, F32, kind=kind).ap()
    x_d = dram("x", [NSEQ * SEQ, DM])
    mem_d = dram("mem", [NSEQ * 256, DM])
    w_in = dram("w_in", [DM, 12288])
    w_oa = dram("w_oa", [DM, DM])
    w_ob = dram("w_ob", [DM, DM])
    w_oc = dram("w_oc", [DM, DM])
    w_out = dram("w_out", [DM, DM])
    w_kv = dram("w_mem_kv", [DM, 2048])
    cols_d = dram("cols", [128, NCOLS])
    convw_d = dram("convw", [128, 8 * 31])
    gpost_d = dram("g_post", [DM])
    lamv_d = dram("lamv", [256])
    bias_d = dram("biasdu", [128, 8 * 256])
    out_d = dram("out", [NSEQ * SEQ, DM], kind="ExternalOutput")
    dbg_d = {}
    if debug:
        for name, shp in debug.items():
            dbg_d[name] = dram("dbg_" + name, shp, kind="ExternalOutput")

    with ExitStack() as st:
        S = Sched(nc, st)
        sb = lambda name, shape, dt: st.enter_context(nc.sbuf_tensor("s_" + name, shape, dt))
        psb = lambda name, shape, dt: st.enter_context(nc.psum_tensor("p_" + name, shape, dt))

        cols = sb("cols", [128, NCOLS], F32)
        convw = sb("convw", [128, 8, 31], F32)
        gpost = sb("gpost", [128, DM], F32)
        lamv = sb("lamv", [128, 256], F32)
        biasdu = sb("biasdu", [128, 8, 256], F32)
        identf = sb("identf", [128, 128], F32)
        identb = sb("identb", [128, 128], BF16)
        ones_b = sb("ones_b", [128, 128], BF16)
        ones128 = sb("ones128", [128, 128], BF16)
        ones1024 = sb("ones1024", [128, 128], BF16)
        mhalf = sb("mhalf", [128, 512], F32)
        small = sb("small", [128, 16], F32)
        hbm = sb("hbm", [128, 24], F32)
        hlng = sb("hlng", [128, 8], F32)
        hlnb = sb("hlnb", [128, 8], F32)
        t_const = S.tok("const")
        S.dma_load("sp", t_const, cols[:], cols_d)
        S.dma_load("sp", t_const, convw[:], convw_d.rearrange("p (c j) -> p c j", j=31))
        S.dma_load("sp", t_const, gpost[:], gpost_d.partition_broadcast(128))
        S.dma_load("sp", t_const, lamv[:], lamv_d.partition_broadcast(128))
        S.dma_load("sp", t_const, biasdu[:], bias_d.rearrange("p (h k) -> p h k", k=256))
        t_c2 = S.tok("const2")
        S.emit("pool", lambda e: e.memset(identf[:], 0.0), writes=[t_c2])
        S.emit("pool", lambda e: e.affine_select(out=identf[:], in_=identf[:], pattern=[[-1, 128]],
                                                 compare_op=ALU.not_equal, fill=1.0, base=0,
                                                 channel_multiplier=1), reads=[t_c2], writes=[t_c2])
        S.emit("pool", lambda e: e.memset(mhalf[:], -0.5), writes=[t_c2])
        S.emit("dve", lambda e: e.tensor_copy(out=identb[:], in_=identf[:]), reads=[t_c2], writes=[t_c2])
        S.emit("dve", lambda e: e.memset(ones_b[:], 1.0), writes=[t_c2])
        S.emit("dve", lambda e: e.memset(ones128[:], 1.0 / 128), writes=[t_c2])
        S.emit("dve", lambda e: e.memset(ones1024[:], 1.0 / 1024), writes=[t_c2])
        lam_init = 0.8 - 0.6 * math.exp(-0.3 * 0)
        S.emit("dve", lambda e: e.tensor_tensor(out=lamv[:, 0:64], in0=lamv[:, 0:64], in1=lamv[:, 64:128], op=ALU.mult),
               reads=[t_const], writes=[t_c2])
        S.emit("dve", lambda e: e.tensor_tensor(out=lamv[:, 128:192], in0=lamv[:, 128:192], in1=lamv[:, 192:256], op=ALU.mult),
               reads=[t_const], writes=[t_c2])
        S.emit("dve", lambda e: e.reduce_sum(out=small[:, 2:3], in_=lamv[:, 0:64], axis=AX.X), reads=[t_c2], writes=[t_c2])
        S.emit("dve", lambda e: e.reduce_sum(out=small[:, 3:4], in_=lamv[:, 128:192], axis=AX.X), reads=[t_c2], writes=[t_c2])
        S.emit("act", lambda e: e.activation(out=small[:, 4:6], in_=small[:, 2:4], func=AF.Exp), reads=[t_c2], writes=[t_c2])
        S.emit("dve", lambda e: e.tensor_tensor(out=small[:, 0:1], in0=small[:, 5:6], in1=small[:, 4:5], op=ALU.subtract),
               reads=[t_c2], writes=[t_c2])
        S.emit("dve", lambda e: e.tensor_scalar(out=small[:, 0:1], in0=small[:, 0:1], scalar1=-lam_init, scalar2=None, op0=ALU.add),
               reads=[t_c2], writes=[t_c2])
        S.emit("dve", lambda e: e.tensor_scalar(out=small[:, 1:2], in0=cols[:, C_GSUB:C_GSUB + 1], scalar1=0.5 * (1.0 - lam_init),
                                                scalar2=None, op0=ALU.mult), reads=[t_const], writes=[t_c2])
        S.emit("dve", lambda e: e.tensor_scalar(out=hbm[:], in0=cols[:, C_BM:C_BM + 24], scalar1=0.5, scalar2=None, op0=ALU.mult),
               reads=[t_const], writes=[t_c2])
        S.emit("dve", lambda e: e.tensor_scalar(out=hlng[:], in0=cols[:, C_LNG:C_LNG + 8], scalar1=0.5, scalar2=None, op0=ALU.mult),
               reads=[t_const], writes=[t_c2])
        S.emit("dve", lambda e: e.tensor_scalar(out=hlnb[:], in0=cols[:, C_LNB:C_LNB + 8], scalar1=0.5, scalar2=None, op0=ALU.mult),
               reads=[t_const], writes=[t_c2])
        S.emit("dve", lambda e: e.tensor_scalar(out=convw[:], in0=convw[:], scalar1=0.5, scalar2=None, op0=ALU.mult),
               reads=[t_const], writes=[t_c2])
        CT = [t_const, t_c2]
        nlam = small[:, 0:1]
        gsub = small[:, 1:2]

        hT = sb("hT", [128, 8, SEQ], BF16)
        m2T = sb("m2T", [128, 8, SEQ], BF16)
        big = sb("big", [128, 8192], F32)
        oagT = big[:].bitcast(BF16).rearrange("p (c t) -> p c t", c=8)
        xt = [big[:, i * 1024:(i + 1) * 1024] for i in range(2)]
        xo = [big[:, 2048 + i * 1024:2048 + (i + 1) * 1024] for i in range(2)]
        scr = sb("scr", [128, 16384], BF16)
        junk = sb("junk", [128, 1024], BF16)
        et = [sb("et%d" % i, [128, 512], BF16) for i in range(4)]
        tf = [sb("tf%d" % i, [128, 512], F32) for i in range(8)]
        sqb = [sb("sqb%d" % i, [128, 512], BF16) for i in range(2)]
        diag = sb("diag", [128, 31, 128], BF16)
        ssum = sb("ssum", [128, 8], F32)
        NB = 3
        wb = [sb("wb%d" % i, [128, 8, 512], BF16) for i in range(NB)]
        t_wb = [S.tok("wb%d" % i) for i in range(NB)]
        pbank = [psb("pb%d" % i, [128, 512], F32) for i in range(8)]
        t_pb = [S.tok("pb%d" % i) for i in range(8)]
        PJ, PS_, PO, PL = (0, 1), (2, 3), (4, 5), (6, 7)

        t_hT = [[S.tok("hT%d_%d" % (kc, tt)) for tt in range(4)] for kc in range(8)]
        t_m2 = [[S.tok("m2%d_%d" % (kc, tt)) for tt in range(4)] for kc in range(8)]
        t_oag = [[S.tok("oag%d_%d" % (kc, tt)) for tt in range(4)] for kc in range(8)]
        t_xt = [S.tok("xt%d" % i) for i in range(2)]
        t_xo = [S.tok("xo%d" % i) for i in range(2)]
        t_junk = S.tok("junk")
        t_et = [S.tok("et%d" % i) for i in range(4)]
        t_tf = [S.tok("tf%d" % i) for i in range(8)]
        t_sqb = [S.tok("sqb0"), S.tok("sqb1")]
        t_diag = [S.tok("diag0"), S.tok("diag1")]
        t_ssum = S.tok("ssum")
        hT_all = [t for r in t_hT for t in r]
        print("sbuf bytes remaining:", nc.sbuf_bytes_remaining)

        def MM(out, lhsT, rhs, start, stop, reads, writes):
            S.emit("pe", lambda e: e.matmul(out, lhsT=lhsT, rhs=rhs, start=start, stop=stop), reads, writes)

        def ACT(out, in_, func, reads, writes, eng="act", **kw):
            S.emit(eng, lambda e: e.activation(out=out, in_=in_, func=func, **kw), reads, writes)

        def TS(eng, out, in0, s1, s2, op0, op1, reads, writes):
            if s2 is None:
                S.emit(eng, lambda e: e.tensor_scalar(out=out, in0=in0, scalar1=s1, scalar2=None, op0=op0), reads, writes)
            else:
                S.emit(eng, lambda e: e.tensor_scalar(out=out, in0=in0, scalar1=s1, scalar2=s2, op0=op0, op1=op1), reads, writes)

        def STT(eng, out, in0, scalar, in1, op0, op1, reads, writes):
            S.emit(eng, lambda e: e.scalar_tensor_tensor(out=out, in0=in0, scalar=scalar, in1=in1, op0=op0, op1=op1), reads, writes)

        def TT(eng, out, in0, in1, op, reads, writes):
            S.emit(eng, lambda e: e.tensor_tensor(out=out, in0=in0, in1=in1, op=op), reads, writes)

        def RSQRT(ap, n, reads, writes):
            S.emit("act", lambda e: e.activation(out=ap, in_=ap, func=AF.Sqrt), reads, writes)
            S.emit("dve", lambda e: e.reciprocal(out=ap, in_=ap), writes, writes)

        class Rot:
            def __init__(self, items):
                self.items = items
                self.i = 0

            def next(self):
                v = self.items[self.i]
                self.i = (self.i + 1) % len(self.items)
                return v

        pj = Rot(list(PJ))
        ps_r = Rot(list(PS_))
        et_r = Rot([0, 1, 2, 3])

        wstate = {"next": 0}

        def wload(slices):
            i = wstate["next"]
            wstate["next"] = (i + 1) % NB
            off = 0
            for ap in slices:
                n = ap.shape[1]
                S.dma_load("pool", t_wb[i], wb[i][:, :, off:off + n], ap.rearrange("(kc p) n -> p kc n", p=128))
                off += n
            return i

        steps = []

        def add_step(panels, fn):
            steps.append((panels, fn))

        def run_steps():
            loaded = {}
            for i, (panels, fn) in enumerate(steps):
                cur = loaded.setdefault(i, [])
                while len(cur) < len(panels):
                    cur.append(wload(panels[len(cur)]))
                if i + 1 < len(steps):
                    nxt_panels = steps[i + 1][0]
                    nxt = loaded.setdefault(i + 1, [])
                    free = NB - len(cur)
                    while len(nxt) < len(nxt_panels) and free > 0:
                        nxt.append(wload(nxt_panels[len(nxt)]))
                        free -= 1
                STEP_LOG.append((getattr(fn, "__name__", "?"), len(S.prog["pe"])))
                fn(*cur)

        def rmsnorm_T(src_rows_ap, ntb, gcol0, dst, dst_toks_fn, width):
            for tb in range(ntb):
                bi = tb % 2
                S.dma_load("sp", t_xt[bi], xt[bi], src_rows_ap[tb * 128:(tb + 1) * 128, :])
                S.emit("dve", lambda e: e.memset(ssum[:, 0:1], 0.0), [], [t_ssum])
                ACT(junk[:], xt[bi], AF.Square, [t_xt[bi], t_ssum], [t_junk, t_ssum], accum_out=ssum[:, 0:1])
                TS("dve", ssum[:, 1:2], ssum[:, 0:1], 1.0 / DM, 1e-6, ALU.mult, ALU.add, [t_ssum], [t_ssum])
                RSQRT(ssum[:, 1:2], 1, [t_ssum] + CT, [t_ssum])
                TS("dve", xo[bi], xt[bi], ssum[:, 1:2], None, ALU.mult, None, [t_xt[bi], t_ssum], [t_xo[bi]])
                for half in range(2):
                    b = ps_r.next()
                    for i in range(4):
                        kc = half * 4 + i
                        S.emit("pe", lambda e, b=b, i=i, kc=kc, bi=bi: e.transpose(
                            out=pbank[b][:, i * 128:(i + 1) * 128], in_=xo[bi][:, kc * 128:(kc + 1) * 128],
                            identity=identf[:]), [t_xo[bi]] + CT, [t_pb[b]])
                    for i in range(4):
                        kc = half * 4 + i
                        dt = dst_toks_fn(kc, tb)
                        ACT(dst[:, kc, tb * 128:(tb + 1) * 128], pbank[b][:, i * 128:(i + 1) * 128], AF.Copy,
                            [t_pb[b]] + CT, [dt], scale=cols[:, gcol0 + kc:gcol0 + kc + 1])

        qT = scr[:, 0:4096].rearrange("p (j t) -> p j t", j=2)
        kT = scr[:, 4096:8192].rearrange("p (j t) -> p j t", j=2)
        vv = scr[:, 8192:12288].rearrange("p (b n) -> p b n", b=16)
        sza = scr[:, 12288:16384].rearrange("p (j t) -> p j t", j=2)
        t_q, t_k, t_v, t_sz = S.tok("qT"), S.tok("kT"), S.tok("vv"), S.tok("sza")
        CW = SEQ + 30
        cT = [scr[:, i * 2 * CW:(i + 1) * 2 * CW].rearrange("p (j t) -> p j t", j=2) for i in range(2)]
        t_cT = [S.tok("cT0"), S.tok("cT1")]
        memT = scr[:, 0:2048].rearrange("p (c t) -> p c t", c=8)
        kmT = scr[:, 2048:4096].rearrange("p (c t) -> p c t", c=8)
        vm = scr[:, 4096:6144].rearrange("p (b n) -> p b n", b=2)
        qcT = scr[:, 6144:10240].rearrange("p (j t) -> p j t", j=2)
        szc = scr[:, 10240:14336].rearrange("p (j t) -> p j t", j=2)
        t_memT, t_kmT, t_vm, t_qc, t_szc = S.tok("memT"), S.tok("kmT"), S.tok("vm"), S.tok("qcT"), S.tok("szc")

        def proj_fm(widx, col0, tt, bank):
            for kc in range(8):
                MM(pbank[bank][:], wb[widx][:, kc, col0:col0 + 128], hT[:, kc, tt * 512:(tt + 1) * 512],
                   kc == 0, kc == 7, [t_wb[widx], t_hT[kc][tt]], [t_pb[bank]])

        def silu2_from_psum(bank, dst, dst_tok, tfi):
            ACT(tf[tfi][:], pbank[bank][:], AF.Tanh, [t_pb[bank]], [t_tf[tfi]], scale=0.5)
            STT("dve", dst, tf[tfi][:], 1.0, pbank[bank][:], ALU.add, ALU.mult, [t_tf[tfi], t_pb[bank]], [dst_tok])

        first_b = min(i for i, p in enumerate("ABC") if p in phases) if any(p in phases for p in "ABC") else 0
        def merge_step(seq, b, qd, w_o):
            def fn(widx):
                for dci in range(2):
                    dc = qd * 2 + dci
                    for tt in range(4):
                        by = pj.next()
                        for kc in range(8):
                            MM(pbank[by][:], wb[widx][:, kc, dci * 128:(dci + 1) * 128], oagT[:, kc, tt * 512:(tt + 1) * 512],
                               kc == 0, kc == 7, [t_wb[widx], t_oag[kc][tt]], [t_pb[by]])
                        bg = pj.next()
                        for kc in range(8):
                            MM(pbank[bg][:], wb[widx][:, kc, 256 + dci * 128:256 + (dci + 1) * 128], hT[:, kc, tt * 512:(tt + 1) * 512],
                               kc == 0, kc == 7, [t_wb[widx], t_hT[kc][tt]], [t_pb[bg]])
                        ACT(tf[4][:], pbank[bg][:], AF.Tanh, [t_pb[bg]] + CT, [t_tf[4]], scale=0.5,
                            bias=hbm[:, b * 8 + dc:b * 8 + dc + 1])
                        dst = m2T[:, dc, tt * 512:(tt + 1) * 512]
                        if b == first_b:
                            STT("dve", dst, tf[4][:], 1.0, pbank[by][:], ALU.add, ALU.mult, [t_tf[4], t_pb[by]], [t_m2[dc][tt]])
                        else:
                            STT("dve", tf[5][:], tf[4][:], 1.0, pbank[by][:], ALU.add, ALU.mult, [t_tf[4], t_pb[by]], [t_tf[5]])
                            TT("dve", dst, tf[5][:], dst, ALU.add, [t_tf[5], t_m2[dc][tt]], [t_m2[dc][tt]])
            gcol = 9216 + b * 1024 + qd * 256
            add_step([[w_o[:, qd * 256:(qd + 1) * 256], w_in[:, gcol:gcol + 256]]], fn)

        def attn_group(hg):
            tasks = []
            for j in range(2):
                for G in range(4):
                    nkb = 4 * G + 4
                    for c in range(2):
                        for kb in range(nkb):
                            tasks.append((j, G, c, kb, nkb))
            state = {}
            deferred = []
            epi_n = [0]

            def stage1(t):
                j, G, c, kb, nkb = t
                h = 2 * hg + j
                r0, r1 = c * 64, (c + 1) * 64
                jj = kb - 4 * G
                c0 = max(jj, 0) * 128
                W = 512 - c0
                if jj >= 0:
                    boff, nw = 0, min(256, W)
                elif jj == -1:
                    boff, nw = 128, 128
                else:
                    boff, nw = 0, 0
                bs = ps_r.next()
                MM(pbank[bs][:, 0:W], kT[r0:r1, j, kb * 128:(kb + 1) * 128],
                   qT[r0:r1, j, G * 512 + c0:(G + 1) * 512], True, True, [t_k, t_q], [t_pb[bs]])
                ei = et_r.next()
                if nw > 0:
                    ti = 2 + (kb % 2)
                    TT("dve", tf[ti][:, 0:nw], pbank[bs][:, 0:nw], biasdu[:, h, boff:boff + nw], ALU.add,
                       [t_pb[bs]] + CT, [t_tf[ti]])
                    ACT(et[ei][:, 0:nw], tf[ti][:, 0:nw], AF.Exp, [t_tf[ti]], [t_et[ei]])
                if nw < W:
                    ACT(et[ei][:, nw:W], pbank[bs][:, nw:W], AF.Exp, [t_pb[bs]] + CT, [t_et[ei]],
                        bias=cols[:, C_CFAR + h:C_CFAR + h + 1])
                state[t] = (ei, c0, W)

            def stage2(t):
                j, G, c, kb, nkb = t
                h = 2 * hg + j
                ei, c0, W = state.pop(t)
                po, pl = PO[c], PL[c]
                MM(pbank[po][:, c0:512], vv[:, kb, j * 128:(j + 1) * 128], et[ei][:, 0:W],
                   kb == 0, kb == nkb - 1, [t_v, t_et[ei]], [t_pb[po]])
                MM(pbank[pl][:, c0:512], ones_b[:], et[ei][:, 0:W],
                   kb == 0, kb == nkb - 1, [t_et[ei]] + CT, [t_pb[pl]])
                if c == 1 and kb == nkb - 1:
                    epilogue(j, G, h)

            def epilogue(j, G, h):
                k = epi_n[0] % 2
                epi_n[0] += 1
                to, tr = (0, 1) if k == 0 else (6, 7)
                ACT(tf[to][:], pbank[PO[0]][:], AF.Copy, [t_pb[PO[0]]], [t_tf[to]])
                ACT(tf[tr][:], pbank[PO[1]][:], AF.Copy, [t_pb[PO[1]]], [t_tf[tr]])
                S.emit("dve", lambda e: e.tensor_copy(out=tf[4][:], in_=pbank[PL[0]][:]), [t_pb[PL[0]]], [t_tf[4]])
                S.emit("dve", lambda e: e.tensor_copy(out=tf[5][:], in_=pbank[PL[1]][:]), [t_pb[PL[1]]], [t_tf[5]])
                S.emit("dve", lambda e: e.reciprocal(out=tf[4][:], in_=tf[4][:]), [t_tf[4]], [t_tf[4]])
                S.emit("dve", lambda e: e.reciprocal(out=tf[5][:], in_=tf[5][:]), [t_tf[5]], [t_tf[5]])
                TT("dve", tf[to][:], tf[to][:], tf[4][:], ALU.mult, [t_tf[to], t_tf[4]], [t_tf[to]])
                TT("dve", tf[tr][:], tf[tr][:], tf[5][:], ALU.mult, [t_tf[tr], t_tf[5]], [t_tf[tr]])
                STT("dve", tf[to][:], tf[tr][:], nlam, tf[to][:], ALU.mult, ALU.add, [t_tf[to], t_tf[tr]] + CT, [t_tf[to]])
                TT("dve", sqb[k][:], tf[to][:], tf[to][:], ALU.mult, [t_tf[to]], [t_sqb[k]])

                def part3a():
                    bq = pj.next()
                    state[("bq", k)] = bq
                    MM(pbank[bq][:], ones128[:], sqb[k][:], True, True, [t_sqb[k]] + CT, [t_pb[bq]])
                    TS("dve", tf[tr][:], pbank[bq][:], 1e-6, None, ALU.add, None, [t_pb[bq]], [t_tf[tr]])

                def part3b():
                    RSQRT(tf[tr][:], 512, [t_tf[tr]] + CT, [t_tf[tr]])
                    STT("dve", tf[to][:], tf[to][:], gsub, tf[tr][:], ALU.mult, ALU.mult, [t_tf[to], t_tf[tr]] + CT, [t_tf[to]])
                    TT("dve", oagT[:, h, G * 512:(G + 1) * 512], tf[to][:], sza[:, j, G * 512:(G + 1) * 512], ALU.mult,
                       [t_tf[to], t_sz], [t_oag[h][G]])
                deferred.append([EPI_D1, part3a])
                deferred.append([EPI_D2, part3b])

            def tick():
                for d in list(deferred):
                    d[0] -= 1
                    if d[0] <= 0:
                        deferred.remove(d)
                        d[1]()

            n = len(tasks)
            LOOK = 1
            for i in range(min(LOOK, n)):
                stage1(tasks[i])
            for i in range(n):
                if i + LOOK < n:
                    stage1(tasks[i + LOOK])
                stage2(tasks[i])
                tick()
            while deferred:
                tick()

        def phase_A(seq):
            for hg in range(4):
                def fq(widx, hg=hg):
                    for j in range(2):
                        for tt in range(4):
                            b = pj.next()
                            proj_fm(widx, j * 128, tt, b)
                            ACT(qT[:, j, tt * 512:(tt + 1) * 512], pbank[b][:], AF.Copy, [t_pb[b]], [t_q], scale=0.125)
                add_step([[w_in[:, hg * 256:(hg + 1) * 256]]], fq)

                def fk(widx, hg=hg):
                    for j in range(2):
                        for tt in range(4):
                            b = pj.next()
                            proj_fm(widx, j * 128, tt, b)
                            S.emit("dve", lambda e, b=b, j=j, tt=tt: e.tensor_copy(out=kT[:, j, tt * 512:(tt + 1) * 512], in_=pbank[b][:]),
                                   [t_pb[b]], [t_k])
                add_step([[w_in[:, 1024 + hg * 256:1024 + (hg + 1) * 256]]], fk)

                def fv(widx, hg=hg):
                    for tb in range(16):
                        b = pj.next()
                        tt = tb // 4
                        for kc in range(8):
                            MM(pbank[b][:, 0:256], hT[:, kc, tb * 128:(tb + 1) * 128], wb[widx][:, kc, 0:256],
                               kc == 0, kc == 7, [t_wb[widx], t_hT[kc][tt]], [t_pb[b]])
                        if tb % 2 == 0:
                            ACT(vv[:, tb, :], pbank[b][:, 0:256], AF.Copy, [t_pb[b]], [t_v])
                        else:
                            S.emit("dve", lambda e, b=b, tb=tb: e.tensor_copy(out=vv[:, tb, :], in_=pbank[b][:, 0:256]), [t_pb[b]], [t_v])
                add_step([[w_in[:, 2048 + hg * 256:2048 + (hg + 1) * 256]]], fv)

                def fz(widx, hg=hg):
                    for j in range(2):
                        for tt in range(4):
                            b = pj.next()
                            proj_fm(widx, j * 128, tt, b)
                            silu2_from_psum(b, sza[:, j, tt * 512:(tt + 1) * 512], t_sz, 4 + (tt % 2))
                add_step([[w_in[:, 3072 + hg * 256:3072 + (hg + 1) * 256]]], fz)

                def fa(hg=hg):
                    attn_group(hg)
                add_step([], fa)
            for qd in range(4):
                merge_step(seq, 0, qd, w_oa)

        def phase_B(seq):
            def bar():
                S.barrier(S.COMPUTE)
            add_step([], bar)
            for qd in range(4):
                def fc(widx, qd=qd):
                    ci = qd % 2
                    S.emit("dve", lambda e, ci=ci: e.memset(cT[ci][:, :, 0:30], 0.0), [], [t_cT[ci]])
                    for dci in range(2):
                        dc = qd * 2 + dci
                        for tt in range(4):
                            ba = pj.next()
                            proj_fm(widx, dci * 128, tt, ba)
                            bg = pj.next()
                            proj_fm(widx, 256 + dci * 128, tt, bg)
                            ACT(tf[4][:], pbank[bg][:], AF.Tanh, [t_pb[bg]], [t_tf[4]], scale=0.5)
                            STT("dve", cT[ci][:, dci, 30 + tt * 512:30 + (tt + 1) * 512], tf[4][:], 1.0, pbank[ba][:],
                                ALU.add, ALU.mult, [t_tf[4], t_pb[ba]], [t_cT[ci]])
                    for dci in range(2):
                        dc = qd * 2 + dci
                        for jt in range(31):
                            if jt % 2 == 0:
                                ACT(diag[:, jt, :], identf[:], AF.Copy, CT, [t_diag[0]], scale=convw[:, dc, jt:jt + 1])
                            else:
                                TS("dve", diag[:, jt, :], identf[:], convw[:, dc, jt:jt + 1], None, ALU.mult, None, CT, [t_diag[1]])
                        for tt in range(4):
                            b = pj.next()
                            for jt in range(31):
                                MM(pbank[b][:], diag[:, jt, :], cT[ci][:, dci, tt * 512 + jt:tt * 512 + jt + 512],
                                   jt == 0, jt == 30, [t_diag[jt % 2], t_cT[ci]], [t_pb[b]])
                            ACT(oagT[:, dc, tt * 512:(tt + 1) * 512], pbank[b][:], AF.Identity, [t_pb[b]] + CT, [t_oag[dc][tt]],
                                bias=cols[:, C_CB + dc:C_CB + dc + 1])
                add_step([[w_in[:, 4096 + qd * 256:4096 + (qd + 1) * 256], w_in[:, 5120 + qd * 256:5120 + (qd + 1) * 256]]], fc)

            def f2(w0, w1):
                wz = (w0, w1)
                for tt in range(4):
                    sl = slice(tt * 512, (tt + 1) * 512)
                    bmu, bms = PO[0], PO[1]
                    for dc in range(8):
                        MM(pbank[bmu][:], ones1024[:], oagT[:, dc, sl], dc == 0, dc == 7, [t_oag[dc][tt]] + CT, [t_pb[bmu]])
                    for dc in range(8):
                        ei = et_r.next()
                        ACT(et[ei][:], oagT[:, dc, sl], AF.Square, [t_oag[dc][tt]], [t_et[ei]])
                        MM(pbank[bms][:], ones1024[:], et[ei][:], dc == 0, dc == 7, [t_et[ei]] + CT, [t_pb[bms]])
                    ACT(tf[0][:], pbank[bmu][:], AF.Copy, [t_pb[bmu]], [t_tf[0]])
                    TT("dve", tf[1][:], tf[0][:], tf[0][:], ALU.mult, [t_tf[0]], [t_tf[1]])
                    TT("dve", tf[1][:], pbank[bms][:], tf[1][:], ALU.subtract, [t_pb[bms], t_tf[1]], [t_tf[1]])
                    TS("dve", tf[1][:], tf[1][:], 1e-6, None, ALU.add, None, [t_tf[1]], [t_tf[1]])
                    RSQRT(tf[1][:], 512, [t_tf[1]] + CT, [t_tf[1]])
                    STT("dve", tf[0][:], tf[0][:], -1.0, tf[1][:], ALU.mult, ALU.mult, [t_tf[0], t_tf[1]], [t_tf[0]])
                    for dc in range(8):
                        TT("dve", tf[2][:], oagT[:, dc, sl], tf[1][:], ALU.mult, [t_oag[dc][tt], t_tf[1]], [t_tf[2]])
                        TT("dve", tf[2][:], tf[2][:], tf[0][:], ALU.add, [t_tf[2], t_tf[0]], [t_tf[2]])
                        ACT(tf[3][:], tf[2][:], AF.Tanh, [t_tf[2]] + CT, [t_tf[3]], scale=hlng[:, dc:dc + 1], bias=hlnb[:, dc:dc + 1])
                        ACT(tf[2][:], tf[2][:], AF.Identity, [t_tf[2]] + CT, [t_tf[2]], scale=cols[:, C_LNG + dc:C_LNG + dc + 1],
                            bias=cols[:, C_LNB + dc:C_LNB + dc + 1])
                        STT("dve", tf[2][:], tf[3][:], 1.0, tf[2][:], ALU.add, ALU.mult, [t_tf[3], t_tf[2]], [t_tf[2]])
                        bz = pj.next()
                        proj_fm(wz[dc // 4], (dc % 4) * 128, tt, bz)
                        ACT(tf[4][:], pbank[bz][:], AF.Tanh, [t_pb[bz]], [t_tf[4]], scale=0.5)
                        STT("dve", tf[5][:], tf[4][:], 1.0, pbank[bz][:], ALU.add, ALU.mult, [t_tf[4], t_pb[bz]], [t_tf[5]])
                        STT("dve", oagT[:, dc, sl], tf[2][:], 0.25, tf[5][:], ALU.mult, ALU.mult, [t_tf[2], t_tf[5]], [t_oag[dc][tt]])
            add_step([[w_in[:, 6144:6656]], [w_in[:, 6656:7168]]], f2)
            for qd in range(4):
                merge_step(seq, 1, qd, w_ob)

        def phase_C(seq):
            def fmem():
                S.barrier(S.ENGS, dma_toks=t_xt + t_xo)
                rmsnorm_T(mem_d[seq * 256:(seq + 1) * 256, :], 2, C_GMEM, memT, lambda kc, tb: t_memT, 256)
                S.barrier(S.COMPUTE, dma_toks=t_xt + t_xo)
            add_step([], fmem)
            for p in range(2):
                def fkk(widx, p=p):
                    for ci in range(4):
                        b = pj.next()
                        for kc in range(8):
                            MM(pbank[b][:, 0:256], wb[widx][:, kc, ci * 128:(ci + 1) * 128], memT[:, kc, :], kc == 0, kc == 7,
                               [t_wb[widx], t_memT], [t_pb[b]])
                        ACT(kmT[:, p * 4 + ci, :], pbank[b][:, 0:256], AF.Copy, [t_pb[b]], [t_kmT], scale=1.0 / 16)
                add_step([[w_kv[:, p * 512:(p + 1) * 512]]], fkk)
            for p in range(2):
                def fvv(widx, p=p):
                    for kb in range(2):
                        b = pj.next()
                        for kc in range(8):
                            MM(pbank[b][:], memT[:, kc, kb * 128:(kb + 1) * 128], wb[widx][:, kc, :], kc == 0, kc == 7,
                               [t_wb[widx], t_memT], [t_pb[b]])
                        ACT(vm[:, kb, p * 512:(p + 1) * 512], pbank[b][:], AF.Copy, [t_pb[b]], [t_vm])
                add_step([[w_kv[:, 1024 + p * 512:1024 + (p + 1) * 512]]], fvv)
            for hx in range(4):
                def fx(widx, hx=hx):
                    for dci in range(2):
                        for tt in range(4):
                            b = pj.next()
                            proj_fm(widx, dci * 128, tt, b)
                            ACT(qcT[:, dci, tt * 512:(tt + 1) * 512], pbank[b][:], AF.Copy, [t_pb[b]], [t_qc])
                            b2 = pj.next()
                            proj_fm(widx, 256 + dci * 128, tt, b2)
                            silu2_from_psum(b2, szc[:, dci, tt * 512:(tt + 1) * 512], t_szc, 4 + (tt % 2))
                    def s1(G):
                        sl = slice(G * 512, (G + 1) * 512)
                        eis = []
                        for kb in range(2):
                            bs = ps_r.next()
                            for dci in range(2):
                                MM(pbank[bs][:], kmT[:, hx * 2 + dci, kb * 128:(kb + 1) * 128], qcT[:, dci, sl], dci == 0, dci == 1,
                                   [t_kmT, t_qc], [t_pb[bs]])
                            ei = et_r.next()
                            ACT(et[ei][:], pbank[bs][:], AF.Exp, [t_pb[bs]], [t_et[ei]])
                            eis.append(ei)
                        return eis

                    def s2(G, eis):
                        sl = slice(G * 512, (G + 1) * 512)
                        for ec in range(2):
                            for kb in range(2):
                                MM(pbank[PO[ec]][:], vm[:, kb, hx * 256 + ec * 128:hx * 256 + (ec + 1) * 128], et[eis[kb]][:],
                                   kb == 0, kb == 1, [t_vm, t_et[eis[kb]]], [t_pb[PO[ec]]])
                        for kb in range(2):
                            MM(pbank[PL[0]][:], ones_b[:], et[eis[kb]][:], kb == 0, kb == 1, [t_et[eis[kb]]] + CT, [t_pb[PL[0]]])
                        for ec in range(2):
                            ACT(tf[1 + ec][:], pbank[PO[ec]][:], AF.Copy, [t_pb[PO[ec]]], [t_tf[1 + ec]])
                        S.emit("dve", lambda e: e.tensor_copy(out=tf[0][:], in_=pbank[PL[0]][:]), [t_pb[PL[0]]], [t_tf[0]])
                        S.emit("dve", lambda e: e.reciprocal(out=tf[0][:], in_=tf[0][:]), [t_tf[0]], [t_tf[0]])
                        for ec in range(2):
                            TT("dve", tf[1 + ec][:], tf[1 + ec][:], tf[0][:], ALU.mult, [t_tf[1 + ec], t_tf[0]], [t_tf[1 + ec]])
                            STT("dve", oagT[:, hx * 2 + ec, sl], tf[1 + ec][:], 0.5, szc[:, ec, sl], ALU.mult, ALU.mult,
                                [t_tf[1 + ec], t_szc], [t_oag[hx * 2 + ec][G]])
                    e_next = s1(0)
                    for G in range(4):
                        e_cur = e_next
                        if G < 3:
                            e_next = s1(G + 1)
                        s2(G, e_cur)
                add_step([[w_in[:, 7168 + hx * 256:7168 + (hx + 1) * 256], w_in[:, 8192 + hx * 256:8192 + (hx + 1) * 256]]], fx)
            for qd in range(4):
                merge_step(seq, 2, qd, w_oc)

        def phase_D(seq):
            def fd(w0, w1):
                S.barrier(S.ENGS, dma_toks=[])
                wz = (w0, w1)
                for tb in range(16):
                    bi = tb % 2
                    tt = tb // 4
                    S.dma_load("sp", t_xt[bi], xt[bi], x_d[seq * SEQ + tb * 128:seq * SEQ + (tb + 1) * 128, :])
                    banks = (PO[bi], PL[bi])
                    for hf in range(2):
                        for kc in range(8):
                            MM(pbank[banks[hf]][:], m2T[:, kc, tb * 128:(tb + 1) * 128], wb[wz[hf]][:, kc, :], kc == 0, kc == 7,
                               [t_wb[wz[hf]], t_m2[kc][tt]], [t_pb[banks[hf]]])
                    S.emit("dve", lambda e: e.memset(ssum[:, 2:4], 0.0), [], [t_ssum])
                    for hf in range(2):
                        ACT(junk[:, 0:512], pbank[banks[hf]][:], AF.Square, [t_pb[banks[hf]], t_ssum], [t_junk, t_ssum],
                            accum_out=ssum[:, 2 + hf:3 + hf])
                    TT("dve", ssum[:, 4:5], ssum[:, 2:3], ssum[:, 3:4], ALU.add, [t_ssum], [t_ssum])
                    TS("dve", ssum[:, 4:5], ssum[:, 4:5], 0.25 / DM, 1e-6, ALU.mult, ALU.add, [t_ssum], [t_ssum])
                    RSQRT(ssum[:, 4:5], 1, [t_ssum] + CT, [t_ssum])
                    TS("dve", ssum[:, 5:6], ssum[:, 4:5], 0.5, None, ALU.mult, None, [t_ssum], [t_ssum])
                    for hf in range(2):
                        cs = slice(hf * 512, (hf + 1) * 512)
                        STT("dve", xo[bi][:, cs], pbank[banks[hf]][:], ssum[:, 5:6], gpost[:, cs], ALU.mult, ALU.mult,
                            [t_pb[banks[hf]], t_ssum] + CT, [t_xo[bi]])
                        TT("dve", xo[bi][:, cs], xo[bi][:, cs], xt[bi][:, cs], ALU.add, [t_xo[bi], t_xt[bi]], [t_xo[bi]])
                    S.dma_store("sp", t_xo[bi], out_d[seq * SEQ + tb * 128:seq * SEQ + (tb + 1) * 128, :], xo[bi])
            add_step([[w_out[:, 0:512]], [w_out[:, 512:1024]]], fd)

        for seq in range(NSEQ):
            def f0(seq=seq):
                S.barrier(S.ENGS, dma_toks=t_xt + t_xo)
                rmsnorm_T(x_d[seq * SEQ:(seq + 1) * SEQ, :], 16, C_GPRE, hT, lambda kc, tb: t_hT[kc][tb // 4], SEQ)
                S.barrier(S.COMPUTE, dma_toks=t_xt + t_xo)
            add_step([], f0)
            if "A" in phases:
                phase_A(seq)
            if "B" in phases:
                phase_B(seq)
            if "C" in phases:
                phase_C(seq)
            if debug and seq == 0:
                def fdbg():
                    t_dbg = S.tok("dbg")
                    for name in debug:
                        src = {"hT": hT, "m2T": m2T, "oagT": oagT}[name]
                        rd = hT_all if name == "hT" else [t for r in (t_m2 if name == "m2T" else t_oag) for t in r]
                        for kc in range(8):
                            for tt in range(4):
                                S.emit("dve", lambda e, kc=kc, tt=tt, src=src: e.tensor_copy(out=tf[0][:], in_=src[:, kc, tt * 512:(tt + 1) * 512]),
                                       rd, [t_tf[0]])
                                S.dma_store("sp", t_tf[0], dbg_d[name][kc * 128:(kc + 1) * 128, tt * 512:(tt + 1) * 512], tf[0][:])
                add_step([], fdbg)
            if "D" in phases:
                phase_D(seq)
        if phases != "0ABCD":
            def ftouch():
                t_touch = S.tok("touch")
                for ap in (mem_d, w_in, w_oa, w_ob, w_oc, w_out, w_kv):
                    S.dma_load("sp", t_touch, ssum[0:1, 0:8], ap[0:1, 0:8])
                S.emit("dve", lambda e: e.memset(tf[1][:], 0.0), [], [t_tf[1]])
                S.dma_store("sp", t_tf[1], out_d[0:128, 0:512], tf[1][:])
            add_step([], ftouch)
        run_steps()
        S.build()
        print("instr counts", {e: len(S.prog[e]) for e in S.ENGS}, "signals", S.sigcount, "waits", S.nwaits)
    return nc


def _t5_bucket_host(rel):
    nb, max_exact = 16, 8
    try:
        import jax
        import jax.numpy as jnp
        cpu = jax.devices("cpu")[0]
        with jax.default_device(cpu):
            r = jnp.asarray(rel, dtype=jnp.int32)
            ret = (r > 0).astype(jnp.int32) * nb
            n = jnp.abs(r)
            nf = jnp.maximum(n, 1).astype(jnp.float32)
            large = max_exact + (jnp.log(nf / max_exact) / math.log(128 / max_exact) * (nb - max_exact)).astype(jnp.int32)
            large = jnp.minimum(large, nb - 1)
            return np.asarray(ret + jnp.where(n < max_exact, n, large))
    except Exception:
        ret = (rel > 0).astype(np.int32) * nb
        n = np.abs(rel)
        nf = np.maximum(n, 1).astype(np.float32)
        large = max_exact + (np.log(nf / np.float32(max_exact)) / np.float32(math.log(128 / max_exact))
                             * np.float32(nb - max_exact)).astype(np.int32)
        large = np.minimum(large, nb - 1)
        return ret + np.where(n < max_exact, n, large)


def _colvec(v):
    v = np.asarray(v, np.float32).reshape(-1)
    return np.ascontiguousarray(v.reshape(-1, 128).T)


def prep_shared(rel_bias, g_pre, g_mem, w_in, b_merge, lam_q1, lam_k1, lam_q2, lam_k2, g_subln, w_oa, conv_w,
                conv_b, ln_g, ln_b, w_ob, w_mem_kv, w_oc, w_out, g_post):
    f = lambda a: np.ascontiguousarray(np.asarray(a, np.float32))
    rel_bias = f(rel_bias)
    cols = np.zeros((128, NCOLS), np.float32)
    cols[:, C_GPRE:C_GPRE + 8] = _colvec(g_pre[0])
    cols[:, C_GMEM:C_GMEM + 8] = _colvec(g_mem[0])
    cols[:, C_BM:C_BM + 24] = _colvec(b_merge[0])
    cols[:, C_CB:C_CB + 8] = _colvec(conv_b[0])
    cols[:, C_LNG:C_LNG + 8] = _colvec(ln_g[0])
    cols[:, C_LNB:C_LNB + 8] = _colvec(ln_b[0])
    cols[:, C_GSUB:C_GSUB + 1] = _colvec(g_subln[0])
    far_b = int(_t5_bucket_host(np.array([-1000], np.int32))[0])
    cols[:, C_CFAR:C_CFAR + 8] = np.broadcast_to(rel_bias[far_b][None, :], (128, 8))
    cw = f(conv_w)[0]
    convw = np.ascontiguousarray(cw.reshape(31, 8, 128).transpose(2, 1, 0)).reshape(128, 8 * 31)
    kk = np.arange(128, dtype=np.int32)[:, None]
    qq = np.arange(128, dtype=np.int32)[None, :]
    relD = kk - qq
    relU = kk - qq - 128
    bD = _t5_bucket_host(relD)
    bU = _t5_bucket_host(relU)
    allowed = (kk // 64) <= (qq // 64)
    ext = np.concatenate([rel_bias, np.full((1, 8), NEG, np.float32)], axis=0)
    idxD = np.where(allowed, bD, 32)
    D = ext[idxD]
    U = ext[bU]
    biasdu = np.ascontiguousarray(np.concatenate([D, U], axis=1).transpose(0, 2, 1)).reshape(128, 8 * 256)
    lamv = np.concatenate([f(lam_q1)[0], f(lam_k1)[0], f(lam_q2)[0], f(lam_k2)[0]]).astype(np.float32)
    return {
        "w_in": f(w_in)[0], "w_oa": f(w_oa)[0], "w_ob": f(w_ob)[0], "w_oc": f(w_oc)[0], "w_out": f(w_out)[0],
        "w_mem_kv": f(w_mem_kv)[0], "cols": cols, "convw": convw, "g_post": f(g_post)[0], "lamv": lamv,
        "biasdu": biasdu.astype(np.float32),
    }


_PROG = {}


def kernel(x, mem, rel_bias, g_pre, g_mem, w_in, b_merge, lam_q1, lam_k1, lam_q2, lam_k2, g_subln, w_oa, conv_w,
           conv_b, ln_g, ln_b, w_ob, w_mem_kv, w_oc, w_out, g_post):
    shared = prep_shared(rel_bias, g_pre, g_mem, w_in, b_merge, lam_q1, lam_k1, lam_q2, lam_k2, g_subln, w_oa,
                         conv_w, conv_b, ln_g, ln_b, w_ob, w_mem_kv, w_oc, w_out, g_post)
    x = np.asarray(x, np.float32)
    mem = np.asarray(mem, np.float32)
    B = x.shape[0]
    per = B // NCORES
    if "p" not in _PROG:
        _PROG["p"] = build_program(NSEQ=per)
    nc = _PROG["p"]
    in_maps = []
    for c in range(NCORES):
        m = dict(shared)
        m["x"] = np.ascontiguousarray(x[c * per:(c + 1) * per].reshape(per * SEQ, DM))
        m["mem"] = np.ascontiguousarray(mem[c * per:(c + 1) * per].reshape(per * 256, DM))
        in_maps.append(m)
    res = run_bass_kernel_spmd(nc, in_maps, core_ids=list(range(NCORES)))
    outs = [np.asarray(r["out"], np.float32).reshape(per, SEQ, DM) for r in res.results]
    return np.concatenate(outs, axis=0)
```

```python
import math
from contextlib import ExitStack
import numpy as np
import concourse.bass as bass
import concourse.mybir as mybir
from concourse.bass_utils import run_bass_kernel_spmd

F32 = mybir.dt.float32
BF16 = mybir.dt.bfloat16
AF = mybir.ActivationFunctionType
ALU = mybir.AluOpType
AX = mybir.AxisListType

SAME_ENGINE_SYNC = True
EPOCH = 20000
NEG = -30000.0
SEQ = 2048
DM = 1024
NCORES = 8


class Tk:
    __slots__ = ("name", "lw", "rd", "dsem", "dcnt")

    def __init__(self, name):
        self.name = name
        self.lw = None
        self.rd = {}
        self.dsem = None
        self.dcnt = 0


class Ins:
    __slots__ = ("eng", "fn", "cdeps", "raw", "dwaits", "sig", "val", "dma", "waits")

    def __init__(self, eng, fn):
        self.eng = eng
        self.fn = fn
        self.cdeps = set()
        self.raw = set()
        self.dwaits = {}
        self.sig = False
        self.val = None
        self.dma = None
        self.waits = []


class Sched:
    ENGS = ("pe", "act", "dve", "pool", "sp")
    COMPUTE = ("pe", "act", "dve", "pool")

    def __init__(self, nc, stack):
        self.nc = nc
        self.stack = stack
        self.prog = {e: [] for e in self.ENGS}
        self.esem = {e: [] for e in self.ENGS}
        self.toks = []
        self.last_compute = {e: None for e in self.ENGS}
        self.pending = {e: (set(), {}) for e in self.ENGS}

    def tok(self, name):
        t = Tk(name)
        self.toks.append(t)
        return t

    def _dep(self, ins, p, raw):
        if p is None or p is ins:
            return
        if p.dma is not None:
            t = p.dma
            ins.dwaits[t] = max(ins.dwaits.get(t, 0), 16 * t.dcnt)
        else:
            ins.cdeps.add(p)
            if raw:
                ins.raw.add(p)

    def emit(self, eng, fn, reads=(), writes=(), dma=None):
        ins = Ins(eng, fn)
        pc, pd = self.pending[eng]
        if pc or pd:
            for p in pc:
                if p.eng != eng:
                    ins.cdeps.add(p)
            for t, v in pd.items():
                ins.dwaits[t] = max(ins.dwaits.get(t, 0), v)
            self.pending[eng] = (set(), {})
        for t in reads:
            self._dep(ins, t.lw, True)
        for t in writes:
            self._dep(ins, t.lw, False)
            for r in t.rd.values():
                self._dep(ins, r, False)
        if dma is not None:
            if dma.dsem is None:
                dma.dsem = self.stack.enter_context(self.nc.semaphore("ds_" + dma.name))
            dma.dcnt += 1
            ins.dma = dma
        else:
            self.last_compute[eng] = ins
        for t in reads:
            key = eng if dma is None else ("dma", id(ins))
            t.rd[key] = ins
        for t in writes:
            t.lw = ins
            t.rd = {}
        self.prog[eng].append(ins)
        return ins

    def barrier(self, engs, dma_toks=()):
        lasts = set(self.last_compute[e] for e in self.COMPUTE if self.last_compute[e] is not None)
        for e in engs:
            pc, pd = self.pending[e]
            pc |= lasts
            for t in dma_toks:
                if t.dsem is not None:
                    pd[t] = max(pd.get(t, 0), 16 * t.dcnt)

    def dma_load(self, q, tok, out_ap, in_ap, **kw):
        return self.emit(q, lambda e: e.dma_start(out=out_ap, in_=in_ap, **kw), writes=[tok], dma=tok)

    def dma_store(self, q, tok, out_ap, in_ap, **kw):
        return self.emit(q, lambda e: e.dma_start(out=out_ap, in_=in_ap, **kw), reads=[tok], dma=tok)

    def _needs(self, ins, p):
        if p.eng != ins.eng:
            return True
        return SAME_ENGINE_SYNC and p.eng != "pe"

    def finalize(self):
        for e in self.ENGS:
            for ins in self.prog[e]:
                for p in ins.cdeps:
                    if self._needs(ins, p):
                        p.sig = True
        for e in self.ENGS:
            c = 0
            for ins in self.prog[e]:
                if ins.sig:
                    ep, v = divmod(c, EPOCH)
                    if ep >= len(self.esem[e]):
                        self.esem[e].append(self.stack.enter_context(self.nc.semaphore("es_%s%d" % (e, ep))))
                    ins.val = (ep, v + 1)
                    c += 1
        self.sigcount = {e: sum(1 for i in self.prog[e] if i.sig) for e in self.ENGS}
        self.nwaits = 0
        for e in self.ENGS:
            known = {}
            for ins in self.prog[e]:
                need = {}
                for p in ins.cdeps:
                    if not self._needs(ins, p):
                        continue
                    need[p.eng] = max(need.get(p.eng, (0, 0)), p.val)
                for pe_, v in need.items():
                    if known.get(pe_, (0, 0)) < v:
                        ins.waits.append((self.esem[pe_][v[0]], v[1]))
                        known[pe_] = v
                for t, v in ins.dwaits.items():
                    k = ("d", id(t))
                    if known.get(k, 0) < v:
                        ins.waits.append((t.dsem, v))
                        known[k] = v
                self.nwaits += len(ins.waits)

    def simulate(self):
        sem = {}
        pc = {e: 0 for e in self.ENGS}
        progress = True
        while progress:
            progress = False
            for e in self.ENGS:
                while pc[e] < len(self.prog[e]):
                    ins = self.prog[e][pc[e]]
                    if all(sem.get(id(s), 0) >= v for (s, v) in ins.waits):
                        if ins.dma is not None:
                            sem[id(ins.dma.dsem)] = sem.get(id(ins.dma.dsem), 0) + 16
                        elif ins.sig:
                            k = id(self.esem[e][ins.val[0]])
                            sem[k] = sem.get(k, 0) + 1
                        pc[e] += 1
                        progress = True
                    else:
                        break
        stuck = {e: (pc[e], len(self.prog[e])) for e in self.ENGS if pc[e] < len(self.prog[e])}
        if stuck:
            for e in stuck:
                ins = self.prog[e][pc[e]]
                print("STUCK", e, pc[e], [(s.name, v, sem.get(id(s), 0)) for (s, v) in ins.waits])
            raise RuntimeError("deadlock in generated program: %s" % stuck)

    def run_engine(self, ename, e):
        for ins in self.prog[ename]:
            for (s, v) in ins.waits:
                e.wait_ge(s, v)
            bi = ins.fn(e)
            if ins.dma is not None:
                bi.then_inc(ins.dma.dsem, 16)
            elif ins.sig:
                bi.then_inc(self.esem[ename][ins.val[0]], 1)

    def build(self):
        fin = {}
        for t in self.toks:
            if t.dsem is not None:
                fin[t] = 16 * t.dcnt
        self.finalize()
        self.simulate()
        nc = self.nc
        with nc.Block() as block:
            @block.tensor
            def _(e):
                self.run_engine("pe", e)

            @block.scalar
            def _(e):
                self.run_engine("act", e)

            @block.vector
            def _(e):
                self.run_engine("dve", e)

            @block.gpsimd
            def _(e):
                self.run_engine("pool", e)

            @block.sync
            def _(e):
                self.run_engine("sp", e)
                for t, v in fin.items():
                    e.wait_ge(t.dsem, v)


STEP_LOG = []
EPI_D1 = 13
EPI_D2 = 17
NCOLS = 80
C_GPRE, C_GMEM, C_BM, C_CB, C_LNG, C_LNB, C_GSUB, C_CFAR = 0, 8, 16, 40, 48, 56, 64, 65


def build_program(NSEQ=2, debug=None, phases="0ABCD"):
    nc = bass.Bass("TRN2", target_bir_lowering=False)
    dram = lambda n, shp, kind="ExternalInput": nc.dram_tensor(n, shp, F32, kind=kind).ap()
    x_d = dram("x", [NSEQ * SEQ, DM])
    mem_d = dram("mem", [NSEQ * 256, DM])
    w_in = dram("w_in", [DM, 12288])
    w_oa = dram("w_oa", [DM, DM])
    w_ob = dram("w_ob", [DM, DM])
    w_oc = dram("w_oc", [DM, DM])
    w_out = dram("w_out", [DM, DM])
    w_kv = dram("w_mem_kv", [DM, 2048])
    cols_d = dram("cols", [128, NCOLS])
    convw_d = dram("convw", [128, 8 * 31])
    gpost_d = dram("g_post", [DM])
    lamv_d = dram("lamv", [256])
    bias_d = dram("biasdu", [128, 8 * 256])
    out_d = dram("out", [NSEQ * SEQ, DM], kind="ExternalOutput")
    dbg_d = {}
    if debug:
        for name, shp in debug.items():
            dbg_d[name] = dram("dbg_" + name, shp, kind="ExternalOutput")

    with ExitStack() as st:
        S = Sched(nc, st)
        sb = lambda name, shape, dt: st.enter_context(nc.sbuf_tensor("s_" + name, shape, dt))
        psb = lambda name, shape, dt: st.enter_context(nc.psum_tensor("p_" + name, shape, dt))

        cols = sb("cols", [128, NCOLS], F32)
        convw = sb("convw", [128, 8, 31], F32)
        gpost = sb("gpost", [128, DM], F32)
        lamv = sb("lamv", [128, 256], F32)
        bhi = sb("bhi", [128, 8, 256], BF16)
        blo = sb("blo", [128, 8, 256], BF16)
        identf = sb("identf", [128, 128], F32)
        identb = sb("identb", [128, 128], BF16)
        ones_b = sb("ones_b", [128, 128], BF16)
        ones128 = sb("ones128", [128, 128], BF16)
        ones1024 = sb("ones1024", [128, 128], BF16)
        mhalf = sb("mhalf", [128, 512], F32)
        small = sb("small", [128, 16], F32)
        hbm = sb("hbm", [128, 24], F32)
        hlng = sb("hlng", [128, 8], F32)
        hlnb = sb("hlnb", [128, 8], F32)
        t_const = S.tok("const")
        S.dma_load("sp", t_const, cols[:], cols_d)
        S.dma_load("sp", t_const, convw[:], convw_d.rearrange("p (c j) -> p c j", j=31))
        S.dma_load("sp", t_const, gpost[:], gpost_d.partition_broadcast(128))
        S.dma_load("sp", t_const, lamv[:], lamv_d.partition_broadcast(128))
        t_c2 = S.tok("const2")
        S.emit("pool", lambda e: e.memset(identf[:], 0.0), writes=[t_c2])
        S.emit("pool", lambda e: e.affine_select(out=identf[:], in_=identf[:], pattern=[[-1, 128]],
                                                 compare_op=ALU.not_equal, fill=1.0, base=0,
                                                 channel_multiplier=1), reads=[t_c2], writes=[t_c2])
        S.emit("pool", lambda e: e.memset(mhalf[:], -0.5), writes=[t_c2])
        S.emit("dve", lambda e: e.tensor_copy(out=identb[:], in_=identf[:]), reads=[t_c2], writes=[t_c2])
        S.emit("dve", lambda e: e.memset(ones_b[:], 1.0), writes=[t_c2])
        S.emit("dve", lambda e: e.memset(ones128[:], 1.0 / 128), writes=[t_c2])
        S.emit("dve", lambda e: e.memset(ones1024[:], 1.0 / 1024), writes=[t_c2])
        lam_init = 0.8 - 0.6 * math.exp(-0.3 * 0)
        S.emit("dve", lambda e: e.tensor_tensor(out=lamv[:, 0:64], in0=lamv[:, 0:64], in1=lamv[:, 64:128], op=ALU.mult),
               reads=[t_const], writes=[t_c2])
        S.emit("dve", lambda e: e.tensor_tensor(out=lamv[:, 128:192], in0=lamv[:, 128:192], in1=lamv[:, 192:256], op=ALU.mult),
               reads=[t_const], writes=[t_c2])
        S.emit("dve", lambda e: e.reduce_sum(out=small[:, 2:3], in_=lamv[:, 0:64], axis=AX.X), reads=[t_c2], writes=[t_c2])
        S.emit("dve", lambda e: e.reduce_sum(out=small[:, 3:4], in_=lamv[:, 128:192], axis=AX.X), reads=[t_c2], writes=[t_c2])
        S.emit("act", lambda e: e.activation(out=small[:, 4:6], in_=small[:, 2:4], func=AF.Exp), reads=[t_c2], writes=[t_c2])
        S.emit("dve", lambda e: e.tensor_tensor(out=small[:, 0:1], in0=small[:, 5:6], in1=small[:, 4:5], op=ALU.subtract),
               reads=[t_c2], writes=[t_c2])
        S.emit("dve", lambda e: e.tensor_scalar(out=small[:, 0:1], in0=small[:, 0:1], scalar1=-lam_init, scalar2=None, op0=ALU.add),
               reads=[t_c2], writes=[t_c2])
        S.emit("dve", lambda e: e.tensor_scalar(out=small[:, 1:2], in0=cols[:, C_GSUB:C_GSUB + 1], scalar1=0.5 * (1.0 - lam_init),
                                                scalar2=None, op0=ALU.mult), reads=[t_const], writes=[t_c2])
        S.emit("dve", lambda e: e.tensor_scalar(out=hbm[:], in0=cols[:, C_BM:C_BM + 24], scalar1=0.5, scalar2=None, op0=ALU.mult),
               reads=[t_const], writes=[t_c2])
        S.emit("dve", lambda e: e.tensor_scalar(out=hlng[:], in0=cols[:, C_LNG:C_LNG + 8], scalar1=0.5, scalar2=None, op0=ALU.mult),
               reads=[t_const], writes=[t_c2])
        S.emit("dve", lambda e: e.tensor_scalar(out=hlnb[:], in0=cols[:, C_LNB:C_LNB + 8], scalar1=0.5, scalar2=None, op0=ALU.mult),
               reads=[t_const], writes=[t_c2])
        S.emit("dve", lambda e: e.tensor_scalar(out=convw[:], in0=convw[:], scalar1=0.5, scalar2=None, op0=ALU.mult),
               reads=[t_const], writes=[t_c2])
        CT = [t_const, t_c2]
        nlam = small[:, 0:1]
        gsub = small[:, 1:2]

        hT = sb("hT", [128, 8, SEQ], BF16)
        m2T = sb("m2T", [128, 8, SEQ], BF16)
        big = sb("big", [128, 8192], F32)
        oagT = big[:].bitcast(BF16).rearrange("p (c t) -> p c t", c=8)
        xt = [big[:, i * 1024:(i + 1) * 1024] for i in range(2)]
        xo = [big[:, 2048 + i * 1024:2048 + (i + 1) * 1024] for i in range(2)]
        scr = sb("scr", [128, 16384], BF16)
        junk = sb("junk", [128, 1024], BF16)
        et = [sb("et%d" % i, [128, 512], BF16) for i in range(4)]
        tf = [sb("tf%d" % i, [128, 512], F32) for i in range(8)]
        sqb = [sb("sqb%d" % i, [128, 512], BF16) for i in range(2)]
        diag = sb("diag", [128, 31, 128], BF16)
        ssum = sb("ssum", [128, 8], F32)
        NB = 3
        wb = [sb("wb%d" % i, [128, 8, 512], BF16) for i in range(NB)]
        t_wb = [S.tok("wb%d" % i) for i in range(NB)]
        pbank = [psb("pb%d" % i, [128, 512], F32) for i in range(8)]
        t_pb = [S.tok("pb%d" % i) for i in range(8)]
        PJ, PS_, PO, PL = (0, 1), (2, 3), (4, 5), (6, 7)

        t_hT = [[S.tok("hT%d_%d" % (kc, tt)) for tt in range(4)] for kc in range(8)]
        t_m2 = [[S.tok("m2%d_%d" % (kc, tt)) for tt in range(4)] for kc in range(8)]
        t_oag = [[S.tok("oag%d_%d" % (kc, tt)) for tt in range(4)] for kc in range(8)]
        t_xt = [S.tok("xt%d" % i) for i in range(2)]
        t_xo = [S.tok("xo%d" % i) for i in range(2)]
        t_junk = S.tok("junk")
        t_et = [S.tok("et%d" % i) for i in range(4)]
        t_tf = [S.tok("tf%d" % i) for i in range(8)]
        t_sqb = [S.tok("sqb0"), S.tok("sqb1")]
        t_diag = [S.tok("diag0"), S.tok("diag1")]
        t_ssum = S.tok("ssum")
        hT_all = [t for r in t_hT for t in r]
        print("sbuf bytes remaining:", nc.sbuf_bytes_remaining)

        def MM(out, lhsT, rhs, start, stop, reads, writes):
            S.emit("pe", lambda e: e.matmul(out, lhsT=lhsT, rhs=rhs, start=start, stop=stop), reads, writes)

        def ACT(out, in_, func, reads, writes, eng="act", **kw):
            S.emit(eng, lambda e: e.activation(out=out, in_=in_, func=func, **kw), reads, writes)

        def TS(eng, out, in0, s1, s2, op0, op1, reads, writes):
            if s2 is None:
                S.emit(eng, lambda e: e.tensor_scalar(out=out, in0=in0, scalar1=s1, scalar2=None, op0=op0), reads, writes)
            else:
                S.emit(eng, lambda e: e.tensor_scalar(out=out, in0=in0, scalar1=s1, scalar2=s2, op0=op0, op1=op1), reads, writes)

        def STT(eng, out, in0, scalar, in1, op0, op1, reads, writes):
            S.emit(eng, lambda e: e.scalar_tensor_tensor(out=out, in0=in0, scalar=scalar, in1=in1, op0=op0, op1=op1), reads, writes)

        def TT(eng, out, in0, in1, op, reads, writes):
            S.emit(eng, lambda e: e.tensor_tensor(out=out, in0=in0, in1=in1, op=op), reads, writes)

        def RSQRT(ap, n, reads, writes):
            S.emit("act", lambda e: e.activation(out=ap, in_=ap, func=AF.Sqrt), reads, writes)
            S.emit("dve", lambda e: e.reciprocal(out=ap, in_=ap), writes, writes)

        class Rot:
            def __init__(self, items):
                self.items = items
                self.i = 0

            def next(self):
                v = self.items[self.i]
                self.i = (self.i + 1) % len(self.items)
                return v

        pj = Rot(list(PJ))
        ps_r = Rot(list(PS_))
        et_r = Rot([0, 1, 2, 3])
        ps3 = Rot([PS_[0], PS_[1], PJ[1]])
        pj6 = Rot([0, 1, 4, 5, 6, 7])
        pjz = Rot([0, 1, 6, 7])

        wstate = {"next": 0}

        def wload(slices):
            i = wstate["next"]
            wstate["next"] = (i + 1) % NB
            off = 0
            for ap in slices:
                n = ap.shape[1]
                S.dma_load("pool", t_wb[i], wb[i][:, :, off:off + n], ap.rearrange("(kc p) n -> p kc n", p=128))
                off += n
            return i

        steps = []

        def add_step(panels, fn):
            steps.append((panels, fn))

        def run_steps():
            loaded = {}
            for i, (panels, fn) in enumerate(steps):
                cur = loaded.setdefault(i, [])
                while len(cur) < len(panels):
                    cur.append(wload(panels[len(cur)]))
                if i + 1 < len(steps):
                    nxt_panels = steps[i + 1][0]
                    nxt = loaded.setdefault(i + 1, [])
                    free = NB - len(cur)
                    while len(nxt) < len(nxt_panels) and free > 0:
                        nxt.append(wload(nxt_panels[len(nxt)]))
                        free -= 1
                STEP_LOG.append((getattr(fn, "__name__", "?"), len(S.prog["pe"])))
                fn(*cur)

        def rmsnorm_T(src_rows_ap, ntb, gcol0, dst, dst_toks_fn, width):
            for tb in range(ntb):
                bi = tb % 2
                S.dma_load("sp", t_xt[bi], xt[bi], src_rows_ap[tb * 128:(tb + 1) * 128, :])
                S.emit("dve", lambda e: e.memset(ssum[:, 0:1], 0.0), [], [t_ssum])
                ACT(junk[:], xt[bi], AF.Square, [t_xt[bi], t_ssum], [t_junk, t_ssum], accum_out=ssum[:, 0:1])
                TS("dve", ssum[:, 1:2], ssum[:, 0:1], 1.0 / DM, 1e-6, ALU.mult, ALU.add, [t_ssum], [t_ssum])
                RSQRT(ssum[:, 1:2], 1, [t_ssum] + CT, [t_ssum])
                TS("dve", xo[bi], xt[bi], ssum[:, 1:2], None, ALU.mult, None, [t_xt[bi], t_ssum], [t_xo[bi]])
                for half in range(2):
                    b = ps_r.next()
                    for i in range(4):
                        kc = half * 4 + i
                        S.emit("pe", lambda e, b=b, i=i, kc=kc, bi=bi: e.transpose(
                            out=pbank[b][:, i * 128:(i + 1) * 128], in_=xo[bi][:, kc * 128:(kc + 1) * 128],
                            identity=identf[:]), [t_xo[bi]] + CT, [t_pb[b]])
                    for i in range(4):
                        kc = half * 4 + i
                        dt = dst_toks_fn(kc, tb)
                        ACT(dst[:, kc, tb * 128:(tb + 1) * 128], pbank[b][:, i * 128:(i + 1) * 128], AF.Copy,
                            [t_pb[b]] + CT, [dt], scale=cols[:, gcol0 + kc:gcol0 + kc + 1])

        qT = scr[:, 0:4096].rearrange("p (j t) -> p j t", j=2)
        kT = scr[:, 4096:8192].rearrange("p (j t) -> p j t", j=2)
        vv = scr[:, 8192:12288].rearrange("p (b n) -> p b n", b=16)
        sza = scr[:, 12288:16384].rearrange("p (j t) -> p j t", j=2)
        t_q, t_k, t_v, t_sz = S.tok("qT"), S.tok("kT"), S.tok("vv"), S.tok("sza")
        bsf = scr[:, 0:4096].bitcast(F32).rearrange("p (h k) -> p h k", k=256)
        S.dma_load("sp", t_q, bsf, bias_d.rearrange("p (h k) -> p h k", k=256))
        t_bias = S.tok("biashl")
        for h in range(8):
            TS("dve", bsf[:, h, :], bsf[:, h, :], cols[:, C_CFAR + h:C_CFAR + h + 1], None, ALU.subtract, None, [t_q] + CT, [t_q])
            S.emit("dve", lambda e, h=h: e.tensor_copy(out=bhi[:, h, :], in_=bsf[:, h, :]), [t_q], [t_bias])
            TT("dve", blo[:, h, :], bsf[:, h, :], bhi[:, h, :], ALU.subtract, [t_q, t_bias], [t_bias])
        CW = SEQ + 30
        cT = [scr[:, i * 2 * CW:(i + 1) * 2 * CW].rearrange("p (j t) -> p j t", j=2) for i in range(2)]
        t_cT = [S.tok("cT0"), S.tok("cT1")]
        memT = scr[:, 0:2048].rearrange("p (c t) -> p c t", c=8)
        kmT = scr[:, 2048:4096].rearrange("p (c t) -> p c t", c=8)
        vm = scr[:, 4096:6144].rearrange("p (b n) -> p b n", b=2)
        qcT = scr[:, 6144:10240].rearrange("p (j t) -> p j t", j=2)
        szc = scr[:, 10240:14336].rearrange("p (j t) -> p j t", j=2)
        t_memT, t_kmT, t_vm, t_qc, t_szc = S.tok("memT"), S.tok("kmT"), S.tok("vm"), S.tok("qcT"), S.tok("szc")

        def proj_fm(widx, col0, tt, bank):
            for kc in range(8):
                MM(pbank[bank][:], wb[widx][:, kc, col0:col0 + 128], hT[:, kc, tt * 512:(tt + 1) * 512],
                   kc == 0, kc == 7, [t_wb[widx], t_hT[kc][tt]], [t_pb[bank]])

        def silu2_from_psum(bank, dst, dst_tok, tfi):
            ACT(tf[tfi][:], pbank[bank][:], AF.Tanh, [t_pb[bank]], [t_tf[tfi]], scale=0.5)
            STT("dve", dst, tf[tfi][:], 1.0, pbank[bank][:], ALU.add, ALU.mult, [t_tf[tfi], t_pb[bank]], [dst_tok])

        first_b = min(i for i, p in enumerate("ABC") if p in phases) if any(p in phases for p in "ABC") else 0
        def merge_step(seq, b, qd, w_o):
            def fn(widx):
                for dci in range(2):
                    dc = qd * 2 + dci
                    for tt in range(4):
                        by = pj6.next()
                        for kc in range(8):
                            MM(pbank[by][:], wb[widx][:, kc, dci * 128:(dci + 1) * 128], oagT[:, kc, tt * 512:(tt + 1) * 512],
                               kc == 0, kc == 7, [t_wb[widx], t_oag[kc][tt]], [t_pb[by]])
                        bg = pj6.next()
                        for kc in range(8):
                            MM(pbank[bg][:], wb[widx][:, kc, 256 + dci * 128:256 + (dci + 1) * 128], hT[:, kc, tt * 512:(tt + 1) * 512],
                               kc == 0, kc == 7, [t_wb[widx], t_hT[kc][tt]], [t_pb[bg]])
                        ACT(tf[4][:], pbank[bg][:], AF.Tanh, [t_pb[bg]] + CT, [t_tf[4]], scale=0.5,
                            bias=hbm[:, b * 8 + dc:b * 8 + dc + 1])
                        dst = m2T[:, dc, tt * 512:(tt + 1) * 512]
                        if b == first_b:
                            STT("dve", dst, tf[4][:], 1.0, pbank[by][:], ALU.add, ALU.mult, [t_tf[4], t_pb[by]], [t_m2[dc][tt]])
                        else:
                            STT("dve", tf[5][:], tf[4][:], 1.0, pbank[by][:], ALU.add, ALU.mult, [t_tf[4], t_pb[by]], [t_tf[5]])
                            TT("dve", dst, tf[5][:], dst, ALU.add, [t_tf[5], t_m2[dc][tt]], [t_m2[dc][tt]])
            gcol = 9216 + b * 1024 + qd * 256
            add_step([[w_o[:, qd * 256:(qd + 1) * 256], w_in[:, gcol:gcol + 256]]], fn)

        def attn_group(hg):
            tasks = []
            for j in range(2):
                for G in range(4):
                    nkb = 4 * G + 4
                    for c in range(2):
                        for kb in range(nkb):
                            tasks.append((j, G, c, kb, nkb))
            state = {}
            deferred = []
            epi_n = [0]

            def stage1(t):
                j, G, c, kb, nkb = t
                h = 2 * hg + j
                r0, r1 = c * 64, (c + 1) * 64
                jj = kb - 4 * G
                c0 = max(jj, 0) * 128
                W = 512 - c0
                if jj >= 0:
                    boff, nw = 0, min(256, W)
                elif jj == -1:
                    boff, nw = 128, 128
                else:
                    boff, nw = 0, 0
                bs = ps3.next()
                MM(pbank[bs][:, 0:W], kT[r0:r1, j, kb * 128:(kb + 1) * 128],
                   qT[r0:r1, j, G * 512 + c0:(G + 1) * 512], True, nw == 0, [t_k, t_q], [t_pb[bs]])
                if nw > 0:
                    MM(pbank[bs][:, 0:nw], identb[:], bhi[:, h, boff:boff + nw], False, False, [t_bias] + CT, [t_pb[bs]])
                    MM(pbank[bs][:, 0:nw], identb[:], blo[:, h, boff:boff + nw], False, True, [t_bias] + CT, [t_pb[bs]])
                ei = et_r.next()
                ACT(et[ei][:, 0:W], pbank[bs][:, 0:W], AF.Exp, [t_pb[bs]], [t_et[ei]])
                state[t] = (ei, c0, W)

            def stage2(t):
                j, G, c, kb, nkb = t
                h = 2 * hg + j
                ei, c0, W = state.pop(t)
                po, pl = PO[c], PL[c]
                MM(pbank[po][:, c0:512], vv[:, kb, j * 128:(j + 1) * 128], et[ei][:, 0:W],
                   kb == 0, kb == nkb - 1, [t_v, t_et[ei]], [t_pb[po]])
                MM(pbank[pl][:, c0:512], ones_b[:], et[ei][:, 0:W],
                   kb == 0, kb == nkb - 1, [t_et[ei]] + CT, [t_pb[pl]])
                if c == 1 and kb == nkb - 1:
                    epilogue(j, G, h)

            def epilogue(j, G, h):
                k = epi_n[0] % 2
                epi_n[0] += 1
                to, tr = (0, 1) if k == 0 else (6, 7)
                ACT(tf[to][:], pbank[PO[0]][:], AF.Copy, [t_pb[PO[0]]], [t_tf[to]])
                ACT(tf[tr][:], pbank[PO[1]][:], AF.Copy, [t_pb[PO[1]]], [t_tf[tr]])
                S.emit("dve", lambda e: e.tensor_copy(out=tf[4][:], in_=pbank[PL[0]][:]), [t_pb[PL[0]]], [t_tf[4]])
                S.emit("dve", lambda e: e.tensor_copy(out=tf[5][:], in_=pbank[PL[1]][:]), [t_pb[PL[1]]], [t_tf[5]])
                S.emit("dve", lambda e: e.reciprocal(out=tf[4][:], in_=tf[4][:]), [t_tf[4]], [t_tf[4]])
                S.emit("dve", lambda e: e.reciprocal(out=tf[5][:], in_=tf[5][:]), [t_tf[5]], [t_tf[5]])
                TT("dve", tf[to][:], tf[to][:], tf[4][:], ALU.mult, [t_tf[to], t_tf[4]], [t_tf[to]])
                TT("dve", tf[tr][:], tf[tr][:], tf[5][:], ALU.mult, [t_tf[tr], t_tf[5]], [t_tf[tr]])
                STT("dve", tf[to][:], tf[tr][:], nlam, tf[to][:], ALU.mult, ALU.add, [t_tf[to], t_tf[tr]] + CT, [t_tf[to]])
                TT("dve", sqb[k][:], tf[to][:], tf[to][:], ALU.mult, [t_tf[to]], [t_sqb[k]])

                def part3a():
                    bq = PJ[0]
                    MM(pbank[bq][:], ones128[:], sqb[k][:], True, True, [t_sqb[k]] + CT, [t_pb[bq]])
                    TS("dve", tf[tr][:], pbank[bq][:], 1e-6, None, ALU.add, None, [t_pb[bq]], [t_tf[tr]])

                def part3b():
                    RSQRT(tf[tr][:], 512, [t_tf[tr]] + CT, [t_tf[tr]])
                    STT("dve", tf[to][:], tf[to][:], gsub, tf[tr][:], ALU.mult, ALU.mult, [t_tf[to], t_tf[tr]] + CT, [t_tf[to]])
                    TT("dve", oagT[:, h, G * 512:(G + 1) * 512], tf[to][:], sza[:, j, G * 512:(G + 1) * 512], ALU.mult,
                       [t_tf[to], t_sz], [t_oag[h][G]])
                deferred.append([EPI_D1, part3a])
                deferred.append([EPI_D2, part3b])

            def tick():
                for d in list(deferred):
                    d[0] -= 1
                    if d[0] <= 0:
                        deferred.remove(d)
                        d[1]()

            n = len(tasks)
            LOOK = 2
            for i in range(min(LOOK, n)):
                stage1(tasks[i])
            for i in range(n):
                if i + LOOK < n:
                    stage1(tasks[i + LOOK])
                stage2(tasks[i])
                tick()
            while deferred:
                tick()

        def phase_A(seq):
            for hg in range(4):
                def fq(widx, hg=hg):
                    for j in range(2):
                        for tt in range(4):
                            b = pj6.next()
                            proj_fm(widx, j * 128, tt, b)
                            ACT(qT[:, j, tt * 512:(tt + 1) * 512], pbank[b][:], AF.Copy, [t_pb[b]], [t_q], scale=0.125)
                add_step([[w_in[:, hg * 256:(hg + 1) * 256]]], fq)

                def fk(widx, hg=hg):
                    for j in range(2):
                        for tt in range(4):
                            b = pj6.next()
                            proj_fm(widx, j * 128, tt, b)
                            S.emit("dve", lambda e, b=b, j=j, tt=tt: e.tensor_copy(out=kT[:, j, tt * 512:(tt + 1) * 512], in_=pbank[b][:]),
                                   [t_pb[b]], [t_k])
                add_step([[w_in[:, 1024 + hg * 256:1024 + (hg + 1) * 256]]], fk)

                def fv(widx, hg=hg):
                    for tb in range(16):
                        b = pj6.next()
                        tt = tb // 4
                        for kc in range(8):
                            MM(pbank[b][:, 0:256], hT[:, kc, tb * 128:(tb + 1) * 128], wb[widx][:, kc, 0:256],
                               kc == 0, kc == 7, [t_wb[widx], t_hT[kc][tt]], [t_pb[b]])
                        if tb % 2 == 0:
                            ACT(vv[:, tb, :], pbank[b][:, 0:256], AF.Copy, [t_pb[b]], [t_v])
                        else:
                            S.emit("dve", lambda e, b=b, tb=tb: e.tensor_copy(out=vv[:, tb, :], in_=pbank[b][:, 0:256]), [t_pb[b]], [t_v])
                add_step([[w_in[:, 2048 + hg * 256:2048 + (hg + 1) * 256]]], fv)

                def fz(widx, hg=hg):
                    for j in range(2):
                        for tt in range(4):
                            b = pj6.next()
                            proj_fm(widx, j * 128, tt, b)
                            silu2_from_psum(b, sza[:, j, tt * 512:(tt + 1) * 512], t_sz, 4 + (tt % 2))
                add_step([[w_in[:, 3072 + hg * 256:3072 + (hg + 1) * 256]]], fz)

                def fa(hg=hg):
                    attn_group(hg)
                add_step([], fa)
            for qd in range(4):
                merge_step(seq, 0, qd, w_oa)

        def phase_B(seq):
            def bar():
                S.barrier(S.COMPUTE)
            add_step([], bar)
            for qd in range(4):
                def fc(widx, qd=qd):
                    ci = qd % 2
                    S.emit("dve", lambda e, ci=ci: e.memset(cT[ci][:, :, 0:30], 0.0), [], [t_cT[ci]])
                    for dci in range(2):
                        dc = qd * 2 + dci
                        for tt in range(4):
                            ba = pj6.next()
                            proj_fm(widx, dci * 128, tt, ba)
                            bg = pj6.next()
                            proj_fm(widx, 256 + dci * 128, tt, bg)
                            ACT(tf[4][:], pbank[bg][:], AF.Tanh, [t_pb[bg]], [t_tf[4]], scale=0.5)
                            STT("dve", cT[ci][:, dci, 30 + tt * 512:30 + (tt + 1) * 512], tf[4][:], 1.0, pbank[ba][:],
                                ALU.add, ALU.mult, [t_tf[4], t_pb[ba]], [t_cT[ci]])
                    for dci in range(2):
                        dc = qd * 2 + dci
                        for jt in range(31):
                            if jt % 2 == 0:
                                ACT(diag[:, jt, :], identf[:], AF.Copy, CT, [t_diag[0]], scale=convw[:, dc, jt:jt + 1])
                            else:
                                TS("dve", diag[:, jt, :], identf[:], convw[:, dc, jt:jt + 1], None, ALU.mult, None, CT, [t_diag[1]])
                        for tt in range(4):
                            b = pj6.next()
                            for jt in range(31):
                                MM(pbank[b][:], diag[:, jt, :], cT[ci][:, dci, tt * 512 + jt:tt * 512 + jt + 512],
                                   jt == 0, jt == 30, [t_diag[jt % 2], t_cT[ci]], [t_pb[b]])
                            ACT(oagT[:, dc, tt * 512:(tt + 1) * 512], pbank[b][:], AF.Identity, [t_pb[b]] + CT, [t_oag[dc][tt]],
                                bias=cols[:, C_CB + dc:C_CB + dc + 1])
                add_step([[w_in[:, 4096 + qd * 256:4096 + (qd + 1) * 256], w_in[:, 5120 + qd * 256:5120 + (qd + 1) * 256]]], fc)

            def f2(w0, w1):
                wz = (w0, w1)
                for tt in range(4):
                    sl = slice(tt * 512, (tt + 1) * 512)
                    bmu, bms = PO[0], PO[1]
                    for dc in range(8):
                        MM(pbank[bmu][:], ones1024[:], oagT[:, dc, sl], dc == 0, dc == 7, [t_oag[dc][tt]] + CT, [t_pb[bmu]])
                    for dc in range(8):
                        ei = et_r.next()
                        ACT(et[ei][:], oagT[:, dc, sl], AF.Square, [t_oag[dc][tt]], [t_et[ei]])
                        MM(pbank[bms][:], ones1024[:], et[ei][:], dc == 0, dc == 7, [t_et[ei]] + CT, [t_pb[bms]])
                    ACT(tf[0][:], pbank[bmu][:], AF.Copy, [t_pb[bmu]], [t_tf[0]])
                    TT("dve", tf[1][:], tf[0][:], tf[0][:], ALU.mult, [t_tf[0]], [t_tf[1]])
                    TT("dve", tf[1][:], pbank[bms][:], tf[1][:], ALU.subtract, [t_pb[bms], t_tf[1]], [t_tf[1]])
                    TS("dve", tf[1][:], tf[1][:], 1e-6, None, ALU.add, None, [t_tf[1]], [t_tf[1]])
                    RSQRT(tf[1][:], 512, [t_tf[1]] + CT, [t_tf[1]])
                    STT("dve", tf[0][:], tf[0][:], -1.0, tf[1][:], ALU.mult, ALU.mult, [t_tf[0], t_tf[1]], [t_tf[0]])
                    for dc in range(8):
                        a, bq_ = (2, 3) if dc % 2 == 0 else (4, 5)
                        TT("dve", tf[a][:], oagT[:, dc, sl], tf[1][:], ALU.mult, [t_oag[dc][tt], t_tf[1]], [t_tf[a]])
                        TT("dve", tf[a][:], tf[a][:], tf[0][:], ALU.add, [t_tf[a], t_tf[0]], [t_tf[a]])
                        ACT(tf[bq_][:], tf[a][:], AF.Tanh, [t_tf[a]] + CT, [t_tf[bq_]], scale=hlng[:, dc:dc + 1], bias=hlnb[:, dc:dc + 1])
                        ACT(tf[a][:], tf[a][:], AF.Identity, [t_tf[a]] + CT, [t_tf[a]], scale=cols[:, C_LNG + dc:C_LNG + dc + 1],
                            bias=cols[:, C_LNB + dc:C_LNB + dc + 1])
                        STT("dve", tf[a][:], tf[bq_][:], 1.0, tf[a][:], ALU.add, ALU.mult, [t_tf[bq_], t_tf[a]], [t_tf[a]])
                        bz = pjz.next()
                        proj_fm(wz[dc // 4], (dc % 4) * 128, tt, bz)
                        ACT(tf[bq_][:], pbank[bz][:], AF.Tanh, [t_pb[bz]], [t_tf[bq_]], scale=0.5)
                        STT("dve", tf[bq_][:], tf[bq_][:], 1.0, pbank[bz][:], ALU.add, ALU.mult, [t_tf[bq_], t_pb[bz]], [t_tf[bq_]])
                        STT("dve", oagT[:, dc, sl], tf[a][:], 0.25, tf[bq_][:], ALU.mult, ALU.mult, [t_tf[a], t_tf[bq_]], [t_oag[dc][tt]])
            add_step([[w_in[:, 6144:6656]], [w_in[:, 6656:7168]]], f2)
            for qd in range(4):
                merge_step(seq, 1, qd, w_ob)

        def phase_C(seq):
            def fmem():
                S.barrier(S.ENGS, dma_toks=t_xt + t_xo)
                rmsnorm_T(mem_d[seq * 256:(seq + 1) * 256, :], 2, C_GMEM, memT, lambda kc, tb: t_memT, 256)
                S.barrier(S.COMPUTE, dma_toks=t_xt + t_xo)
            add_step([], fmem)
            for p in range(2):
                def fkk(widx, p=p):
                    for ci in range(4):
                        b = pj.next()
                        for kc in range(8):
                            MM(pbank[b][:, 0:256], wb[widx][:, kc, ci * 128:(ci + 1) * 128], memT[:, kc, :], kc == 0, kc == 7,
                               [t_wb[widx], t_memT], [t_pb[b]])
                        ACT(kmT[:, p * 4 + ci, :], pbank[b][:, 0:256], AF.Copy, [t_pb[b]], [t_kmT], scale=1.0 / 16)
                add_step([[w_kv[:, p * 512:(p + 1) * 512]]], fkk)
            for p in range(2):
                def fvv(widx, p=p):
                    for kb in range(2):
                        b = pj.next()
                        for kc in range(8):
                            MM(pbank[b][:], memT[:, kc, kb * 128:(kb + 1) * 128], wb[widx][:, kc, :], kc == 0, kc == 7,
                               [t_wb[widx], t_memT], [t_pb[b]])
                        ACT(vm[:, kb, p * 512:(p + 1) * 512], pbank[b][:], AF.Copy, [t_pb[b]], [t_vm])
                add_step([[w_kv[:, 1024 + p * 512:1024 + (p + 1) * 512]]], fvv)
            for hx in range(4):
                def fx(widx, hx=hx):
                    for dci in range(2):
                        for tt in range(4):
                            b = pj.next()
                            proj_fm(widx, dci * 128, tt, b)
                            ACT(qcT[:, dci, tt * 512:(tt + 1) * 512], pbank[b][:], AF.Copy, [t_pb[b]], [t_qc])
                            b2 = pj.next()
                            proj_fm(widx, 256 + dci * 128, tt, b2)
                            silu2_from_psum(b2, szc[:, dci, tt * 512:(tt + 1) * 512], t_szc, 4 + (tt % 2))
                    def s1(G):
                        sl = slice(G * 512, (G + 1) * 512)
                        eis = []
                        for kb in range(2):
                            bs = ps_r.next()
                            for dci in range(2):
                                MM(pbank[bs][:], kmT[:, hx * 2 + dci, kb * 128:(kb + 1) * 128], qcT[:, dci, sl], dci == 0, dci == 1,
                                   [t_kmT, t_qc], [t_pb[bs]])
                            ei = et_r.next()
                            ACT(et[ei][:], pbank[bs][:], AF.Exp, [t_pb[bs]], [t_et[ei]])
                            eis.append(ei)
                        return eis

                    def s2(G, eis):
                        sl = slice(G * 512, (G + 1) * 512)
                        for ec in range(2):
                            for kb in range(2):
                                MM(pbank[PO[ec]][:], vm[:, kb, hx * 256 + ec * 128:hx * 256 + (ec + 1) * 128], et[eis[kb]][:],
                                   kb == 0, kb == 1, [t_vm, t_et[eis[kb]]], [t_pb[PO[ec]]])
                        for kb in range(2):
                            MM(pbank[PL[0]][:], ones_b[:], et[eis[kb]][:], kb == 0, kb == 1, [t_et[eis[kb]]] + CT, [t_pb[PL[0]]])
                        for ec in range(2):
                            ACT(tf[1 + ec][:], pbank[PO[ec]][:], AF.Copy, [t_pb[PO[ec]]], [t_tf[1 + ec]])
                        S.emit("dve", lambda e: e.tensor_copy(out=tf[0][:], in_=pbank[PL[0]][:]), [t_pb[PL[0]]], [t_tf[0]])
                        S.emit("dve", lambda e: e.reciprocal(out=tf[0][:], in_=tf[0][:]), [t_tf[0]], [t_tf[0]])
                        for ec in range(2):
                            TT("dve", tf[1 + ec][:], tf[1 + ec][:], tf[0][:], ALU.mult, [t_tf[1 + ec], t_tf[0]], [t_tf[1 + ec]])
                            STT("dve", oagT[:, hx * 2 + ec, sl], tf[1 + ec][:], 0.5, szc[:, ec, sl], ALU.mult, ALU.mult,
                                [t_tf[1 + ec], t_szc], [t_oag[hx * 2 + ec][G]])
                    e_next = s1(0)
                    for G in range(4):
                        e_cur = e_next
                        if G < 3:
                            e_next = s1(G + 1)
                        s2(G, e_cur)
                add_step([[w_in[:, 7168 + hx * 256:7168 + (hx + 1) * 256], w_in[:, 8192 + hx * 256:8192 + (hx + 1) * 256]]], fx)
            for qd in range(4):
                merge_step(seq, 2, qd, w_oc)

        def phase_D(seq):
            def fd(w0, w1):
                S.barrier(S.ENGS, dma_toks=[])
                wz = (w0, w1)
                for tb in range(16):
                    bi = tb % 2
                    tt = tb // 4
                    S.dma_load("sp", t_xt[bi], xt[bi], x_d[seq * SEQ + tb * 128:seq * SEQ + (tb + 1) * 128, :])
                    banks = (PO[bi], PL[bi])
                    for hf in range(2):
                        for kc in range(8):
                            MM(pbank[banks[hf]][:], m2T[:, kc, tb * 128:(tb + 1) * 128], wb[wz[hf]][:, kc, :], kc == 0, kc == 7,
                               [t_wb[wz[hf]], t_m2[kc][tt]], [t_pb[banks[hf]]])
                    S.emit("dve", lambda e: e.memset(ssum[:, 2:4], 0.0), [], [t_ssum])
                    for hf in range(2):
                        ACT(junk[:, 0:512], pbank[banks[hf]][:], AF.Square, [t_pb[banks[hf]], t_ssum], [t_junk, t_ssum],
                            accum_out=ssum[:, 2 + hf:3 + hf])
                    TT("dve", ssum[:, 4:5], ssum[:, 2:3], ssum[:, 3:4], ALU.add, [t_ssum], [t_ssum])
                    TS("dve", ssum[:, 4:5], ssum[:, 4:5], 0.25 / DM, 1e-6, ALU.mult, ALU.add, [t_ssum], [t_ssum])
                    RSQRT(ssum[:, 4:5], 1, [t_ssum] + CT, [t_ssum])
                    TS("dve", ssum[:, 5:6], ssum[:, 4:5], 0.5, None, ALU.mult, None, [t_ssum], [t_ssum])
                    for hf in range(2):
                        cs = slice(hf * 512, (hf + 1) * 512)
                        STT("dve", xo[bi][:, cs], pbank[banks[hf]][:], ssum[:, 5:6], gpost[:, cs], ALU.mult, ALU.mult,
                            [t_pb[banks[hf]], t_ssum] + CT, [t_xo[bi]])
                        TT("dve", xo[bi][:, cs], xo[bi][:, cs], xt[bi][:, cs], ALU.add, [t_xo[bi], t_xt[bi]], [t_xo[bi]])
                    S.dma_store("sp", t_xo[bi], out_d[seq * SEQ + tb * 128:seq * SEQ + (tb + 1) * 128, :], xo[bi])
            add_step([[w_out[:, 0:512]], [w_out[:, 512:1024]]], fd)

        for seq in range(NSEQ):
            def f0(seq=seq):
                S.barrier(S.ENGS, dma_toks=t_xt + t_xo)
                rmsnorm_T(x_d[seq * SEQ:(seq + 1) * SEQ, :], 16, C_GPRE, hT, lambda kc, tb: t_hT[kc][tb // 4], SEQ)
                S.barrier(S.COMPUTE, dma_toks=t_xt + t_xo)
            add_step([], f0)
            if "A" in phases:
                phase_A(seq)
            if "B" in phases:
                phase_B(seq)
            if "C" in phases:
                phase_C(seq)
            if debug and seq == 0:
                def fdbg():
                    t_dbg = S.tok("dbg")
                    for name in debug:
                        src = {"hT": hT, "m2T": m2T, "oagT": oagT}[name]
                        rd = hT_all if name == "hT" else [t for r in (t_m2 if name == "m2T" else t_oag) for t in r]
                        for kc in range(8):
                            for tt in range(4):
                                S.emit("dve", lambda e, kc=kc, tt=tt, src=src: e.tensor_copy(out=tf[0][:], in_=src[:, kc, tt * 512:(tt + 1) * 512]),
                                       rd, [t_tf[0]])
                                S.dma_store("sp", t_tf[0], dbg_d[name][kc * 128:(kc + 1) * 128, tt * 512:(tt + 1) * 512], tf[0][:])
                add_step([], fdbg)
            if "D" in phases:
                phase_D(seq)
        if phases != "0ABCD":
            def ftouch():
                t_touch = S.tok("touch")
                for ap in (mem_d, w_in, w_oa, w_ob, w_oc, w_out, w_kv):
                    S.dma_load("sp", t_touch, ssum[0:1, 0:8], ap[0:1, 0:8])
                S.emit("dve", lambda e: e.memset(tf[1][:], 0.0), [], [t_tf[1]])
                S.dma_store("sp", t_tf[1], out_d[0:128, 0:512], tf[1][:])
            add_step([], ftouch)
        run_steps()
        S.build()
        print("instr counts", {e: len(S.prog[e]) for e in S.ENGS}, "signals", S.sigcount, "waits", S.nwaits)
    return nc


def _t5_bucket_host(rel):
    nb, max_exact = 16, 8
    try:
        import jax
        import jax.numpy as jnp
        cpu = jax.devices("cpu")[0]
        with jax.default_device(cpu):
            r = jnp.asarray(rel, dtype=jnp.int32)
            ret = (r > 0).astype(jnp.int32) * nb
            n = jnp.abs(r)
            nf = jnp.maximum(n, 1).astype(jnp.float32)
            large = max_exact + (jnp.log(nf / max_exact) / math.log(128 / max_exact) * (nb - max_exact)).astype(jnp.int32)
            large = jnp.minimum(large, nb - 1)
            return np.asarray(ret + jnp.where(n < max_exact, n, large))
    except Exception:
        ret = (rel > 0).astype(np.int32) * nb
        n = np.abs(rel)
        nf = np.maximum(n, 1).astype(np.float32)
        large = max_exact + (np.log(nf / np.float32(max_exact)) / np.float32(math.log(128 / max_exact))
                             * np.float32(nb - max_exact)).astype(np.int32)
        large = np.minimum(large, nb - 1)
        return ret + np.where(n < max_exact, n, large)


def _colvec(v):
    v = np.asarray(v, np.float32).reshape(-1)
    return np.ascontiguousarray(v.reshape(-1, 128).T)


def prep_shared(rel_bias, g_pre, g_mem, w_in, b_merge, lam_q1, lam_k1, lam_q2, lam_k2, g_subln, w_oa, conv_w,
                conv_b, ln_g, ln_b, w_ob, w_mem_kv, w_oc, w_out, g_post):
    f = lambda a: np.ascontiguousarray(np.asarray(a, np.float32))
    rel_bias = f(rel_bias)
    cols = np.zeros((128, NCOLS), np.float32)
    cols[:, C_GPRE:C_GPRE + 8] = _colvec(g_pre[0])
    cols[:, C_GMEM:C_GMEM + 8] = _colvec(g_mem[0])
    cols[:, C_BM:C_BM + 24] = _colvec(b_merge[0])
    cols[:, C_CB:C_CB + 8] = _colvec(conv_b[0])
    cols[:, C_LNG:C_LNG + 8] = _colvec(ln_g[0])
    cols[:, C_LNB:C_LNB + 8] = _colvec(ln_b[0])
    cols[:, C_GSUB:C_GSUB + 1] = _colvec(g_subln[0])
    far_b = int(_t5_bucket_host(np.array([-1000], np.int32))[0])
    cols[:, C_CFAR:C_CFAR + 8] = np.broadcast_to(rel_bias[far_b][None, :], (128, 8))
    cw = f(conv_w)[0]
    convw = np.ascontiguousarray(cw.reshape(31, 8, 128).transpose(2, 1, 0)).reshape(128, 8 * 31)
    kk = np.arange(128, dtype=np.int32)[:, None]
    qq = np.arange(128, dtype=np.int32)[None, :]
    relD = kk - qq
    relU = kk - qq - 128
    bD = _t5_bucket_host(relD)
    bU = _t5_bucket_host(relU)
    allowed = (kk // 64) <= (qq // 64)
    ext = np.concatenate([rel_bias, np.full((1, 8), NEG, np.float32)], axis=0)
    idxD = np.where(allowed, bD, 32)
    D = ext[idxD]
    U = ext[bU]
    biasdu = np.ascontiguousarray(np.concatenate([D, U], axis=1).transpose(0, 2, 1)).reshape(128, 8 * 256)
    lamv = np.concatenate([f(lam_q1)[0], f(lam_k1)[0], f(lam_q2)[0], f(lam_k2)[0]]).astype(np.float32)
    return {
        "w_in": f(w_in)[0], "w_oa": f(w_oa)[0], "w_ob": f(w_ob)[0], "w_oc": f(w_oc)[0], "w_out": f(w_out)[0],
        "w_mem_kv": f(w_mem_kv)[0], "cols": cols, "convw": convw, "g_post": f(g_post)[0], "lamv": lamv,
        "biasdu": biasdu.astype(np.float32),
    }


_PROG = {}


def kernel(x, mem, rel_bias, g_pre, g_mem, w_in, b_merge, lam_q1, lam_k1, lam_q2, lam_k2, g_subln, w_oa, conv_w,
           conv_b, ln_g, ln_b, w_ob, w_mem_kv, w_oc, w_out, g_post):
    shared = prep_shared(rel_bias, g_pre, g_mem, w_in, b_merge, lam_q1, lam_k1, lam_q2, lam_k2, g_subln, w_oa,
                         conv_w, conv_b, ln_g, ln_b, w_ob, w_mem_kv, w_oc, w_out, g_post)
    x = np.asarray(x, np.float32)
    mem = np.asarray(mem, np.float32)
    B = x.shape[0]
    per = B // NCORES
    if "p" not in _PROG:
        _PROG["p"] = build_program(NSEQ=per)
    nc = _PROG["p"]
    in_maps = []
    for c in range(NCORES):
        m = dict(shared)
        m["x"] = np.ascontiguousarray(x[c * per:(c + 1) * per].reshape(per * SEQ, DM))
        m["mem"] = np.ascontiguousarray(mem[c * per:(c + 1) * per].reshape(per * 256, DM))
        in_maps.append(m)
    res = run_bass_kernel_spmd(nc, in_maps, core_ids=list(range(NCORES)))
    outs = [np.asarray(r["out"], np.float32).reshape(per, SEQ, DM) for r in res.results]
    return np.concatenate(outs, axis=0)
```

```python
import math
from contextlib import ExitStack
import numpy as np
import concourse.bass as bass
import concourse.mybir as mybir
from concourse.bass_utils import run_bass_kernel_spmd

F32 = mybir.dt.float32
BF16 = mybir.dt.bfloat16
AF = mybir.ActivationFunctionType
ALU = mybir.AluOpType
AX = mybir.AxisListType

SAME_ENGINE_SYNC = True
EPOCH = 20000
NEG = -30000.0
SEQ = 2048
DM = 1024
NCORES = 8


class Tk:
    __slots__ = ("name", "lw", "rd", "dsem", "dcnt")

    def __init__(self, name):
        self.name = name
        self.lw = None
        self.rd = {}
        self.dsem = None
        self.dcnt = 0


class Ins:
    __slots__ = ("eng", "fn", "cdeps", "raw", "dwaits", "sig", "val", "dma", "waits")

    def __init__(self, eng, fn):
        self.eng = eng
        self.fn = fn
        self.cdeps = set()
        self.raw = set()
        self.dwaits = {}
        self.sig = False
        self.val = None
        self.dma = None
        self.waits = []


class Sched:
    ENGS = ("pe", "act", "dve", "pool", "sp")
    COMPUTE = ("pe", "act", "dve", "pool")

    def __init__(self, nc, stack):
        self.nc = nc
        self.stack = stack
        self.prog = {e: [] for e in self.ENGS}
        self.esem = {e: [] for e in self.ENGS}
        self.toks = []
        self.last_compute = {e: None for e in self.ENGS}
        self.pending = {e: (set(), {}) for e in self.ENGS}

    def tok(self, name):
        t = Tk(name)
        self.toks.append(t)
        return t

    def _dep(self, ins, p, raw):
        if p is None or p is ins:
            return
        if p.dma is not None:
            t = p.dma
            ins.dwaits[t] = max(ins.dwaits.get(t, 0), 16 * t.dcnt)
        else:
            ins.cdeps.add(p)
            if raw:
                ins.raw.add(p)

    def emit(self, eng, fn, reads=(), writes=(), dma=None):
        ins = Ins(eng, fn)
        pc, pd = self.pending[eng]
        if pc or pd:
            for p in pc:
                if p.eng != eng:
                    ins.cdeps.add(p)
            for t, v in pd.items():
                ins.dwaits[t] = max(ins.dwaits.get(t, 0), v)
            self.pending[eng] = (set(), {})
        for t in reads:
            self._dep(ins, t.lw, True)
        for t in writes:
            self._dep(ins, t.lw, False)
            for r in t.rd.values():
                self._dep(ins, r, False)
        if dma is not None:
            if dma.dsem is None:
                dma.dsem = self.stack.enter_context(self.nc.semaphore("ds_" + dma.name))
            dma.dcnt += 1
            ins.dma = dma
        else:
            self.last_compute[eng] = ins
        for t in reads:
            key = eng if dma is None else ("dma", id(ins))
            t.rd[key] = ins
        for t in writes:
            t.lw = ins
            t.rd = {}
        self.prog[eng].append(ins)
        return ins

    def barrier(self, engs, dma_toks=()):
        lasts = set(self.last_compute[e] for e in self.COMPUTE if self.last_compute[e] is not None)
        for e in engs:
            pc, pd = self.pending[e]
            pc |= lasts
            for t in dma_toks:
                if t.dsem is not None:
                    pd[t] = max(pd.get(t, 0), 16 * t.dcnt)

    def dma_load(self, q, tok, out_ap, in_ap, **kw):
        return self.emit(q, lambda e: e.dma_start(out=out_ap, in_=in_ap, **kw), writes=[tok], dma=tok)

    def dma_store(self, q, tok, out_ap, in_ap, **kw):
        return self.emit(q, lambda e: e.dma_start(out=out_ap, in_=in_ap, **kw), reads=[tok], dma=tok)

    def _needs(self, ins, p):
        if p.eng != ins.eng:
            return True
        return SAME_ENGINE_SYNC and p.eng != "pe"

    def finalize(self):
        for e in self.ENGS:
            for ins in self.prog[e]:
                for p in ins.cdeps:
                    if self._needs(ins, p):
                        p.sig = True
        for e in self.ENGS:
            c = 0
            for ins in self.prog[e]:
                if ins.sig:
                    ep, v = divmod(c, EPOCH)
                    if ep >= len(self.esem[e]):
                        self.esem[e].append(self.stack.enter_context(self.nc.semaphore("es_%s%d" % (e, ep))))
                    ins.val = (ep, v + 1)
                    c += 1
        self.sigcount = {e: sum(1 for i in self.prog[e] if i.sig) for e in self.ENGS}
        self.nwaits = 0
        for e in self.ENGS:
            known = {}
            for ins in self.prog[e]:
                need = {}
                for p in ins.cdeps:
                    if not self._needs(ins, p):
                        continue
                    need[p.eng] = max(need.get(p.eng, (0, 0)), p.val)
                for pe_, v in need.items():
                    if known.get(pe_, (0, 0)) < v:
                        ins.waits.append((self.esem[pe_][v[0]], v[1]))
                        known[pe_] = v
                for t, v in ins.dwaits.items():
                    k = ("d", id(t))
                    if known.get(k, 0) < v:
                        ins.waits.append((t.dsem, v))
                        known[k] = v
                self.nwaits += len(ins.waits)

    def simulate(self):
        sem = {}
        pc = {e: 0 for e in self.ENGS}
        progress = True
        while progress:
            progress = False
            for e in self.ENGS:
                while pc[e] < len(self.prog[e]):
                    ins = self.prog[e][pc[e]]
                    if all(sem.get(id(s), 0) >= v for (s, v) in ins.waits):
                        if ins.dma is not None:
                            sem[id(ins.dma.dsem)] = sem.get(id(ins.dma.dsem), 0) + 16
                        elif ins.sig:
                            k = id(self.esem[e][ins.val[0]])
                            sem[k] = sem.get(k, 0) + 1
                        pc[e] += 1
                        progress = True
                    else:
                        break
        stuck = {e: (pc[e], len(self.prog[e])) for e in self.ENGS if pc[e] < len(self.prog[e])}
        if stuck:
            for e in stuck:
                ins = self.prog[e][pc[e]]
                print("STUCK", e, pc[e], [(s.name, v, sem.get(id(s), 0)) for (s, v) in ins.waits])
            raise RuntimeError("deadlock in generated program: %s" % stuck)

    def run_engine(self, ename, e):
        for ins in self.prog[ename]:
            for (s, v) in ins.waits:
                e.wait_ge(s, v)
            bi = ins.fn(e)
            if ins.dma is not None:
                bi.then_inc(ins.dma.dsem, 16)
            elif ins.sig:
                bi.then_inc(self.esem[ename][ins.val[0]], 1)

    def build(self):
        fin = {}
        for t in self.toks:
            if t.dsem is not None:
                fin[t] = 16 * t.dcnt
        self.finalize()
        self.simulate()
        nc = self.nc
        with nc.Block() as block:
            @block.tensor
            def _(e):
                self.run_engine("pe", e)

            @block.scalar
            def _(e):
                self.run_engine("act", e)

            @block.vector
            def _(e):
                self.run_engine("dve", e)

            @block.gpsimd
            def _(e):
                self.run_engine("pool", e)

            @block.sync
            def _(e):
                self.run_engine("sp", e)
                for t, v in fin.items():
                    e.wait_ge(t.dsem, v)


STEP_LOG = []
EPI_D1 = 13
EPI_D2 = 17
NCOLS = 80
C_GPRE, C_GMEM, C_BM, C_CB, C_LNG, C_LNB, C_GSUB, C_CFAR = 0, 8, 16, 40, 48, 56, 64, 65


def build_program(NSEQ=2, debug=None, phases="0ABCD"):
    nc = bass.Bass("TRN2", target_bir_lowering=False)
    dram = lambda n, shp, kind="ExternalInput": nc.dram_tensor(n, shp, F32, kind=kind).ap()
    x_d = dram("x", [NSEQ * SEQ, DM])
    mem_d = dram("mem", [NSEQ * 256, DM])
    w_in = dram("w_in", [DM, 12288])
    w_oa = dram("w_oa", [DM, DM])
    w_ob = dram("w_ob", [DM, DM])
    w_oc = dram("w_oc", [DM, DM])
    w_out = dram("w_out", [DM, DM])
    w_kv = dram("w_mem_kv", [DM, 2048])
    cols_d = dram("cols", [128, NCOLS])
    convw_d = dram("convw", [128, 8 * 31])
    gpost_d = dram("g_post", [DM])
    lamv_d = dram("lamv", [256])
    bias_d = dram("biasdu", [128, 8 * 256])
    out_d = dram("out", [NSEQ * SEQ, DM], kind="ExternalOutput")
    dbg_d = {}
    if debug:
        for name, shp in debug.items():
            dbg_d[name] = dram("dbg_" + name, shp, kind="ExternalOutput")

    with ExitStack() as st:
        S = Sched(nc, st)
        sb = lambda name, shape, dt: st.enter_context(nc.sbuf_tensor("s_" + name, shape, dt))
        psb = lambda name, shape, dt: st.enter_context(nc.psum_tensor("p_" + name, shape, dt))

        cols = sb("cols", [128, NCOLS], F32)
        convw = sb("convw", [128, 8, 31], F32)
        gpost = sb("gpost", [128, DM], F32)
        lamv = sb("lamv", [128, 256], F32)
        bhi = sb("bhi", [128, 8, 256], BF16)
        blo = sb("blo", [128, 8, 256], BF16)
        identf = sb("identf", [128, 128], F32)
        identb = sb("identb", [128, 128], BF16)
        ones_b = sb("ones_b", [128, 128], BF16)
        ones128 = sb("ones128", [128, 128], BF16)
        ones1024 = sb("ones1024", [128, 128], BF16)
        mhalf = sb("mhalf", [128, 512], F32)
        small = sb("small", [128, 16], F32)
        hbm = sb("hbm", [128, 24], F32)
        hlng = sb("hlng", [128, 8], F32)
        hlnb = sb("hlnb", [128, 8], F32)
        t_const = S.tok("const")
        S.dma_load("sp", t_const, cols[:], cols_d)
        S.dma_load("sp", t_const, convw[:], convw_d.rearrange("p (c j) -> p c j", j=31))
        S.dma_load("sp", t_const, gpost[:], gpost_d.partition_broadcast(128))
        S.dma_load("sp", t_const, lamv[:], lamv_d.partition_broadcast(128))
        t_c2 = S.tok("const2")
        S.emit("pool", lambda e: e.memset(identf[:], 0.0), writes=[t_c2])
        S.emit("pool", lambda e: e.affine_select(out=identf[:], in_=identf[:], pattern=[[-1, 128]],
                                                 compare_op=ALU.not_equal, fill=1.0, base=0,
                                                 channel_multiplier=1), reads=[t_c2], writes=[t_c2])
        S.emit("pool", lambda e: e.memset(mhalf[:], -0.5), writes=[t_c2])
        S.emit("dve", lambda e: e.tensor_copy(out=identb[:], in_=identf[:]), reads=[t_c2], writes=[t_c2])
        S.emit("dve", lambda e: e.memset(ones_b[:], 1.0), writes=[t_c2])
        S.emit("dve", lambda e: e.memset(ones128[:], 1.0 / 128), writes=[t_c2])
        S.emit("dve", lambda e: e.memset(ones1024[:], 1.0 / 1024), writes=[t_c2])
        lam_init = 0.8 - 0.6 * math.exp(-0.3 * 0)
        S.emit("dve", lambda e: e.tensor_tensor(out=lamv[:, 0:64], in0=lamv[:, 0:64], in1=lamv[:, 64:128], op=ALU.mult),
               reads=[t_const], writes=[t_c2])
        S.emit("dve", lambda e: e.tensor_tensor(out=lamv[:, 128:192], in0=lamv[:, 128:192], in1=lamv[:, 192:256], op=ALU.mult),
               reads=[t_const], writes=[t_c2])
        S.emit("dve", lambda e: e.reduce_sum(out=small[:, 2:3], in_=lamv[:, 0:64], axis=AX.X), reads=[t_c2], writes=[t_c2])
        S.emit("dve", lambda e: e.reduce_sum(out=small[:, 3:4], in_=lamv[:, 128:192], axis=AX.X), reads=[t_c2], writes=[t_c2])
        S.emit("act", lambda e: e.activation(out=small[:, 4:6], in_=small[:, 2:4], func=AF.Exp), reads=[t_c2], writes=[t_c2])
        S.emit("dve", lambda e: e.tensor_tensor(out=small[:, 0:1], in0=small[:, 5:6], in1=small[:, 4:5], op=ALU.subtract),
               reads=[t_c2], writes=[t_c2])
        S.emit("dve", lambda e: e.tensor_scalar(out=small[:, 0:1], in0=small[:, 0:1], scalar1=-lam_init, scalar2=None, op0=ALU.add),
               reads=[t_c2], writes=[t_c2])
        S.emit("dve", lambda e: e.tensor_scalar(out=small[:, 1:2], in0=cols[:, C_GSUB:C_GSUB + 1], scalar1=0.5 * (1.0 - lam_init),
                                                scalar2=None, op0=ALU.mult), reads=[t_const], writes=[t_c2])
        S.emit("dve", lambda e: e.tensor_scalar(out=hbm[:], in0=cols[:, C_BM:C_BM + 24], scalar1=0.5, scalar2=None, op0=ALU.mult),
               reads=[t_const], writes=[t_c2])
        S.emit("dve", lambda e: e.tensor_scalar(out=hlng[:], in0=cols[:, C_LNG:C_LNG + 8], scalar1=0.5, scalar2=None, op0=ALU.mult),
               reads=[t_const], writes=[t_c2])
        S.emit("dve", lambda e: e.tensor_scalar(out=hlnb[:], in0=cols[:, C_LNB:C_LNB + 8], scalar1=0.5, scalar2=None, op0=ALU.mult),
               reads=[t_const], writes=[t_c2])
        S.emit("dve", lambda e: e.tensor_scalar(out=convw[:], in0=convw[:], scalar1=0.5, scalar2=None, op0=ALU.mult),
               reads=[t_const], writes=[t_c2])
        CT = [t_const, t_c2]
        nlam = small[:, 0:1]
        gsub = small[:, 1:2]

        hT = sb("hT", [128, 8, SEQ], BF16)
        m2T = sb("m2T", [128, 8, SEQ], BF16)
        big = sb("big", [128, 8192], F32)
        oagT = big[:].bitcast(BF16).rearrange("p (c t) -> p c t", c=8)
        xt = [big[:, i * 1024:(i + 1) * 1024] for i in range(2)]
        xo = [big[:, 2048 + i * 1024:2048 + (i + 1) * 1024] for i in range(2)]
        scr = sb("scr", [128, 16384], BF16)
        junk = sb("junk", [128, 1024], BF16)
        et = [sb("et%d" % i, [128, 512], BF16) for i in range(4)]
        tf = [sb("tf%d" % i, [128, 512], F32) for i in range(8)]
        sqb = [sb("sqb%d" % i, [128, 512], BF16) for i in range(2)]
        NPE = 23
        diagb = [sb("diag%d" % i, [128, NPE, 128], BF16) for i in range(2)]
        dcnt = [0]
        ssum = sb("ssum", [128, 8], F32)
        NB = 3
        wb = [sb("wb%d" % i, [128, 8, 512], BF16) for i in range(NB)]
        t_wb = [S.tok("wb%d" % i) for i in range(NB)]
        pbank = [psb("pb%d" % i, [128, 512], F32) for i in range(8)]
        t_pb = [S.tok("pb%d" % i) for i in range(8)]
        PJ, PS_, PO, PL = (0, 1), (2, 3), (4, 5), (6, 7)

        t_hT = [[S.tok("hT%d_%d" % (kc, tt)) for tt in range(4)] for kc in range(8)]
        t_m2 = [[S.tok("m2%d_%d" % (kc, tt)) for tt in range(4)] for kc in range(8)]
        t_oag = [[S.tok("oag%d_%d" % (kc, tt)) for tt in range(4)] for kc in range(8)]
        t_xt = [S.tok("xt%d" % i) for i in range(2)]
        t_xo = [S.tok("xo%d" % i) for i in range(2)]
        t_junk = S.tok("junk")
        t_et = [S.tok("et%d" % i) for i in range(4)]
        t_tf = [S.tok("tf%d" % i) for i in range(8)]
        t_sqb = [S.tok("sqb0"), S.tok("sqb1")]
        t_diagb = [S.tok("diag0"), S.tok("diag1")]
        t_ssum = S.tok("ssum")
        hT_all = [t for r in t_hT for t in r]
        print("sbuf bytes remaining:", nc.sbuf_bytes_remaining)

        def MM(out, lhsT, rhs, start, stop, reads, writes):
            S.emit("pe", lambda e: e.matmul(out, lhsT=lhsT, rhs=rhs, start=start, stop=stop), reads, writes)

        def ACT(out, in_, func, reads, writes, eng="act", **kw):
            S.emit(eng, lambda e: e.activation(out=out, in_=in_, func=func, **kw), reads, writes)

        def TS(eng, out, in0, s1, s2, op0, op1, reads, writes):
            if s2 is None:
                S.emit(eng, lambda e: e.tensor_scalar(out=out, in0=in0, scalar1=s1, scalar2=None, op0=op0), reads, writes)
            else:
                S.emit(eng, lambda e: e.tensor_scalar(out=out, in0=in0, scalar1=s1, scalar2=s2, op0=op0, op1=op1), reads, writes)

        def STT(eng, out, in0, scalar, in1, op0, op1, reads, writes):
            S.emit(eng, lambda e: e.scalar_tensor_tensor(out=out, in0=in0, scalar=scalar, in1=in1, op0=op0, op1=op1), reads, writes)

        def TT(eng, out, in0, in1, op, reads, writes):
            S.emit(eng, lambda e: e.tensor_tensor(out=out, in0=in0, in1=in1, op=op), reads, writes)

        def RSQRT(ap, n, reads, writes):
            S.emit("act", lambda e: e.activation(out=ap, in_=ap, func=AF.Sqrt), reads, writes)
            S.emit("dve", lambda e: e.reciprocal(out=ap, in_=ap), writes, writes)

        class Rot:
            def __init__(self, items):
                self.items = items
                self.i = 0

            def next(self):
                v = self.items[self.i]
                self.i = (self.i + 1) % len(self.items)
                return v

        pj = Rot(list(PJ))
        ps_r = Rot(list(PS_))
        et_r = Rot([0, 1, 2, 3])
        ps3 = Rot([PS_[0], PS_[1], PJ[1]])
        pj6 = Rot([0, 1, 4, 5, 6, 7])
        pjz = Rot([0, 1, 6, 7])

        wstate = {"next": 0}

        def wload(slices):
            i = wstate["next"]
            wstate["next"] = (i + 1) % NB
            off = 0
            for ap in slices:
                n = ap.shape[1]
                S.dma_load("pool", t_wb[i], wb[i][:, :, off:off + n], ap.rearrange("(kc p) n -> p kc n", p=128))
                off += n
            return i

        steps = []

        def add_step(panels, fn):
            steps.append((panels, fn))

        def run_steps():
            loaded = {}
            for i, (panels, fn) in enumerate(steps):
                cur = loaded.setdefault(i, [])
                while len(cur) < len(panels):
                    cur.append(wload(panels[len(cur)]))
                if i + 1 < len(steps):
                    nxt_panels = steps[i + 1][0]
                    nxt = loaded.setdefault(i + 1, [])
                    free = NB - len(cur)
                    while len(nxt) < len(nxt_panels) and free > 0:
                        nxt.append(wload(nxt_panels[len(nxt)]))
                        free -= 1
                STEP_LOG.append((getattr(fn, "__name__", "?"), len(S.prog["pe"])))
                fn(*cur)

        def rmsnorm_T(src_rows_ap, ntb, gcol0, dst, dst_toks_fn, width):
            for tb in range(ntb):
                bi = tb % 2
                S.dma_load("sp", t_xt[bi], xt[bi], src_rows_ap[tb * 128:(tb + 1) * 128, :])
                S.emit("dve", lambda e: e.memset(ssum[:, 0:1], 0.0), [], [t_ssum])
                ACT(junk[:], xt[bi], AF.Square, [t_xt[bi], t_ssum], [t_junk, t_ssum], accum_out=ssum[:, 0:1])
                TS("dve", ssum[:, 1:2], ssum[:, 0:1], 1.0 / DM, 1e-6, ALU.mult, ALU.add, [t_ssum], [t_ssum])
                RSQRT(ssum[:, 1:2], 1, [t_ssum] + CT, [t_ssum])
                TS("dve", xo[bi], xt[bi], ssum[:, 1:2], None, ALU.mult, None, [t_xt[bi], t_ssum], [t_xo[bi]])
                for half in range(2):
                    b = ps_r.next()
                    for i in range(4):
                        kc = half * 4 + i
                        S.emit("pe", lambda e, b=b, i=i, kc=kc, bi=bi: e.transpose(
                            out=pbank[b][:, i * 128:(i + 1) * 128], in_=xo[bi][:, kc * 128:(kc + 1) * 128],
                            identity=identf[:]), [t_xo[bi]] + CT, [t_pb[b]])
                    for i in range(4):
                        kc = half * 4 + i
                        dt = dst_toks_fn(kc, tb)
                        ACT(dst[:, kc, tb * 128:(tb + 1) * 128], pbank[b][:, i * 128:(i + 1) * 128], AF.Copy,
                            [t_pb[b]] + CT, [dt], scale=cols[:, gcol0 + kc:gcol0 + kc + 1])

        qT = scr[:, 0:4096].rearrange("p (j t) -> p j t", j=2)
        kT = scr[:, 4096:8192].rearrange("p (j t) -> p j t", j=2)
        vv = scr[:, 8192:12288].rearrange("p (b n) -> p b n", b=16)
        sza = scr[:, 12288:16384].rearrange("p (j t) -> p j t", j=2)
        t_q, t_k, t_v, t_sz = S.tok("qT"), S.tok("kT"), S.tok("vv"), S.tok("sza")
        bsf = scr[:, 0:4096].bitcast(F32).rearrange("p (h k) -> p h k", k=256)
        S.dma_load("sp", t_q, bsf, bias_d.rearrange("p (h k) -> p h k", k=256))
        t_bias = S.tok("biashl")
        for h in range(8):
            TS("dve", bsf[:, h, :], bsf[:, h, :], cols[:, C_CFAR + h:C_CFAR + h + 1], None, ALU.subtract, None, [t_q] + CT, [t_q])
            S.emit("dve", lambda e, h=h: e.tensor_copy(out=bhi[:, h, :], in_=bsf[:, h, :]), [t_q], [t_bias])
            TT("dve", blo[:, h, :], bsf[:, h, :], bhi[:, h, :], ALU.subtract, [t_q, t_bias], [t_bias])
        CW = SEQ + 30
        cT = [scr[:, i * 2 * CW:(i + 1) * 2 * CW].rearrange("p (j t) -> p j t", j=2) for i in range(2)]
        t_cT = [S.tok("cT0"), S.tok("cT1")]
        memT = scr[:, 0:2048].rearrange("p (c t) -> p c t", c=8)
        kmT = scr[:, 2048:4096].rearrange("p (c t) -> p c t", c=8)
        vm = scr[:, 4096:6144].rearrange("p (b n) -> p b n", b=2)
        qcT = scr[:, 6144:10240].rearrange("p (j t) -> p j t", j=2)
        szc = scr[:, 10240:14336].rearrange("p (j t) -> p j t", j=2)
        t_memT, t_kmT, t_vm, t_qc, t_szc = S.tok("memT"), S.tok("kmT"), S.tok("vm"), S.tok("qcT"), S.tok("szc")

        def proj_fm(widx, col0, tt, bank):
            for kc in range(8):
                MM(pbank[bank][:], wb[widx][:, kc, col0:col0 + 128], hT[:, kc, tt * 512:(tt + 1) * 512],
                   kc == 0, kc == 7, [t_wb[widx], t_hT[kc][tt]], [t_pb[bank]])

        def silu2_from_psum(bank, dst, dst_tok, tfi):
            ACT(tf[tfi][:], pbank[bank][:], AF.Tanh, [t_pb[bank]], [t_tf[tfi]], scale=0.5)
            STT("dve", dst, tf[tfi][:], 1.0, pbank[bank][:], ALU.add, ALU.mult, [t_tf[tfi], t_pb[bank]], [dst_tok])

        first_b = min(i for i, p in enumerate("ABC") if p in phases) if any(p in phases for p in "ABC") else 0
        def merge_step(seq, b, qd, w_o):
            def fn(widx):
                for dci in range(2):
                    dc = qd * 2 + dci
                    for tt in range(4):
                        by = pj6.next()
                        for kc in range(8):
                            MM(pbank[by][:], wb[widx][:, kc, dci * 128:(dci + 1) * 128], oagT[:, kc, tt * 512:(tt + 1) * 512],
                               kc == 0, kc == 7, [t_wb[widx], t_oag[kc][tt]], [t_pb[by]])
                        bg = pj6.next()
                        for kc in range(8):
                            MM(pbank[bg][:], wb[widx][:, kc, 256 + dci * 128:256 + (dci + 1) * 128], hT[:, kc, tt * 512:(tt + 1) * 512],
                               kc == 0, kc == 7, [t_wb[widx], t_hT[kc][tt]], [t_pb[bg]])
                        ACT(tf[4][:], pbank[bg][:], AF.Tanh, [t_pb[bg]] + CT, [t_tf[4]], scale=0.5,
                            bias=hbm[:, b * 8 + dc:b * 8 + dc + 1])
                        dst = m2T[:, dc, tt * 512:(tt + 1) * 512]
                        if b == first_b:
                            STT("dve", dst, tf[4][:], 1.0, pbank[by][:], ALU.add, ALU.mult, [t_tf[4], t_pb[by]], [t_m2[dc][tt]])
                        else:
                            STT("dve", tf[5][:], tf[4][:], 1.0, pbank[by][:], ALU.add, ALU.mult, [t_tf[4], t_pb[by]], [t_tf[5]])
                            TT("dve", dst, tf[5][:], dst, ALU.add, [t_tf[5], t_m2[dc][tt]], [t_m2[dc][tt]])
            gcol = 9216 + b * 1024 + qd * 256
            add_step([[w_o[:, qd * 256:(qd + 1) * 256], w_in[:, gcol:gcol + 256]]], fn)

        def attn_group(hg):
            tasks = []
            for j in range(2):
                for G in range(4):
                    nkb = 4 * G + 4
                    for c in range(2):
                        for kb in range(nkb):
                            tasks.append((j, G, c, kb, nkb))
            state = {}
            deferred = []
            epi_n = [0]

            def stage1(t):
                j, G, c, kb, nkb = t
                h = 2 * hg + j
                r0, r1 = c * 64, (c + 1) * 64
                jj = kb - 4 * G
                c0 = max(jj, 0) * 128
                W = 512 - c0
                if jj >= 0:
                    boff, nw = 0, min(256, W)
                elif jj == -1:
                    boff, nw = 128, 128
                else:
                    boff, nw = 0, 0
                bs = ps3.next()
                MM(pbank[bs][:, 0:W], kT[r0:r1, j, kb * 128:(kb + 1) * 128],
                   qT[r0:r1, j, G * 512 + c0:(G + 1) * 512], True, nw == 0, [t_k, t_q], [t_pb[bs]])
                if nw > 0:
                    MM(pbank[bs][:, 0:nw], identb[:], bhi[:, h, boff:boff + nw], False, False, [t_bias] + CT, [t_pb[bs]])
                    MM(pbank[bs][:, 0:nw], identb[:], blo[:, h, boff:boff + nw], False, True, [t_bias] + CT, [t_pb[bs]])
                ei = et_r.next()
                ACT(et[ei][:, 0:W], pbank[bs][:, 0:W], AF.Exp, [t_pb[bs]], [t_et[ei]])
                state[t] = (ei, c0, W)

            def stage2(t):
                j, G, c, kb, nkb = t
                h = 2 * hg + j
                ei, c0, W = state.pop(t)
                po, pl = PO[c], PL[c]
                MM(pbank[po][:, c0:512], vv[:, kb, j * 128:(j + 1) * 128], et[ei][:, 0:W],
                   kb == 0, kb == nkb - 1, [t_v, t_et[ei]], [t_pb[po]])
                MM(pbank[pl][:, c0:512], ones_b[:], et[ei][:, 0:W],
                   kb == 0, kb == nkb - 1, [t_et[ei]] + CT, [t_pb[pl]])
                if c == 1 and kb == nkb - 1:
                    epilogue(j, G, h)

            def epilogue(j, G, h):
                k = epi_n[0] % 2
                epi_n[0] += 1
                to, tr = (0, 1) if k == 0 else (6, 7)
                ACT(tf[to][:], pbank[PO[0]][:], AF.Copy, [t_pb[PO[0]]], [t_tf[to]])
                ACT(tf[tr][:], pbank[PO[1]][:], AF.Copy, [t_pb[PO[1]]], [t_tf[tr]])
                S.emit("dve", lambda e: e.tensor_copy(out=tf[4][:], in_=pbank[PL[0]][:]), [t_pb[PL[0]]], [t_tf[4]])
                S.emit("dve", lambda e: e.tensor_copy(out=tf[5][:], in_=pbank[PL[1]][:]), [t_pb[PL[1]]], [t_tf[5]])
                S.emit("dve", lambda e: e.reciprocal(out=tf[4][:], in_=tf[4][:]), [t_tf[4]], [t_tf[4]])
                S.emit("dve", lambda e: e.reciprocal(out=tf[5][:], in_=tf[5][:]), [t_tf[5]], [t_tf[5]])
                TT("dve", tf[to][:], tf[to][:], tf[4][:], ALU.mult, [t_tf[to], t_tf[4]], [t_tf[to]])
                TT("dve", tf[tr][:], tf[tr][:], tf[5][:], ALU.mult, [t_tf[tr], t_tf[5]], [t_tf[tr]])
                STT("dve", tf[to][:], tf[tr][:], nlam, tf[to][:], ALU.mult, ALU.add, [t_tf[to], t_tf[tr]] + CT, [t_tf[to]])
                TT("dve", sqb[k][:], tf[to][:], tf[to][:], ALU.mult, [t_tf[to]], [t_sqb[k]])

                def part3a():
                    bq = PJ[0]
                    MM(pbank[bq][:], ones128[:], sqb[k][:], True, True, [t_sqb[k]] + CT, [t_pb[bq]])
                    TS("dve", tf[tr][:], pbank[bq][:], 1e-6, None, ALU.add, None, [t_pb[bq]], [t_tf[tr]])

                def part3b():
                    RSQRT(tf[tr][:], 512, [t_tf[tr]] + CT, [t_tf[tr]])
                    STT("dve", tf[to][:], tf[to][:], gsub, tf[tr][:], ALU.mult, ALU.mult, [t_tf[to], t_tf[tr]] + CT, [t_tf[to]])
                    TT("dve", oagT[:, h, G * 512:(G + 1) * 512], tf[to][:], sza[:, j, G * 512:(G + 1) * 512], ALU.mult,
                       [t_tf[to], t_sz], [t_oag[h][G]])
                deferred.append([EPI_D1, part3a])
                deferred.append([EPI_D2, part3b])

            def tick():
                for d in list(deferred):
                    d[0] -= 1
                    if d[0] <= 0:
                        deferred.remove(d)
                        d[1]()

            n = len(tasks)
            LOOK = 2
            for i in range(min(LOOK, n)):
                stage1(tasks[i])
            for i in range(n):
                if i + LOOK < n:
                    stage1(tasks[i + LOOK])
                stage2(tasks[i])
                tick()
            while deferred:
                tick()

        def phase_A(seq):
            for hg in range(4):
                def fq(widx, hg=hg):
                    for j in range(2):
                        for tt in range(4):
                            b = pj6.next()
                            proj_fm(widx, j * 128, tt, b)
                            ACT(qT[:, j, tt * 512:(tt + 1) * 512], pbank[b][:], AF.Copy, [t_pb[b]], [t_q], scale=0.125)
                add_step([[w_in[:, hg * 256:(hg + 1) * 256]]], fq)

                def fk(widx, hg=hg):
                    for j in range(2):
                        for tt in range(4):
                            b = pj6.next()
                            proj_fm(widx, j * 128, tt, b)
                            S.emit("dve", lambda e, b=b, j=j, tt=tt: e.tensor_copy(out=kT[:, j, tt * 512:(tt + 1) * 512], in_=pbank[b][:]),
                                   [t_pb[b]], [t_k])
                add_step([[w_in[:, 1024 + hg * 256:1024 + (hg + 1) * 256]]], fk)

                def fv(widx, hg=hg):
                    for tb in range(16):
                        b = pj6.next()
                        tt = tb // 4
                        for kc in range(8):
                            MM(pbank[b][:, 0:256], hT[:, kc, tb * 128:(tb + 1) * 128], wb[widx][:, kc, 0:256],
                               kc == 0, kc == 7, [t_wb[widx], t_hT[kc][tt]], [t_pb[b]])
                        if tb % 2 == 0:
                            ACT(vv[:, tb, :], pbank[b][:, 0:256], AF.Copy, [t_pb[b]], [t_v])
                        else:
                            S.emit("dve", lambda e, b=b, tb=tb: e.tensor_copy(out=vv[:, tb, :], in_=pbank[b][:, 0:256]), [t_pb[b]], [t_v])
                add_step([[w_in[:, 2048 + hg * 256:2048 + (hg + 1) * 256]]], fv)

                def fz(widx, hg=hg):
                    for j in range(2):
                        for tt in range(4):
                            b = pj6.next()
                            proj_fm(widx, j * 128, tt, b)
                            silu2_from_psum(b, sza[:, j, tt * 512:(tt + 1) * 512], t_sz, 4 + (tt % 2))
                add_step([[w_in[:, 3072 + hg * 256:3072 + (hg + 1) * 256]]], fz)

                def fa(hg=hg):
                    attn_group(hg)
                add_step([], fa)
            for qd in range(4):
                merge_step(seq, 0, qd, w_oa)

        def phase_B(seq):
            def bar():
                S.barrier(S.COMPUTE)
            add_step([], bar)
            for qd in range(4):
                def fc(widx, qd=qd):
                    ci = qd % 2
                    S.emit("dve", lambda e, ci=ci: e.memset(cT[ci][:, :, 0:30], 0.0), [], [t_cT[ci]])
                    for dci in range(2):
                        dc = qd * 2 + dci
                        for tt in range(4):
                            ba = pj6.next()
                            proj_fm(widx, dci * 128, tt, ba)
                            bg = pj6.next()
                            proj_fm(widx, 256 + dci * 128, tt, bg)
                            ACT(tf[4][:], pbank[bg][:], AF.Tanh, [t_pb[bg]], [t_tf[4]], scale=0.5)
                            STT("dve", cT[ci][:, dci, 30 + tt * 512:30 + (tt + 1) * 512], tf[4][:], 1.0, pbank[ba][:],
                                ALU.add, ALU.mult, [t_tf[4], t_pb[ba]], [t_cT[ci]])
                    for dci in range(2):
                        dc = qd * 2 + dci
                        k = dcnt[0] % 2
                        dcnt[0] += 1
                        dg, tdg = diagb[k], t_diagb[k]
                        for jt in range(NPE):
                            ACT(dg[:, jt, :], identf[:], AF.Copy, CT, [tdg], scale=convw[:, dc, jt:jt + 1])
                        for tt in range(4):
                            b = pj6.next()
                            for jt in range(NPE):
                                MM(pbank[b][:], dg[:, jt, :], cT[ci][:, dci, tt * 512 + jt:tt * 512 + jt + 512],
                                   jt == 0, jt == NPE - 1, [tdg, t_cT[ci]], [t_pb[b]])
                            ta = tt % 2
                            TS("dve", tf[ta][:], cT[ci][:, dci, tt * 512 + NPE:tt * 512 + NPE + 512], convw[:, dc, NPE:NPE + 1],
                               cols[:, C_CB + dc:C_CB + dc + 1], ALU.mult, ALU.add, [t_cT[ci]] + CT, [t_tf[ta]])
                            for jt in range(NPE + 1, 31):
                                STT("dve", tf[ta][:], cT[ci][:, dci, tt * 512 + jt:tt * 512 + jt + 512], convw[:, dc, jt:jt + 1],
                                    tf[ta][:], ALU.mult, ALU.add, [t_cT[ci], t_tf[ta]] + CT, [t_tf[ta]])
                            TT("dve", oagT[:, dc, tt * 512:(tt + 1) * 512], pbank[b][:], tf[ta][:], ALU.add,
                               [t_pb[b], t_tf[ta]], [t_oag[dc][tt]])
                add_step([[w_in[:, 4096 + qd * 256:4096 + (qd + 1) * 256], w_in[:, 5120 + qd * 256:5120 + (qd + 1) * 256]]], fc)

            def f2(w0, w1):
                wz = (w0, w1)
                for tt in range(4):
                    sl = slice(tt * 512, (tt + 1) * 512)
                    bmu, bms = PO[0], PO[1]
                    for dc in range(8):
                        MM(pbank[bmu][:], ones1024[:], oagT[:, dc, sl], dc == 0, dc == 7, [t_oag[dc][tt]] + CT, [t_pb[bmu]])
                    for dc in range(8):
                        ei = et_r.next()
                        ACT(et[ei][:], oagT[:, dc, sl], AF.Square, [t_oag[dc][tt]], [t_et[ei]])
                        MM(pbank[bms][:], ones1024[:], et[ei][:], dc == 0, dc == 7, [t_et[ei]] + CT, [t_pb[bms]])
                    ACT(tf[0][:], pbank[bmu][:], AF.Copy, [t_pb[bmu]], [t_tf[0]])
                    TT("dve", tf[1][:], tf[0][:], tf[0][:], ALU.mult, [t_tf[0]], [t_tf[1]])
                    TT("dve", tf[1][:], pbank[bms][:], tf[1][:], ALU.subtract, [t_pb[bms], t_tf[1]], [t_tf[1]])
                    TS("dve", tf[1][:], tf[1][:], 1e-6, None, ALU.add, None, [t_tf[1]], [t_tf[1]])
                    RSQRT(tf[1][:], 512, [t_tf[1]] + CT, [t_tf[1]])
                    STT("dve", tf[0][:], tf[0][:], -1.0, tf[1][:], ALU.mult, ALU.mult, [t_tf[0], t_tf[1]], [t_tf[0]])
                    def stA(dc):
                        a, bq_ = (2, 3) if dc % 2 == 0 else (4, 5)
                        TT("dve", tf[a][:], oagT[:, dc, sl], tf[1][:], ALU.mult, [t_oag[dc][tt], t_tf[1]], [t_tf[a]])
                        TT("dve", tf[a][:], tf[a][:], tf[0][:], ALU.add, [t_tf[a], t_tf[0]], [t_tf[a]])
                        ACT(tf[bq_][:], tf[a][:], AF.Tanh, [t_tf[a]] + CT, [t_tf[bq_]], scale=hlng[:, dc:dc + 1], bias=hlnb[:, dc:dc + 1])
                        ACT(tf[a][:], tf[a][:], AF.Identity, [t_tf[a]] + CT, [t_tf[a]], scale=cols[:, C_LNG + dc:C_LNG + dc + 1],
                            bias=cols[:, C_LNB + dc:C_LNB + dc + 1])
                        bz = pjz.next()
                        proj_fm(wz[dc // 4], (dc % 4) * 128, tt, bz)
                        return bz

                    def stB(dc, bz):
                        a, bq_ = (2, 3) if dc % 2 == 0 else (4, 5)
                        STT("dve", tf[a][:], tf[bq_][:], 1.0, tf[a][:], ALU.add, ALU.mult, [t_tf[bq_], t_tf[a]], [t_tf[a]])
                        ACT(tf[bq_][:], pbank[bz][:], AF.Tanh, [t_pb[bz]], [t_tf[bq_]], scale=0.5)
                        STT("dve", tf[bq_][:], tf[bq_][:], 1.0, pbank[bz][:], ALU.add, ALU.mult, [t_tf[bq_], t_pb[bz]], [t_tf[bq_]])
                        STT("dve", oagT[:, dc, sl], tf[a][:], 0.25, tf[bq_][:], ALU.mult, ALU.mult, [t_tf[a], t_tf[bq_]], [t_oag[dc][tt]])
                    bzs = {0: stA(0)}
                    for dc in range(8):
                        if dc + 1 < 8:
                            bzs[dc + 1] = stA(dc + 1)
                        stB(dc, bzs[dc])
            add_step([[w_in[:, 6144:6656]], [w_in[:, 6656:7168]]], f2)
            for qd in range(4):
                merge_step(seq, 1, qd, w_ob)

        def phase_C(seq):
            def fmem():
                S.barrier(S.ENGS, dma_toks=t_xt + t_xo)
                rmsnorm_T(mem_d[seq * 256:(seq + 1) * 256, :], 2, C_GMEM, memT, lambda kc, tb: t_memT, 256)
                S.barrier(S.COMPUTE, dma_toks=t_xt + t_xo)
            add_step([], fmem)
            for p in range(2):
                def fkk(widx, p=p):
                    for ci in range(4):
                        b = pj.next()
                        for kc in range(8):
                            MM(pbank[b][:, 0:256], wb[widx][:, kc, ci * 128:(ci + 1) * 128], memT[:, kc, :], kc == 0, kc == 7,
                               [t_wb[widx], t_memT], [t_pb[b]])
                        ACT(kmT[:, p * 4 + ci, :], pbank[b][:, 0:256], AF.Copy, [t_pb[b]], [t_kmT], scale=1.0 / 16)
                add_step([[w_kv[:, p * 512:(p + 1) * 512]]], fkk)
            for p in range(2):
                def fvv(widx, p=p):
                    for kb in range(2):
                        b = pj.next()
                        for kc in range(8):
                            MM(pbank[b][:], memT[:, kc, kb * 128:(kb + 1) * 128], wb[widx][:, kc, :], kc == 0, kc == 7,
                               [t_wb[widx], t_memT], [t_pb[b]])
                        ACT(vm[:, kb, p * 512:(p + 1) * 512], pbank[b][:], AF.Copy, [t_pb[b]], [t_vm])
                add_step([[w_kv[:, 1024 + p * 512:1024 + (p + 1) * 512]]], fvv)
            for hx in range(4):
                def fx(widx, hx=hx):
                    for dci in range(2):
                        for tt in range(4):
                            b = pj.next()
                            proj_fm(widx, dci * 128, tt, b)
                            ACT(qcT[:, dci, tt * 512:(tt + 1) * 512], pbank[b][:], AF.Copy, [t_pb[b]], [t_qc])
                            b2 = pj.next()
                            proj_fm(widx, 256 + dci * 128, tt, b2)
                            silu2_from_psum(b2, szc[:, dci, tt * 512:(tt + 1) * 512], t_szc, 4 + (tt % 2))
                    def s1(G):
                        sl = slice(G * 512, (G + 1) * 512)
                        eis = []
                        for kb in range(2):
                            bs = ps_r.next()
                            for dci in range(2):
                                MM(pbank[bs][:], kmT[:, hx * 2 + dci, kb * 128:(kb + 1) * 128], qcT[:, dci, sl], dci == 0, dci == 1,
                                   [t_kmT, t_qc], [t_pb[bs]])
                            ei = et_r.next()
                            ACT(et[ei][:], pbank[bs][:], AF.Exp, [t_pb[bs]], [t_et[ei]])
                            eis.append(ei)
                        return eis

                    def s2(G, eis):
                        sl = slice(G * 512, (G + 1) * 512)
                        for ec in range(2):
                            for kb in range(2):
                                MM(pbank[PO[ec]][:], vm[:, kb, hx * 256 + ec * 128:hx * 256 + (ec + 1) * 128], et[eis[kb]][:],
                                   kb == 0, kb == 1, [t_vm, t_et[eis[kb]]], [t_pb[PO[ec]]])
                        for kb in range(2):
                            MM(pbank[PL[0]][:], ones_b[:], et[eis[kb]][:], kb == 0, kb == 1, [t_et[eis[kb]]] + CT, [t_pb[PL[0]]])
                        for ec in range(2):
                            ACT(tf[1 + ec][:], pbank[PO[ec]][:], AF.Copy, [t_pb[PO[ec]]], [t_tf[1 + ec]])
                        S.emit("dve", lambda e: e.tensor_copy(out=tf[0][:], in_=pbank[PL[0]][:]), [t_pb[PL[0]]], [t_tf[0]])
                        S.emit("dve", lambda e: e.reciprocal(out=tf[0][:], in_=tf[0][:]), [t_tf[0]], [t_tf[0]])
                        for ec in range(2):
                            TT("dve", tf[1 + ec][:], tf[1 + ec][:], tf[0][:], ALU.mult, [t_tf[1 + ec], t_tf[0]], [t_tf[1 + ec]])
                            STT("dve", oagT[:, hx * 2 + ec, sl], tf[1 + ec][:], 0.5, szc[:, ec, sl], ALU.mult, ALU.mult,
                                [t_tf[1 + ec], t_szc], [t_oag[hx * 2 + ec][G]])
                    e_next = s1(0)
                    for G in range(4):
                        e_cur = e_next
                        if G < 3:
                            e_next = s1(G + 1)
                        s2(G, e_cur)
                add_step([[w_in[:, 7168 + hx * 256:7168 + (hx + 1) * 256], w_in[:, 8192 + hx * 256:8192 + (hx + 1) * 256]]], fx)
            for qd in range(4):
                merge_step(seq, 2, qd, w_oc)

        def phase_D(seq):
            def fd(w0, w1):
                S.barrier(S.ENGS, dma_toks=[])
                wz = (w0, w1)
                for tb in range(16):
                    bi = tb % 2
                    tt = tb // 4
                    S.dma_load("sp", t_xt[bi], xt[bi], x_d[seq * SEQ + tb * 128:seq * SEQ + (tb + 1) * 128, :])
                    banks = (PO[bi], PL[bi])
                    for hf in range(2):
                        for kc in range(8):
                            MM(pbank[banks[hf]][:], m2T[:, kc, tb * 128:(tb + 1) * 128], wb[wz[hf]][:, kc, :], kc == 0, kc == 7,
                               [t_wb[wz[hf]], t_m2[kc][tt]], [t_pb[banks[hf]]])
                    S.emit("dve", lambda e: e.memset(ssum[:, 2:4], 0.0), [], [t_ssum])
                    for hf in range(2):
                        ACT(junk[:, 0:512], pbank[banks[hf]][:], AF.Square, [t_pb[banks[hf]], t_ssum], [t_junk, t_ssum],
                            accum_out=ssum[:, 2 + hf:3 + hf])
                    TT("dve", ssum[:, 4:5], ssum[:, 2:3], ssum[:, 3:4], ALU.add, [t_ssum], [t_ssum])
                    TS("dve", ssum[:, 4:5], ssum[:, 4:5], 0.25 / DM, 1e-6, ALU.mult, ALU.add, [t_ssum], [t_ssum])
                    RSQRT(ssum[:, 4:5], 1, [t_ssum] + CT, [t_ssum])
                    TS("dve", ssum[:, 5:6], ssum[:, 4:5], 0.5, None, ALU.mult, None, [t_ssum], [t_ssum])
                    for hf in range(2):
                        cs = slice(hf * 512, (hf + 1) * 512)
                        STT("dve", xo[bi][:, cs], pbank[banks[hf]][:], ssum[:, 5:6], gpost[:, cs], ALU.mult, ALU.mult,
                            [t_pb[banks[hf]], t_ssum] + CT, [t_xo[bi]])
                        TT("dve", xo[bi][:, cs], xo[bi][:, cs], xt[bi][:, cs], ALU.add, [t_xo[bi], t_xt[bi]], [t_xo[bi]])
                    S.dma_store("sp", t_xo[bi], out_d[seq * SEQ + tb * 128:seq * SEQ + (tb + 1) * 128, :], xo[bi])
            add_step([[w_out[:, 0:512]], [w_out[:, 512:1024]]], fd)

        for seq in range(NSEQ):
            def f0(seq=seq):
                S.barrier(S.ENGS, dma_toks=t_xt + t_xo)
                rmsnorm_T(x_d[seq * SEQ:(seq + 1) * SEQ, :], 16, C_GPRE, hT, lambda kc, tb: t_hT[kc][tb // 4], SEQ)
                S.barrier(S.COMPUTE, dma_toks=t_xt + t_xo)
            add_step([], f0)
            if "A" in phases:
                phase_A(seq)
            if "B" in phases:
                phase_B(seq)
            if "C" in phases:
                phase_C(seq)
            if debug and seq == 0:
                def fdbg():
                    t_dbg = S.tok("dbg")
                    for name in debug:
                        src = {"hT": hT, "m2T": m2T, "oagT": oagT}[name]
                        rd = hT_all if name == "hT" else [t for r in (t_m2 if name == "m2T" else t_oag) for t in r]
                        for kc in range(8):
                            for tt in range(4):
                                S.emit("dve", lambda e, kc=kc, tt=tt, src=src: e.tensor_copy(out=tf[0][:], in_=src[:, kc, tt * 512:(tt + 1) * 512]),
                                       rd, [t_tf[0]])
                                S.dma_store("sp", t_tf[0], dbg_d[name][kc * 128:(kc + 1) * 128, tt * 512:(tt + 1) * 512], tf[0][:])
                add_step([], fdbg)
            if "D" in phases:
                phase_D(seq)
        if phases != "0ABCD":
            def ftouch():
                t_touch = S.tok("touch")
                for ap in (mem_d, w_in, w_oa, w_ob, w_oc, w_out, w_kv):
                    S.dma_load("sp", t_touch, ssum[0:1, 0:8], ap[0:1, 0:8])
                S.emit("dve", lambda e: e.memset(tf[1][:], 0.0), [], [t_tf[1]])
                S.dma_store("sp", t_tf[1], out_d[0:128, 0:512], tf[1][:])
            add_step([], ftouch)
        run_steps()
        S.build()
        print("instr counts", {e: len(S.prog[e]) for e in S.ENGS}, "signals", S.sigcount, "waits", S.nwaits)
    return nc


def _t5_bucket_host(rel):
    nb, max_exact = 16, 8
    try:
        import jax
        import jax.numpy as jnp
        cpu = jax.devices("cpu")[0]
        with jax.default_device(cpu):
            r = jnp.asarray(rel, dtype=jnp.int32)
            ret = (r > 0).astype(jnp.int32) * nb
            n = jnp.abs(r)
            nf = jnp.maximum(n, 1).astype(jnp.float32)
            large = max_exact + (jnp.log(nf / max_exact) / math.log(128 / max_exact) * (nb - max_exact)).astype(jnp.int32)
            large = jnp.minimum(large, nb - 1)
            return np.asarray(ret + jnp.where(n < max_exact, n, large))
    except Exception:
        ret = (rel > 0).astype(np.int32) * nb
        n = np.abs(rel)
        nf = np.maximum(n, 1).astype(np.float32)
        large = max_exact + (np.log(nf / np.float32(max_exact)) / np.float32(math.log(128 / max_exact))
                             * np.float32(nb - max_exact)).astype(np.int32)
        large = np.minimum(large, nb - 1)
        return ret + np.where(n < max_exact, n, large)


def _colvec(v):
    v = np.asarray(v, np.float32).reshape(-1)
    return np.ascontiguousarray(v.reshape(-1, 128).T)


def prep_shared(rel_bias, g_pre, g_mem, w_in, b_merge, lam_q1, lam_k1, lam_q2, lam_k2, g_subln, w_oa, conv_w,
                conv_b, ln_g, ln_b, w_ob, w_mem_kv, w_oc, w_out, g_post):
    f = lambda a: np.ascontiguousarray(np.asarray(a, np.float32))
    rel_bias = f(rel_bias)
    cols = np.zeros((128, NCOLS), np.float32)
    cols[:, C_GPRE:C_GPRE + 8] = _colvec(g_pre[0])
    cols[:, C_GMEM:C_GMEM + 8] = _colvec(g_mem[0])
    cols[:, C_BM:C_BM + 24] = _colvec(b_merge[0])
    cols[:, C_CB:C_CB + 8] = _colvec(conv_b[0])
    cols[:, C_LNG:C_LNG + 8] = _colvec(ln_g[0])
    cols[:, C_LNB:C_LNB + 8] = _colvec(ln_b[0])
    cols[:, C_GSUB:C_GSUB + 1] = _colvec(g_subln[0])
    far_b = int(_t5_bucket_host(np.array([-1000], np.int32))[0])
    cols[:, C_CFAR:C_CFAR + 8] = np.broadcast_to(rel_bias[far_b][None, :], (128, 8))
    cw = f(conv_w)[0]
    convw = np.ascontiguousarray(cw.reshape(31, 8, 128).transpose(2, 1, 0)).reshape(128, 8 * 31)
    kk = np.arange(128, dtype=np.int32)[:, None]
    qq = np.arange(128, dtype=np.int32)[None, :]
    relD = kk - qq
    relU = kk - qq - 128
    bD = _t5_bucket_host(relD)
    bU = _t5_bucket_host(relU)
    allowed = (kk // 64) <= (qq // 64)
    ext = np.concatenate([rel_bias, np.full((1, 8), NEG, np.float32)], axis=0)
    idxD = np.where(allowed, bD, 32)
    D = ext[idxD]
    U = ext[bU]
    biasdu = np.ascontiguousarray(np.concatenate([D, U], axis=1).transpose(0, 2, 1)).reshape(128, 8 * 256)
    lamv = np.concatenate([f(lam_q1)[0], f(lam_k1)[0], f(lam_q2)[0], f(lam_k2)[0]]).astype(np.float32)
    return {
        "w_in": f(w_in)[0], "w_oa": f(w_oa)[0], "w_ob": f(w_ob)[0], "w_oc": f(w_oc)[0], "w_out": f(w_out)[0],
        "w_mem_kv": f(w_mem_kv)[0], "cols": cols, "convw": convw, "g_post": f(g_post)[0], "lamv": lamv,
        "biasdu": biasdu.astype(np.float32),
    }


_PROG = {}


def kernel(x, mem, rel_bias, g_pre, g_mem, w_in, b_merge, lam_q1, lam_k1, lam_q2, lam_k2, g_subln, w_oa, conv_w,
           conv_b, ln_g, ln_b, w_ob, w_mem_kv, w_oc, w_out, g_post):
    shared = prep_shared(rel_bias, g_pre, g_mem, w_in, b_merge, lam_q1, lam_k1, lam_q2, lam_k2, g_subln, w_oa,
                         conv_w, conv_b, ln_g, ln_b, w_ob, w_mem_kv, w_oc, w_out, g_post)
    x = np.asarray(x, np.float32)
    mem = np.asarray(mem, np.float32)
    B = x.shape[0]
    per = B // NCORES
    if "p" not in _PROG:
        _PROG["p"] = build_program(NSEQ=per)
    nc = _PROG["p"]
    in_maps = []
    for c in range(NCORES):
        m = dict(shared)
        m["x"] = np.ascontiguousarray(x[c * per:(c + 1) * per].reshape(per * SEQ, DM))
        m["mem"] = np.ascontiguousarray(mem[c * per:(c + 1) * per].reshape(per * 256, DM))
        in_maps.append(m)
    res = run_bass_kernel_spmd(nc, in_maps, core_ids=list(range(NCORES)))
    outs = [np.asarray(r["out"], np.float32).reshape(per, SEQ, DM) for r in res.results]
    return np.concatenate(outs, axis=0)
```

```python
import math
from contextlib import ExitStack
import numpy as np
import concourse.bass as bass
import concourse.mybir as mybir
from concourse.bass_utils import run_bass_kernel_spmd

F32 = mybir.dt.float32
BF16 = mybir.dt.bfloat16
AF = mybir.ActivationFunctionType
ALU = mybir.AluOpType
AX = mybir.AxisListType

SAME_ENGINE_SYNC = True
EPOCH = 20000
NEG = -30000.0
SEQ = 2048
DM = 1024
NCORES = 8


class Tk:
    __slots__ = ("name", "lw", "rd", "dsem", "dcnt")

    def __init__(self, name):
        self.name = name
        self.lw = None
        self.rd = {}
        self.dsem = None
        self.dcnt = 0


class Ins:
    __slots__ = ("eng", "fn", "cdeps", "raw", "dwaits", "sig", "val", "dma", "waits")

    def __init__(self, eng, fn):
        self.eng = eng
        self.fn = fn
        self.cdeps = set()
        self.raw = set()
        self.dwaits = {}
        self.sig = False
        self.val = None
        self.dma = None
        self.waits = []


class Sched:
    ENGS = ("pe", "act", "dve", "pool", "sp")
    COMPUTE = ("pe", "act", "dve", "pool")

    def __init__(self, nc, stack):
        self.nc = nc
        self.stack = stack
        self.prog = {e: [] for e in self.ENGS}
        self.esem = {e: [] for e in self.ENGS}
        self.toks = []
        self.last_compute = {e: None for e in self.ENGS}
        self.pending = {e: (set(), {}) for e in self.ENGS}

    def tok(self, name):
        t = Tk(name)
        self.toks.append(t)
        return t

    def _dep(self, ins, p, raw):
        if p is None or p is ins:
            return
        if p.dma is not None:
            t = p.dma
            ins.dwaits[t] = max(ins.dwaits.get(t, 0), 16 * t.dcnt)
        else:
            ins.cdeps.add(p)
            if raw:
                ins.raw.add(p)

    def emit(self, eng, fn, reads=(), writes=(), dma=None):
        ins = Ins(eng, fn)
        pc, pd = self.pending[eng]
        if pc or pd:
            for p in pc:
                if p.eng != eng:
                    ins.cdeps.add(p)
            for t, v in pd.items():
                ins.dwaits[t] = max(ins.dwaits.get(t, 0), v)
            self.pending[eng] = (set(), {})
        for t in reads:
            self._dep(ins, t.lw, True)
        for t in writes:
            self._dep(ins, t.lw, False)
            for r in t.rd.values():
                self._dep(ins, r, False)
        if dma is not None:
            if dma.dsem is None:
                dma.dsem = self.stack.enter_context(self.nc.semaphore("ds_" + dma.name))
            dma.dcnt += 1
            ins.dma = dma
        else:
            self.last_compute[eng] = ins
        for t in reads:
            key = eng if dma is None else ("dma", id(ins))
            t.rd[key] = ins
        for t in writes:
            t.lw = ins
            t.rd = {}
        self.prog[eng].append(ins)
        return ins

    def barrier(self, engs, dma_toks=()):
        lasts = set(self.last_compute[e] for e in self.COMPUTE if self.last_compute[e] is not None)
        for e in engs:
            pc, pd = self.pending[e]
            pc |= lasts
            for t in dma_toks:
                if t.dsem is not None:
                    pd[t] = max(pd.get(t, 0), 16 * t.dcnt)

    def dma_load(self, q, tok, out_ap, in_ap, **kw):
        return self.emit(q, lambda e: e.dma_start(out=out_ap, in_=in_ap, **kw), writes=[tok], dma=tok)

    def dma_store(self, q, tok, out_ap, in_ap, **kw):
        return self.emit(q, lambda e: e.dma_start(out=out_ap, in_=in_ap, **kw), reads=[tok], dma=tok)

    def _needs(self, ins, p):
        if p.eng != ins.eng:
            return True
        return SAME_ENGINE_SYNC and p.eng != "pe"

    def finalize(self):
        for e in self.ENGS:
            for ins in self.prog[e]:
                for p in ins.cdeps:
                    if self._needs(ins, p):
                        p.sig = True
        for e in self.ENGS:
            c = 0
            for ins in self.prog[e]:
                if ins.sig:
                    ep, v = divmod(c, EPOCH)
                    if ep >= len(self.esem[e]):
                        self.esem[e].append(self.stack.enter_context(self.nc.semaphore("es_%s%d" % (e, ep))))
                    ins.val = (ep, v + 1)
                    c += 1
        self.sigcount = {e: sum(1 for i in self.prog[e] if i.sig) for e in self.ENGS}
        self.nwaits = 0
        for e in self.ENGS:
            known = {}
            for ins in self.prog[e]:
                need = {}
                for p in ins.cdeps:
                    if not self._needs(ins, p):
                        continue
                    need[p.eng] = max(need.get(p.eng, (0, 0)), p.val)
                for pe_, v in need.items():
                    if known.get(pe_, (0, 0)) < v:
                        ins.waits.append((self.esem[pe_][v[0]], v[1]))
                        known[pe_] = v
                for t, v in ins.dwaits.items():
                    k = ("d", id(t))
                    if known.get(k, 0) < v:
                        ins.waits.append((t.dsem, v))
                        known[k] = v
                self.nwaits += len(ins.waits)

    def simulate(self):
        sem = {}
        pc = {e: 0 for e in self.ENGS}
        progress = True
        while progress:
            progress = False
            for e in self.ENGS:
                while pc[e] < len(self.prog[e]):
                    ins = self.prog[e][pc[e]]
                    if all(sem.get(id(s), 0) >= v for (s, v) in ins.waits):
                        if ins.dma is not None:
                            sem[id(ins.dma.dsem)] = sem.get(id(ins.dma.dsem), 0) + 16
                        elif ins.sig:
                            k = id(self.esem[e][ins.val[0]])
                            sem[k] = sem.get(k, 0) + 1
                        pc[e] += 1
                        progress = True
                    else:
                        break
        stuck = {e: (pc[e], len(self.prog[e])) for e in self.ENGS if pc[e] < len(self.prog[e])}
        if stuck:
            for e in stuck:
                ins = self.prog[e][pc[e]]
                print("STUCK", e, pc[e], [(s.name, v, sem.get(id(s), 0)) for (s, v) in ins.waits])
            raise RuntimeError("deadlock in generated program: %s" % stuck)

    def run_engine(self, ename, e):
        for ins in self.prog[ename]:
            for (s, v) in ins.waits:
                e.wait_ge(s, v)
            bi = ins.fn(e)
            if ins.dma is not None:
                bi.then_inc(ins.dma.dsem, 16)
            elif ins.sig:
                bi.then_inc(self.esem[ename][ins.val[0]], 1)

    def build(self):
        fin = {}
        for t in self.toks:
            if t.dsem is not None:
                fin[t] = 16 * t.dcnt
        self.finalize()
        self.simulate()
        nc = self.nc
        with nc.Block() as block:
            @block.tensor
            def _(e):
                self.run_engine("pe", e)

            @block.scalar
            def _(e):
                self.run_engine("act", e)

            @block.vector
            def _(e):
                self.run_engine("dve", e)

            @block.gpsimd
            def _(e):
                self.run_engine("pool", e)

            @block.sync
            def _(e):
                self.run_engine("sp", e)
                for t, v in fin.items():
                    e.wait_ge(t.dsem, v)


STEP_LOG = []
EPI_D1 = 7
EPI_D2 = 9
NCOLS = 80
C_GPRE, C_GMEM, C_BM, C_CB, C_LNG, C_LNB, C_GSUB, C_CFAR = 0, 8, 16, 40, 48, 56, 64, 65


def build_program(NSEQ=2, debug=None, phases="0ABCD"):
    nc = bass.Bass("TRN2", target_bir_lowering=False)
    dram = lambda n, shp, kind="ExternalInput": nc.dram_tensor(n, shp, F32, kind=kind).ap()
    x_d = dram("x", [NSEQ * SEQ, DM])
    mem_d = dram("mem", [NSEQ * 256, DM])
    w_in = dram("w_in", [DM, 12288])
    w_oa = dram("w_oa", [DM, DM])
    w_ob = dram("w_ob", [DM, DM])
    w_oc = dram("w_oc", [DM, DM])
    w_out = dram("w_out", [DM, DM])
    w_kv = dram("w_mem_kv", [DM, 2048])
    cols_d = dram("cols", [128, NCOLS])
    convw_d = dram("convw", [128, 8 * 31])
    gpost_d = dram("g_post", [DM])
    lamv_d = dram("lamv", [256])
    bias_d = dram("biasdu", [128, 8 * 256])
    out_d = dram("out", [NSEQ * SEQ, DM], kind="ExternalOutput")
    dbg_d = {}
    if debug:
        for name, shp in debug.items():
            dbg_d[name] = dram("dbg_" + name, shp, kind="ExternalOutput")

    with ExitStack() as st:
        S = Sched(nc, st)
        sb = lambda name, shape, dt: st.enter_context(nc.sbuf_tensor("s_" + name, shape, dt))
        psb = lambda name, shape, dt: st.enter_context(nc.psum_tensor("p_" + name, shape, dt))

        cols = sb("cols", [128, NCOLS], F32)
        convw = sb("convw", [128, 8, 31], F32)
        gpost = sb("gpost", [128, DM], F32)
        lamv = sb("lamv", [128, 256], F32)
        bhi = sb("bhi", [128, 8, 256], BF16)
        blo = sb("blo", [128, 8, 256], BF16)
        identf = sb("identf", [128, 128], F32)
        identb = sb("identb", [128, 128], BF16)
        ones_b = sb("ones_b", [128, 128], BF16)
        ones128 = sb("ones128", [128, 128], BF16)
        ones1024 = sb("ones1024", [128, 128], BF16)
        mhalf = sb("mhalf", [128, 512], F32)
        small = sb("small", [128, 16], F32)
        hbm = sb("hbm", [128, 24], F32)
        hlng = sb("hlng", [128, 8], F32)
        hlnb = sb("hlnb", [128, 8], F32)
        t_const = S.tok("const")
        S.dma_load("sp", t_const, cols[:], cols_d)
        S.dma_load("sp", t_const, convw[:], convw_d.rearrange("p (c j) -> p c j", j=31))
        S.dma_load("sp", t_const, gpost[:], gpost_d.partition_broadcast(128))
        S.dma_load("sp", t_const, lamv[:], lamv_d.partition_broadcast(128))
        t_c2 = S.tok("const2")
        S.emit("pool", lambda e: e.memset(identf[:], 0.0), writes=[t_c2])
        S.emit("pool", lambda e: e.affine_select(out=identf[:], in_=identf[:], pattern=[[-1, 128]],
                                                 compare_op=ALU.not_equal, fill=1.0, base=0,
                                                 channel_multiplier=1), reads=[t_c2], writes=[t_c2])
        S.emit("pool", lambda e: e.memset(mhalf[:], -0.5), writes=[t_c2])
        S.emit("dve", lambda e: e.tensor_copy(out=identb[:], in_=identf[:]), reads=[t_c2], writes=[t_c2])
        S.emit("dve", lambda e: e.memset(ones_b[:], 1.0), writes=[t_c2])
        S.emit("dve", lambda e: e.memset(ones128[:], 1.0 / 128), writes=[t_c2])
        S.emit("dve", lambda e: e.memset(ones1024[:], 1.0 / 1024), writes=[t_c2])
        lam_init = 0.8 - 0.6 * math.exp(-0.3 * 0)
        S.emit("dve", lambda e: e.tensor_tensor(out=lamv[:, 0:64], in0=lamv[:, 0:64], in1=lamv[:, 64:128], op=ALU.mult),
               reads=[t_const], writes=[t_c2])
        S.emit("dve", lambda e: e.tensor_tensor(out=lamv[:, 128:192], in0=lamv[:, 128:192], in1=lamv[:, 192:256], op=ALU.mult),
               reads=[t_const], writes=[t_c2])
        S.emit("dve", lambda e: e.reduce_sum(out=small[:, 2:3], in_=lamv[:, 0:64], axis=AX.X), reads=[t_c2], writes=[t_c2])
        S.emit("dve", lambda e: e.reduce_sum(out=small[:, 3:4], in_=lamv[:, 128:192], axis=AX.X), reads=[t_c2], writes=[t_c2])
        S.emit("act", lambda e: e.activation(out=small[:, 4:6], in_=small[:, 2:4], func=AF.Exp), reads=[t_c2], writes=[t_c2])
        S.emit("dve", lambda e: e.tensor_tensor(out=small[:, 0:1], in0=small[:, 5:6], in1=small[:, 4:5], op=ALU.subtract),
               reads=[t_c2], writes=[t_c2])
        S.emit("dve", lambda e: e.tensor_scalar(out=small[:, 0:1], in0=small[:, 0:1], scalar1=-lam_init, scalar2=None, op0=ALU.add),
               reads=[t_c2], writes=[t_c2])
        S.emit("dve", lambda e: e.tensor_scalar(out=small[:, 1:2], in0=cols[:, C_GSUB:C_GSUB + 1], scalar1=0.5 * (1.0 - lam_init),
                                                scalar2=None, op0=ALU.mult), reads=[t_const], writes=[t_c2])
        S.emit("dve", lambda e: e.tensor_scalar(out=hbm[:], in0=cols[:, C_BM:C_BM + 24], scalar1=0.5, scalar2=None, op0=ALU.mult),
               reads=[t_const], writes=[t_c2])
        S.emit("dve", lambda e: e.tensor_scalar(out=hlng[:], in0=cols[:, C_LNG:C_LNG + 8], scalar1=0.5, scalar2=None, op0=ALU.mult),
               reads=[t_const], writes=[t_c2])
        S.emit("dve", lambda e: e.tensor_scalar(out=hlnb[:], in0=cols[:, C_LNB:C_LNB + 8], scalar1=0.5, scalar2=None, op0=ALU.mult),
               reads=[t_const], writes=[t_c2])
        S.emit("dve", lambda e: e.tensor_scalar(out=convw[:], in0=convw[:], scalar1=0.5, scalar2=None, op0=ALU.mult),
               reads=[t_const], writes=[t_c2])
        CT = [t_const, t_c2]
        nlam = small[:, 0:1]
        gsub = small[:, 1:2]

        hT = sb("hT", [128, 8, SEQ], BF16)
        m2T = sb("m2T", [128, 8, SEQ], BF16)
        big = sb("big", [128, 8192], F32)
        oagT = big[:].bitcast(BF16).rearrange("p (c t) -> p c t", c=8)
        xt = [big[:, i * 1024:(i + 1) * 1024] for i in range(2)]
        xo = [big[:, 2048 + i * 1024:2048 + (i + 1) * 1024] for i in range(2)]
        scr = sb("scr", [128, 16384], BF16)
        junk = sb("junk", [128, 1024], BF16)
        et = [sb("et%d" % i, [128, 512], BF16) for i in range(4)]
        tf = [sb("tf%d" % i, [128, 512], F32) for i in range(8)]
        sqb = [sb("sqb%d" % i, [128, 512], BF16) for i in range(2)]
        NPE = 23
        diagb = [sb("diag%d" % i, [128, NPE, 128], BF16) for i in range(2)]
        dcnt = [0]
        ssum = sb("ssum", [128, 8], F32)
        NB = 3
        wb = [sb("wb%d" % i, [128, 8, 512], BF16) for i in range(NB)]
        t_wb = [S.tok("wb%d" % i) for i in range(NB)]
        pbank = [psb("pb%d" % i, [128, 512], F32) for i in range(8)]
        t_pb = [S.tok("pb%d" % i) for i in range(8)]
        PJ, PS_, PO, PL = (0, 1), (2, 3), (4, 5), (6, 7)

        t_hT = [[S.tok("hT%d_%d" % (kc, tt)) for tt in range(4)] for kc in range(8)]
        t_m2 = [[S.tok("m2%d_%d" % (kc, tt)) for tt in range(4)] for kc in range(8)]
        t_oag = [[S.tok("oag%d_%d" % (kc, tt)) for tt in range(4)] for kc in range(8)]
        t_xt = [S.tok("xt%d" % i) for i in range(2)]
        t_xo = [S.tok("xo%d" % i) for i in range(2)]
        t_junk = S.tok("junk")
        t_et = [S.tok("et%d" % i) for i in range(4)]
        t_tf = [S.tok("tf%d" % i) for i in range(8)]
        t_sqb = [S.tok("sqb0"), S.tok("sqb1")]
        t_diagb = [S.tok("diag0"), S.tok("diag1")]
        t_ssum = S.tok("ssum")
        hT_all = [t for r in t_hT for t in r]
        print("sbuf bytes remaining:", nc.sbuf_bytes_remaining)

        def MM(out, lhsT, rhs, start, stop, reads, writes):
            S.emit("pe", lambda e: e.matmul(out, lhsT=lhsT, rhs=rhs, start=start, stop=stop), reads, writes)

        def ACT(out, in_, func, reads, writes, eng="act", **kw):
            S.emit(eng, lambda e: e.activation(out=out, in_=in_, func=func, **kw), reads, writes)

        def TS(eng, out, in0, s1, s2, op0, op1, reads, writes):
            if s2 is None:
                S.emit(eng, lambda e: e.tensor_scalar(out=out, in0=in0, scalar1=s1, scalar2=None, op0=op0), reads, writes)
            else:
                S.emit(eng, lambda e: e.tensor_scalar(out=out, in0=in0, scalar1=s1, scalar2=s2, op0=op0, op1=op1), reads, writes)

        def STT(eng, out, in0, scalar, in1, op0, op1, reads, writes):
            S.emit(eng, lambda e: e.scalar_tensor_tensor(out=out, in0=in0, scalar=scalar, in1=in1, op0=op0, op1=op1), reads, writes)

        def TT(eng, out, in0, in1, op, reads, writes):
            S.emit(eng, lambda e: e.tensor_tensor(out=out, in0=in0, in1=in1, op=op), reads, writes)

        def RSQRT(ap, n, reads, writes):
            S.emit("act", lambda e: e.activation(out=ap, in_=ap, func=AF.Sqrt), reads, writes)
            S.emit("dve", lambda e: e.reciprocal(out=ap, in_=ap), writes, writes)

        class Rot:
            def __init__(self, items):
                self.items = items
                self.i = 0

            def next(self):
                v = self.items[self.i]
                self.i = (self.i + 1) % len(self.items)
                return v

        pj = Rot(list(PJ))
        ps_r = Rot(list(PS_))
        et_r = Rot([0, 1, 2, 3])
        ps4 = Rot([PS_[0], PS_[1], PJ[1], PJ[0]])
        pj6 = Rot([0, 1, 4, 5, 6, 7])
        pjz = Rot([0, 1, 6, 7])
        pjx = Rot([0, 1, 7])

        wstate = {"next": 0}

        def wload(slices):
            i = wstate["next"]
            wstate["next"] = (i + 1) % NB
            off = 0
            for ap in slices:
                n = ap.shape[1]
                S.dma_load("pool", t_wb[i], wb[i][:, :, off:off + n], ap.rearrange("(kc p) n -> p kc n", p=128))
                off += n
            return i

        steps = []

        def add_step(panels, fn):
            steps.append((panels, fn))

        def run_steps():
            loaded = {}
            for i, (panels, fn) in enumerate(steps):
                cur = loaded.setdefault(i, [])
                while len(cur) < len(panels):
                    cur.append(wload(panels[len(cur)]))
                if i + 1 < len(steps):
                    nxt_panels = steps[i + 1][0]
                    nxt = loaded.setdefault(i + 1, [])
                    free = NB - len(cur)
                    while len(nxt) < len(nxt_panels) and free > 0:
                        nxt.append(wload(nxt_panels[len(nxt)]))
                        free -= 1
                STEP_LOG.append((getattr(fn, "__name__", "?"), len(S.prog["pe"])))
                fn(*cur)

        def rmsnorm_T(src_rows_ap, ntb, gcol0, dst, dst_toks_fn, width):
            for tb in range(ntb):
                bi = tb % 2
                S.dma_load("sp", t_xt[bi], xt[bi], src_rows_ap[tb * 128:(tb + 1) * 128, :])
                S.emit("dve", lambda e: e.memset(ssum[:, 0:1], 0.0), [], [t_ssum])
                ACT(junk[:], xt[bi], AF.Square, [t_xt[bi], t_ssum], [t_junk, t_ssum], accum_out=ssum[:, 0:1])
                TS("dve", ssum[:, 1:2], ssum[:, 0:1], 1.0 / DM, 1e-6, ALU.mult, ALU.add, [t_ssum], [t_ssum])
                RSQRT(ssum[:, 1:2], 1, [t_ssum] + CT, [t_ssum])
                TS("dve", xo[bi], xt[bi], ssum[:, 1:2], None, ALU.mult, None, [t_xt[bi], t_ssum], [t_xo[bi]])
                for half in range(2):
                    b = ps_r.next()
                    for i in range(4):
                        kc = half * 4 + i
                        S.emit("pe", lambda e, b=b, i=i, kc=kc, bi=bi: e.transpose(
                            out=pbank[b][:, i * 128:(i + 1) * 128], in_=xo[bi][:, kc * 128:(kc + 1) * 128],
                            identity=identf[:]), [t_xo[bi]] + CT, [t_pb[b]])
                    for i in range(4):
                        kc = half * 4 + i
                        dt = dst_toks_fn(kc, tb)
                        ACT(dst[:, kc, tb * 128:(tb + 1) * 128], pbank[b][:, i * 128:(i + 1) * 128], AF.Copy,
                            [t_pb[b]] + CT, [dt], scale=cols[:, gcol0 + kc:gcol0 + kc + 1])

        qT = scr[:, 0:4096].rearrange("p (j t) -> p j t", j=2)
        kT = scr[:, 4096:8192].rearrange("p (j t) -> p j t", j=2)
        vv = scr[:, 8192:12288].rearrange("p (b n) -> p b n", b=16)
        sza = scr[:, 12288:16384].rearrange("p (j t) -> p j t", j=2)
        t_q, t_k, t_v, t_sz = S.tok("qT"), S.tok("kT"), S.tok("vv"), S.tok("sza")
        bsf = scr[:, 0:4096].bitcast(F32).rearrange("p (h k) -> p h k", k=256)
        S.dma_load("sp", t_q, bsf, bias_d.rearrange("p (h k) -> p h k", k=256))
        t_bias = S.tok("biashl")
        for h in range(8):
            TS("dve", bsf[:, h, :], bsf[:, h, :], cols[:, C_CFAR + h:C_CFAR + h + 1], None, ALU.subtract, None, [t_q] + CT, [t_q])
            S.emit("dve", lambda e, h=h: e.tensor_copy(out=bhi[:, h, :], in_=bsf[:, h, :]), [t_q], [t_bias])
            TT("dve", blo[:, h, :], bsf[:, h, :], bhi[:, h, :], ALU.subtract, [t_q, t_bias], [t_bias])
        CW = SEQ + 30
        cT = [scr[:, i * 2 * CW:(i + 1) * 2 * CW].rearrange("p (j t) -> p j t", j=2) for i in range(2)]
        t_cT = [S.tok("cT0"), S.tok("cT1")]
        memT = scr[:, 0:2048].rearrange("p (c t) -> p c t", c=8)
        kmT = scr[:, 2048:4096].rearrange("p (c t) -> p c t", c=8)
        vm = scr[:, 4096:6144].rearrange("p (b n) -> p b n", b=2)
        qcT = scr[:, 6144:10240].rearrange("p (j t) -> p j t", j=2)
        szc = scr[:, 10240:14336].rearrange("p (j t) -> p j t", j=2)
        t_memT, t_kmT, t_vm, t_qc, t_szc = S.tok("memT"), S.tok("kmT"), S.tok("vm"), S.tok("qcT"), S.tok("szc")

        def proj_fm(widx, col0, tt, bank):
            for kc in range(8):
                MM(pbank[bank][:], wb[widx][:, kc, col0:col0 + 128], hT[:, kc, tt * 512:(tt + 1) * 512],
                   kc == 0, kc == 7, [t_wb[widx], t_hT[kc][tt]], [t_pb[bank]])

        def silu2_from_psum(bank, dst, dst_tok, tfi):
            ACT(tf[tfi][:], pbank[bank][:], AF.Tanh, [t_pb[bank]], [t_tf[tfi]], scale=0.5)
            STT("dve", dst, tf[tfi][:], 1.0, pbank[bank][:], ALU.add, ALU.mult, [t_tf[tfi], t_pb[bank]], [dst_tok])

        first_b = min(i for i, p in enumerate("ABC") if p in phases) if any(p in phases for p in "ABC") else 0
        def merge_step(seq, b, qd, w_o):
            def fn(widx):
                for dci in range(2):
                    dc = qd * 2 + dci
                    for tt in range(4):
                        by = pj6.next()
                        for kc in range(8):
                            MM(pbank[by][:], wb[widx][:, kc, dci * 128:(dci + 1) * 128], oagT[:, kc, tt * 512:(tt + 1) * 512],
                               kc == 0, kc == 7, [t_wb[widx], t_oag[kc][tt]], [t_pb[by]])
                        bg = pj6.next()
                        for kc in range(8):
                            MM(pbank[bg][:], wb[widx][:, kc, 256 + dci * 128:256 + (dci + 1) * 128], hT[:, kc, tt * 512:(tt + 1) * 512],
                               kc == 0, kc == 7, [t_wb[widx], t_hT[kc][tt]], [t_pb[bg]])
                        ACT(tf[4][:], pbank[bg][:], AF.Tanh, [t_pb[bg]] + CT, [t_tf[4]], scale=0.5,
                            bias=hbm[:, b * 8 + dc:b * 8 + dc + 1])
                        dst = m2T[:, dc, tt * 512:(tt + 1) * 512]
                        if b == first_b:
                            STT("dve", dst, tf[4][:], 1.0, pbank[by][:], ALU.add, ALU.mult, [t_tf[4], t_pb[by]], [t_m2[dc][tt]])
                        else:
                            STT("dve", tf[5][:], tf[4][:], 1.0, pbank[by][:], ALU.add, ALU.mult, [t_tf[4], t_pb[by]], [t_tf[5]])
                            TT("dve", dst, tf[5][:], dst, ALU.add, [t_tf[5], t_m2[dc][tt]], [t_m2[dc][tt]])
            gcol = 9216 + b * 1024 + qd * 256
            add_step([[w_o[:, qd * 256:(qd + 1) * 256], w_in[:, gcol:gcol + 256]]], fn)

        def attn_group(hg):
            tasks = []
            for j in range(2):
                for G in range(4):
                    nkb = 4 * G + 4
                    for kb in range(nkb):
                        tasks.append((j, G, kb, nkb))
            state = {}
            deferred = []
            epi_n = [0]

            def stage1(t):
                j, G, kb, nkb = t
                h = 2 * hg + j
                jj = kb - 4 * G
                c0 = max(jj, 0) * 128
                W = 512 - c0
                if jj >= 0:
                    boff, nw = 0, min(256, W)
                elif jj == -1:
                    boff, nw = 128, 128
                else:
                    boff, nw = 0, 0
                bss = [ps4.next(), ps4.next()]
                for c in range(2):
                    r0, r1 = c * 64, (c + 1) * 64
                    MM(pbank[bss[c]][:, 0:W], kT[r0:r1, j, kb * 128:(kb + 1) * 128],
                       qT[r0:r1, j, G * 512 + c0:(G + 1) * 512], True, nw == 0, [t_k, t_q], [t_pb[bss[c]]])
                eis = []
                for c in range(2):
                    bs = bss[c]
                    if nw > 0:
                        MM(pbank[bs][:, 0:nw], identb[:], bhi[:, h, boff:boff + nw], False, False, [t_bias] + CT, [t_pb[bs]])
                        MM(pbank[bs][:, 0:nw], identb[:], blo[:, h, boff:boff + nw], False, True, [t_bias] + CT, [t_pb[bs]])
                    ei = et_r.next()
                    ACT(et[ei][:, 0:W], pbank[bs][:, 0:W], AF.Exp, [t_pb[bs]], [t_et[ei]])
                    eis.append(ei)
                state[t] = (eis, c0, W)

            def stage2(t):
                j, G, kb, nkb = t
                h = 2 * hg + j
                eis, c0, W = state.pop(t)
                for c in range(2):
                    po, pl, ei = PO[c], PL[c], eis[c]
                    MM(pbank[po][:, c0:512], vv[:, kb, j * 128:(j + 1) * 128], et[ei][:, 0:W],
                       kb == 0, kb == nkb - 1, [t_v, t_et[ei]], [t_pb[po]])
                    MM(pbank[pl][:, c0:512], ones_b[:], et[ei][:, 0:W],
                       kb == 0, kb == nkb - 1, [t_et[ei]] + CT, [t_pb[pl]])
                if kb == nkb - 1:
                    epilogue(j, G, h)

            def epilogue(j, G, h):
                k = epi_n[0] % 2
                epi_n[0] += 1
                to, tr = (0, 1) if k == 0 else (6, 7)
                ACT(tf[to][:], pbank[PO[0]][:], AF.Copy, [t_pb[PO[0]]], [t_tf[to]])
                ACT(tf[tr][:], pbank[PO[1]][:], AF.Copy, [t_pb[PO[1]]], [t_tf[tr]])
                S.emit("dve", lambda e: e.tensor_copy(out=tf[4][:], in_=pbank[PL[0]][:]), [t_pb[PL[0]]], [t_tf[4]])
                S.emit("dve", lambda e: e.tensor_copy(out=tf[5][:], in_=pbank[PL[1]][:]), [t_pb[PL[1]]], [t_tf[5]])
                S.emit("dve", lambda e: e.reciprocal(out=tf[4][:], in_=tf[4][:]), [t_tf[4]], [t_tf[4]])
                S.emit("dve", lambda e: e.reciprocal(out=tf[5][:], in_=tf[5][:]), [t_tf[5]], [t_tf[5]])
                TT("dve", tf[to][:], tf[to][:], tf[4][:], ALU.mult, [t_tf[to], t_tf[4]], [t_tf[to]])
                TT("dve", tf[tr][:], tf[tr][:], tf[5][:], ALU.mult, [t_tf[tr], t_tf[5]], [t_tf[tr]])
                STT("dve", tf[to][:], tf[tr][:], nlam, tf[to][:], ALU.mult, ALU.add, [t_tf[to], t_tf[tr]] + CT, [t_tf[to]])
                TT("dve", sqb[k][:], tf[to][:], tf[to][:], ALU.mult, [t_tf[to]], [t_sqb[k]])

                def part3a():
                    bq = ps4.next()
                    MM(pbank[bq][:], ones128[:], sqb[k][:], True, True, [t_sqb[k]] + CT, [t_pb[bq]])
                    TS("dve", tf[tr][:], pbank[bq][:], 1e-6, None, ALU.add, None, [t_pb[bq]], [t_tf[tr]])

                def part3b():
                    RSQRT(tf[tr][:], 512, [t_tf[tr]] + CT, [t_tf[tr]])
                    STT("dve", tf[to][:], tf[to][:], gsub, tf[tr][:], ALU.mult, ALU.mult, [t_tf[to], t_tf[tr]] + CT, [t_tf[to]])
                    TT("dve", oagT[:, h, G * 512:(G + 1) * 512], tf[to][:], sza[:, j, G * 512:(G + 1) * 512], ALU.mult,
                       [t_tf[to], t_sz], [t_oag[h][G]])
                deferred.append([EPI_D1, part3a])
                deferred.append([EPI_D2, part3b])

            def tick():
                for d in list(deferred):
                    d[0] -= 1
                    if d[0] <= 0:
                        deferred.remove(d)
                        d[1]()

            n = len(tasks)
            LOOK = 1
            for i in range(min(LOOK, n)):
                stage1(tasks[i])
            for i in range(n):
                if i + LOOK < n:
                    stage1(tasks[i + LOOK])
                stage2(tasks[i])
                tick()
            while deferred:
                tick()

        def phase_A(seq):
            for hg in range(4):
                def fq(widx, hg=hg):
                    for j in range(2):
                        for tt in range(4):
                            b = pj6.next()
                            proj_fm(widx, j * 128, tt, b)
                            ACT(qT[:, j, tt * 512:(tt + 1) * 512], pbank[b][:], AF.Copy, [t_pb[b]], [t_q], scale=0.125)
                add_step([[w_in[:, hg * 256:(hg + 1) * 256]]], fq)

                def fk(widx, hg=hg):
                    for j in range(2):
                        for tt in range(4):
                            b = pj6.next()
                            proj_fm(widx, j * 128, tt, b)
                            S.emit("dve", lambda e, b=b, j=j, tt=tt: e.tensor_copy(out=kT[:, j, tt * 512:(tt + 1) * 512], in_=pbank[b][:]),
                                   [t_pb[b]], [t_k])
                add_step([[w_in[:, 1024 + hg * 256:1024 + (hg + 1) * 256]]], fk)

                def fv(widx, hg=hg):
                    for tb in range(16):
                        b = pj6.next()
                        tt = tb // 4
                        for kc in range(8):
                            MM(pbank[b][:, 0:256], hT[:, kc, tb * 128:(tb + 1) * 128], wb[widx][:, kc, 0:256],
                               kc == 0, kc == 7, [t_wb[widx], t_hT[kc][tt]], [t_pb[b]])
                        if tb % 2 == 0:
                            ACT(vv[:, tb, :], pbank[b][:, 0:256], AF.Copy, [t_pb[b]], [t_v])
                        else:
                            S.emit("dve", lambda e, b=b, tb=tb: e.tensor_copy(out=vv[:, tb, :], in_=pbank[b][:, 0:256]), [t_pb[b]], [t_v])
                add_step([[w_in[:, 2048 + hg * 256:2048 + (hg + 1) * 256]]], fv)

                def fz(widx, hg=hg):
                    for j in range(2):
                        for tt in range(4):
                            b = pj6.next()
                            proj_fm(widx, j * 128, tt, b)
                            silu2_from_psum(b, sza[:, j, tt * 512:(tt + 1) * 512], t_sz, 4 + (tt % 2))
                add_step([[w_in[:, 3072 + hg * 256:3072 + (hg + 1) * 256]]], fz)

                def fa(hg=hg):
                    attn_group(hg)
                add_step([], fa)
            for qd in range(4):
                merge_step(seq, 0, qd, w_oa)

        def phase_B(seq):
            def bar():
                S.barrier(S.COMPUTE)
            add_step([], bar)
            for qd in range(4):
                def fc(widx, qd=qd):
                    ci = qd % 2
                    S.emit("dve", lambda e, ci=ci: e.memset(cT[ci][:, :, 0:30], 0.0), [], [t_cT[ci]])
                    for dci in range(2):
                        dc = qd * 2 + dci
                        for tt in range(4):
                            ba = pj6.next()
                            proj_fm(widx, dci * 128, tt, ba)
                            bg = pj6.next()
                            proj_fm(widx, 256 + dci * 128, tt, bg)
                            ACT(tf[4][:], pbank[bg][:], AF.Tanh, [t_pb[bg]], [t_tf[4]], scale=0.5)
                            STT("dve", cT[ci][:, dci, 30 + tt * 512:30 + (tt + 1) * 512], tf[4][:], 1.0, pbank[ba][:],
                                ALU.add, ALU.mult, [t_tf[4], t_pb[ba]], [t_cT[ci]])
                    for dci in range(2):
                        dc = qd * 2 + dci
                        k = dcnt[0] % 2
                        dcnt[0] += 1
                        dg, tdg = diagb[k], t_diagb[k]
                        for jt in range(NPE):
                            ACT(dg[:, jt, :], identf[:], AF.Copy, CT, [tdg], scale=convw[:, dc, jt:jt + 1])
                        for tt in range(4):
                            b = pj6.next()
                            for jt in range(NPE):
                                MM(pbank[b][:], dg[:, jt, :], cT[ci][:, dci, tt * 512 + jt:tt * 512 + jt + 512],
                                   jt == 0, jt == NPE - 1, [tdg, t_cT[ci]], [t_pb[b]])
                            ta = tt % 2
                            TS("dve", tf[ta][:], cT[ci][:, dci, tt * 512 + NPE:tt * 512 + NPE + 512], convw[:, dc, NPE:NPE + 1],
                               cols[:, C_CB + dc:C_CB + dc + 1], ALU.mult, ALU.add, [t_cT[ci]] + CT, [t_tf[ta]])
                            for jt in range(NPE + 1, 31):
                                STT("dve", tf[ta][:], cT[ci][:, dci, tt * 512 + jt:tt * 512 + jt + 512], convw[:, dc, jt:jt + 1],
                                    tf[ta][:], ALU.mult, ALU.add, [t_cT[ci], t_tf[ta]] + CT, [t_tf[ta]])
                            TT("dve", oagT[:, dc, tt * 512:(tt + 1) * 512], pbank[b][:], tf[ta][:], ALU.add,
                               [t_pb[b], t_tf[ta]], [t_oag[dc][tt]])
                add_step([[w_in[:, 4096 + qd * 256:4096 + (qd + 1) * 256], w_in[:, 5120 + qd * 256:5120 + (qd + 1) * 256]]], fc)

            def f2(w0, w1):
                wz = (w0, w1)
                for tt in range(4):
                    sl = slice(tt * 512, (tt + 1) * 512)
                    bmu, bms = PO[0], PO[1]
                    for dc in range(8):
                        MM(pbank[bmu][:], ones1024[:], oagT[:, dc, sl], dc == 0, dc == 7, [t_oag[dc][tt]] + CT, [t_pb[bmu]])
                    for dc in range(8):
                        ei = et_r.next()
                        ACT(et[ei][:], oagT[:, dc, sl], AF.Square, [t_oag[dc][tt]], [t_et[ei]])
                        MM(pbank[bms][:], ones1024[:], et[ei][:], dc == 0, dc == 7, [t_et[ei]] + CT, [t_pb[bms]])
                    ACT(tf[0][:], pbank[bmu][:], AF.Copy, [t_pb[bmu]], [t_tf[0]])
                    TT("dve", tf[1][:], tf[0][:], tf[0][:], ALU.mult, [t_tf[0]], [t_tf[1]])
                    TT("dve", tf[1][:], pbank[bms][:], tf[1][:], ALU.subtract, [t_pb[bms], t_tf[1]], [t_tf[1]])
                    TS("dve", tf[1][:], tf[1][:], 1e-6, None, ALU.add, None, [t_tf[1]], [t_tf[1]])
                    RSQRT(tf[1][:], 512, [t_tf[1]] + CT, [t_tf[1]])
                    STT("dve", tf[0][:], tf[0][:], -1.0, tf[1][:], ALU.mult, ALU.mult, [t_tf[0], t_tf[1]], [t_tf[0]])
                    def stA(dc):
                        a, bq_ = (2, 3) if dc % 2 == 0 else (4, 5)
                        TT("dve", tf[a][:], oagT[:, dc, sl], tf[1][:], ALU.mult, [t_oag[dc][tt], t_tf[1]], [t_tf[a]])
                        TT("dve", tf[a][:], tf[a][:], tf[0][:], ALU.add, [t_tf[a], t_tf[0]], [t_tf[a]])
                        ACT(tf[bq_][:], tf[a][:], AF.Tanh, [t_tf[a]] + CT, [t_tf[bq_]], scale=hlng[:, dc:dc + 1], bias=hlnb[:, dc:dc + 1])
                        ACT(tf[a][:], tf[a][:], AF.Identity, [t_tf[a]] + CT, [t_tf[a]], scale=cols[:, C_LNG + dc:C_LNG + dc + 1],
                            bias=cols[:, C_LNB + dc:C_LNB + dc + 1])
                        bz = pjz.next()
                        proj_fm(wz[dc // 4], (dc % 4) * 128, tt, bz)
                        return bz

                    def stB(dc, bz):
                        a, bq_ = (2, 3) if dc % 2 == 0 else (4, 5)
                        STT("dve", tf[a][:], tf[bq_][:], 1.0, tf[a][:], ALU.add, ALU.mult, [t_tf[bq_], t_tf[a]], [t_tf[a]])
                        ACT(tf[bq_][:], pbank[bz][:], AF.Tanh, [t_pb[bz]], [t_tf[bq_]], scale=0.5)
                        STT("dve", tf[bq_][:], tf[bq_][:], 1.0, pbank[bz][:], ALU.add, ALU.mult, [t_tf[bq_], t_pb[bz]], [t_tf[bq_]])
                        STT("dve", oagT[:, dc, sl], tf[a][:], 0.25, tf[bq_][:], ALU.mult, ALU.mult, [t_tf[a], t_tf[bq_]], [t_oag[dc][tt]])
                    bzs = {0: stA(0)}
                    for dc in range(8):
                        if dc + 1 < 8:
                            bzs[dc + 1] = stA(dc + 1)
                        stB(dc, bzs[dc])
            add_step([[w_in[:, 6144:6656]], [w_in[:, 6656:7168]]], f2)
            for qd in range(4):
                merge_step(seq, 1, qd, w_ob)

        def phase_C(seq):
            def fmem():
                S.barrier(S.ENGS, dma_toks=t_xt + t_xo)
                rmsnorm_T(mem_d[seq * 256:(seq + 1) * 256, :], 2, C_GMEM, memT, lambda kc, tb: t_memT, 256)
                S.barrier(S.COMPUTE, dma_toks=t_xt + t_xo)
            add_step([], fmem)
            for p in range(2):
                def fkk(widx, p=p):
                    for ci in range(4):
                        b = pj.next()
                        for kc in range(8):
                            MM(pbank[b][:, 0:256], wb[widx][:, kc, ci * 128:(ci + 1) * 128], memT[:, kc, :], kc == 0, kc == 7,
                               [t_wb[widx], t_memT], [t_pb[b]])
                        ACT(kmT[:, p * 4 + ci, :], pbank[b][:, 0:256], AF.Copy, [t_pb[b]], [t_kmT], scale=1.0 / 16)
                add_step([[w_kv[:, p * 512:(p + 1) * 512]]], fkk)
            for p in range(2):
                def fvv(widx, p=p):
                    for kb in range(2):
                        b = pj.next()
                        for kc in range(8):
                            MM(pbank[b][:], memT[:, kc, kb * 128:(kb + 1) * 128], wb[widx][:, kc, :], kc == 0, kc == 7,
                               [t_wb[widx], t_memT], [t_pb[b]])
                        ACT(vm[:, kb, p * 512:(p + 1) * 512], pbank[b][:], AF.Copy, [t_pb[b]], [t_vm])
                add_step([[w_kv[:, 1024 + p * 512:1024 + (p + 1) * 512]]], fvv)
            for hx in range(4):
                def fx(widx, hx=hx):
                    for dci in range(2):
                        for tt in range(4):
                            b = pjx.next()
                            proj_fm(widx, dci * 128, tt, b)
                            ACT(qcT[:, dci, tt * 512:(tt + 1) * 512], pbank[b][:], AF.Copy, [t_pb[b]], [t_qc])
                            b2 = pjx.next()
                            proj_fm(widx, 256 + dci * 128, tt, b2)
                            silu2_from_psum(b2, szc[:, dci, tt * 512:(tt + 1) * 512], t_szc, 4 + (tt % 2))
                    def s1(G):
                        sl = slice(G * 512, (G + 1) * 512)
                        eis = []
                        for kb in range(2):
                            bs = ps_r.next()
                            for dci in range(2):
                                MM(pbank[bs][:], kmT[:, hx * 2 + dci, kb * 128:(kb + 1) * 128], qcT[:, dci, sl], dci == 0, dci == 1,
                                   [t_kmT, t_qc], [t_pb[bs]])
                            ei = et_r.next()
                            ACT(et[ei][:], pbank[bs][:], AF.Exp, [t_pb[bs]], [t_et[ei]])
                            eis.append(ei)
                        return eis

                    def s2(G, eis):
                        sl = slice(G * 512, (G + 1) * 512)
                        for ec in range(2):
                            for kb in range(2):
                                MM(pbank[PO[ec]][:], vm[:, kb, hx * 256 + ec * 128:hx * 256 + (ec + 1) * 128], et[eis[kb]][:],
                                   kb == 0, kb == 1, [t_vm, t_et[eis[kb]]], [t_pb[PO[ec]]])
                        for kb in range(2):
                            MM(pbank[PL[0]][:], ones_b[:], et[eis[kb]][:], kb == 0, kb == 1, [t_et[eis[kb]]] + CT, [t_pb[PL[0]]])
                        for ec in range(2):
                            ACT(tf[1 + ec][:], pbank[PO[ec]][:], AF.Copy, [t_pb[PO[ec]]], [t_tf[1 + ec]])
                        S.emit("dve", lambda e: e.tensor_copy(out=tf[0][:], in_=pbank[PL[0]][:]), [t_pb[PL[0]]], [t_tf[0]])
                        S.emit("dve", lambda e: e.reciprocal(out=tf[0][:], in_=tf[0][:]), [t_tf[0]], [t_tf[0]])
                        for ec in range(2):
                            TT("dve", tf[1 + ec][:], tf[1 + ec][:], tf[0][:], ALU.mult, [t_tf[1 + ec], t_tf[0]], [t_tf[1 + ec]])
                            STT("dve", oagT[:, hx * 2 + ec, sl], tf[1 + ec][:], 0.5, szc[:, ec, sl], ALU.mult, ALU.mult,
                                [t_tf[1 + ec], t_szc], [t_oag[hx * 2 + ec][G]])
                    e_next = s1(0)
                    for G in range(4):
                        e_cur = e_next
                        if G < 3:
                            e_next = s1(G + 1)
                        s2(G, e_cur)
                add_step([[w_in[:, 7168 + hx * 256:7168 + (hx + 1) * 256], w_in[:, 8192 + hx * 256:8192 + (hx + 1) * 256]]], fx)
            for qd in range(4):
                merge_step(seq, 2, qd, w_oc)

        def phase_D(seq):
            def fd(w0, w1):
                S.barrier(S.ENGS, dma_toks=[])
                wz = (w0, w1)
                for tb in range(16):
                    bi = tb % 2
                    tt = tb // 4
                    S.dma_load("sp", t_xt[bi], xt[bi], x_d[seq * SEQ + tb * 128:seq * SEQ + (tb + 1) * 128, :])
                    banks = (PO[bi], PL[bi])
                    for hf in range(2):
                        for kc in range(8):
                            MM(pbank[banks[hf]][:], m2T[:, kc, tb * 128:(tb + 1) * 128], wb[wz[hf]][:, kc, :], kc == 0, kc == 7,
                               [t_wb[wz[hf]], t_m2[kc][tt]], [t_pb[banks[hf]]])
                    S.emit("dve", lambda e: e.memset(ssum[:, 2:4], 0.0), [], [t_ssum])
                    for hf in range(2):
                        ACT(junk[:, 0:512], pbank[banks[hf]][:], AF.Square, [t_pb[banks[hf]], t_ssum], [t_junk, t_ssum],
                            accum_out=ssum[:, 2 + hf:3 + hf])
                    TT("dve", ssum[:, 4:5], ssum[:, 2:3], ssum[:, 3:4], ALU.add, [t_ssum], [t_ssum])
                    TS("dve", ssum[:, 4:5], ssum[:, 4:5], 0.25 / DM, 1e-6, ALU.mult, ALU.add, [t_ssum], [t_ssum])
                    RSQRT(ssum[:, 4:5], 1, [t_ssum] + CT, [t_ssum])
                    TS("dve", ssum[:, 5:6], ssum[:, 4:5], 0.5, None, ALU.mult, None, [t_ssum], [t_ssum])
                    for hf in range(2):
                        cs = slice(hf * 512, (hf + 1) * 512)
                        STT("dve", xo[bi][:, cs], pbank[banks[hf]][:], ssum[:, 5:6], gpost[:, cs], ALU.mult, ALU.mult,
                            [t_pb[banks[hf]], t_ssum] + CT, [t_xo[bi]])
                        TT("dve", xo[bi][:, cs], xo[bi][:, cs], xt[bi][:, cs], ALU.add, [t_xo[bi], t_xt[bi]], [t_xo[bi]])
                    S.dma_store("sp", t_xo[bi], out_d[seq * SEQ + tb * 128:seq * SEQ + (tb + 1) * 128, :], xo[bi])
            add_step([[w_out[:, 0:512]], [w_out[:, 512:1024]]], fd)

        for seq in range(NSEQ):
            def f0(seq=seq):
                S.barrier(S.ENGS, dma_toks=t_xt + t_xo)
                rmsnorm_T(x_d[seq * SEQ:(seq + 1) * SEQ, :], 16, C_GPRE, hT, lambda kc, tb: t_hT[kc][tb // 4], SEQ)
                S.barrier(S.COMPUTE, dma_toks=t_xt + t_xo)
            add_step([], f0)
            if "A" in phases:
                phase_A(seq)
            if "B" in phases:
                phase_B(seq)
            if "C" in phases:
                phase_C(seq)
            if debug and seq == 0:
                def fdbg():
                    t_dbg = S.tok("dbg")
                    for name in debug:
                        src = {"hT": hT, "m2T": m2T, "oagT": oagT}[name]
                        rd = hT_all if name == "hT" else [t for r in (t_m2 if name == "m2T" else t_oag) for t in r]
                        for kc in range(8):
                            for tt in range(4):
                                S.emit("dve", lambda e, kc=kc, tt=tt, src=src: e.tensor_copy(out=tf[0][:], in_=src[:, kc, tt * 512:(tt + 1) * 512]),
                                       rd, [t_tf[0]])
                                S.dma_store("sp", t_tf[0], dbg_d[name][kc * 128:(kc + 1) * 128, tt * 512:(tt + 1) * 512], tf[0][:])
                add_step([], fdbg)
            if "D" in phases:
                phase_D(seq)
        if phases != "0ABCD":
            def ftouch():
                t_touch = S.tok("touch")
                for ap in (mem_d, w_in, w_oa, w_ob, w_oc, w_out, w_kv):
                    S.dma_load("sp", t_touch, ssum[0:1, 0:8], ap[0:1, 0:8])
                S.emit("dve", lambda e: e.memset(tf[1][:], 0.0), [], [t_tf[1]])
                S.dma_store("sp", t_tf[1], out_d[0:128, 0:512], tf[1][:])
            add_step([], ftouch)
        run_steps()
        S.build()
        print("instr counts", {e: len(S.prog[e]) for e in S.ENGS}, "signals", S.sigcount, "waits", S.nwaits)
    return nc


def _t5_bucket_host(rel):
    nb, max_exact = 16, 8
    try:
        import jax
        import jax.numpy as jnp
        cpu = jax.devices("cpu")[0]
        with jax.default_device(cpu):
            r = jnp.asarray(rel, dtype=jnp.int32)
            ret = (r > 0).astype(jnp.int32) * nb
            n = jnp.abs(r)
            nf = jnp.maximum(n, 1).astype(jnp.float32)
            large = max_exact + (jnp.log(nf / max_exact) / math.log(128 / max_exact) * (nb - max_exact)).astype(jnp.int32)
            large = jnp.minimum(large, nb - 1)
            return np.asarray(ret + jnp.where(n < max_exact, n, large))
    except Exception:
        ret = (rel > 0).astype(np.int32) * nb
        n = np.abs(rel)
        nf = np.maximum(n, 1).astype(np.float32)
        large = max_exact + (np.log(nf / np.float32(max_exact)) / np.float32(math.log(128 / max_exact))
                             * np.float32(nb - max_exact)).astype(np.int32)
        large = np.minimum(large, nb - 1)
        return ret + np.where(n < max_exact, n, large)


def _colvec(v):
    v = np.asarray(v, np.float32).reshape(-1)
    return np.ascontiguousarray(v.reshape(-1, 128).T)


def prep_shared(rel_bias, g_pre, g_mem, w_in, b_merge, lam_q1, lam_k1, lam_q2, lam_k2, g_subln, w_oa, conv_w,
                conv_b, ln_g, ln_b, w_ob, w_mem_kv, w_oc, w_out, g_post):
    f = lambda a: np.ascontiguousarray(np.asarray(a, np.float32))
    rel_bias = f(rel_bias)
    cols = np.zeros((128, NCOLS), np.float32)
    cols[:, C_GPRE:C_GPRE + 8] = _colvec(g_pre[0])
    cols[:, C_GMEM:C_GMEM + 8] = _colvec(g_mem[0])
    cols[:, C_BM:C_BM + 24] = _colvec(b_merge[0])
    cols[:, C_CB:C_CB + 8] = _colvec(conv_b[0])
    cols[:, C_LNG:C_LNG + 8] = _colvec(ln_g[0])
    cols[:, C_LNB:C_LNB + 8] = _colvec(ln_b[0])
    cols[:, C_GSUB:C_GSUB + 1] = _colvec(g_subln[0])
    far_b = int(_t5_bucket_host(np.array([-1000], np.int32))[0])
    cols[:, C_CFAR:C_CFAR + 8] = np.broadcast_to(rel_bias[far_b][None, :], (128, 8))
    cw = f(conv_w)[0]
    convw = np.ascontiguousarray(cw.reshape(31, 8, 128).transpose(2, 1, 0)).reshape(128, 8 * 31)
    kk = np.arange(128, dtype=np.int32)[:, None]
    qq = np.arange(128, dtype=np.int32)[None, :]
    relD = kk - qq
    relU = kk - qq - 128
    bD = _t5_bucket_host(relD)
    bU = _t5_bucket_host(relU)
    allowed = (kk // 64) <= (qq // 64)
    ext = np.concatenate([rel_bias, np.full((1, 8), NEG, np.float32)], axis=0)
    idxD = np.where(allowed, bD, 32)
    D = ext[idxD]
    U = ext[bU]
    biasdu = np.ascontiguousarray(np.concatenate([D, U], axis=1).transpose(0, 2, 1)).reshape(128, 8 * 256)
    lamv = np.concatenate([f(lam_q1)[0], f(lam_k1)[0], f(lam_q2)[0], f(lam_k2)[0]]).astype(np.float32)
    return {
        "w_in": f(w_in)[0], "w_oa": f(w_oa)[0], "w_ob": f(w_ob)[0], "w_oc": f(w_oc)[0], "w_out": f(w_out)[0],
        "w_mem_kv": f(w_mem_kv)[0], "cols": cols, "convw": convw, "g_post": f(g_post)[0], "lamv": lamv,
        "biasdu": biasdu.astype(np.float32),
    }


_PROG = {}


def kernel(x, mem, rel_bias, g_pre, g_mem, w_in, b_merge, lam_q1, lam_k1, lam_q2, lam_k2, g_subln, w_oa, conv_w,
           conv_b, ln_g, ln_b, w_ob, w_mem_kv, w_oc, w_out, g_post):
    shared = prep_shared(rel_bias, g_pre, g_mem, w_in, b_merge, lam_q1, lam_k1, lam_q2, lam_k2, g_subln, w_oa,
                         conv_w, conv_b, ln_g, ln_b, w_ob, w_mem_kv, w_oc, w_out, g_post)
    x = np.asarray(x, np.float32)
    mem = np.asarray(mem, np.float32)
    B = x.shape[0]
    per = B // NCORES
    if "p" not in _PROG:
        _PROG["p"] = build_program(NSEQ=per)
    nc = _PROG["p"]
    in_maps = []
    for c in range(NCORES):
        m = dict(shared)
        m["x"] = np.ascontiguousarray(x[c * per:(c + 1) * per].reshape(per * SEQ, DM))
        m["mem"] = np.ascontiguousarray(mem[c * per:(c + 1) * per].reshape(per * 256, DM))
        in_maps.append(m)
    res = run_bass_kernel_spmd(nc, in_maps, core_ids=list(range(NCORES)))
    outs = [np.asarray(r["out"], np.float32).reshape(per, SEQ, DM) for r in res.results]
    return np.concatenate(outs, axis=0)
```

```python
import math
from contextlib import ExitStack
import numpy as np
import concourse.bass as bass
import concourse.mybir as mybir
from concourse.bass_utils import run_bass_kernel_spmd

F32 = mybir.dt.float32
BF16 = mybir.dt.bfloat16
AF = mybir.ActivationFunctionType
ALU = mybir.AluOpType
AX = mybir.AxisListType

SAME_ENGINE_SYNC = True
EPOCH = 20000
NEG = -30000.0
SEQ = 2048
DM = 1024
NCORES = 8


class Tk:
    __slots__ = ("name", "lw", "rd", "dsem", "dcnt")

    def __init__(self, name):
        self.name = name
        self.lw = None
        self.rd = {}
        self.dsem = None
        self.dcnt = 0


class Ins:
    __slots__ = ("eng", "fn", "cdeps", "raw", "dwaits", "sig", "val", "dma", "waits")

    def __init__(self, eng, fn):
        self.eng = eng
        self.fn = fn
        self.cdeps = set()
        self.raw = set()
        self.dwaits = {}
        self.sig = False
        self.val = None
        self.dma = None
        self.waits = []


class Sched:
    ENGS = ("pe", "act", "dve", "pool", "sp")
    COMPUTE = ("pe", "act", "dve", "pool")

    def __init__(self, nc, stack):
        self.nc = nc
        self.stack = stack
        self.prog = {e: [] for e in self.ENGS}
        self.esem = {e: [] for e in self.ENGS}
        self.toks = []
        self.last_compute = {e: None for e in self.ENGS}
        self.pending = {e: (set(), {}) for e in self.ENGS}

    def tok(self, name):
        t = Tk(name)
        self.toks.append(t)
        return t

    def _dep(self, ins, p, raw):
        if p is None or p is ins:
            return
        if p.dma is not None:
            t = p.dma
            ins.dwaits[t] = max(ins.dwaits.get(t, 0), 16 * t.dcnt)
        else:
            ins.cdeps.add(p)
            if raw:
                ins.raw.add(p)

    def emit(self, eng, fn, reads=(), writes=(), dma=None):
        ins = Ins(eng, fn)
        pc, pd = self.pending[eng]
        if pc or pd:
            for p in pc:
                if p.eng != eng:
                    ins.cdeps.add(p)
            for t, v in pd.items():
                ins.dwaits[t] = max(ins.dwaits.get(t, 0), v)
            self.pending[eng] = (set(), {})
        for t in reads:
            self._dep(ins, t.lw, True)
        for t in writes:
            self._dep(ins, t.lw, False)
            for r in t.rd.values():
                self._dep(ins, r, False)
        if dma is not None:
            if dma.dsem is None:
                dma.dsem = self.stack.enter_context(self.nc.semaphore("ds_" + dma.name))
            dma.dcnt += 1
            ins.dma = dma
        else:
            self.last_compute[eng] = ins
        for t in reads:
            key = eng if dma is None else ("dma", id(ins))
            t.rd[key] = ins
        for t in writes:
            t.lw = ins
            t.rd = {}
        self.prog[eng].append(ins)
        return ins

    def barrier(self, engs, dma_toks=()):
        lasts = set(self.last_compute[e] for e in self.COMPUTE if self.last_compute[e] is not None)
        for e in engs:
            pc, pd = self.pending[e]
            pc |= lasts
            for t in dma_toks:
                if t.dsem is not None:
                    pd[t] = max(pd.get(t, 0), 16 * t.dcnt)

    def dma_load(self, q, tok, out_ap, in_ap, **kw):
        return self.emit(q, lambda e: e.dma_start(out=out_ap, in_=in_ap, **kw), writes=[tok], dma=tok)

    def dma_store(self, q, tok, out_ap, in_ap, **kw):
        return self.emit(q, lambda e: e.dma_start(out=out_ap, in_=in_ap, **kw), reads=[tok], dma=tok)

    def _needs(self, ins, p):
        if p.eng != ins.eng:
            return True
        return SAME_ENGINE_SYNC and p.eng != "pe"

    def finalize(self):
        for e in self.ENGS:
            for ins in self.prog[e]:
                for p in ins.cdeps:
                    if self._needs(ins, p):
                        p.sig = True
        for e in self.ENGS:
            c = 0
            for ins in self.prog[e]:
                if ins.sig:
                    ep, v = divmod(c, EPOCH)
                    if ep >= len(self.esem[e]):
                        self.esem[e].append(self.stack.enter_context(self.nc.semaphore("es_%s%d" % (e, ep))))
                    ins.val = (ep, v + 1)
                    c += 1
        self.sigcount = {e: sum(1 for i in self.prog[e] if i.sig) for e in self.ENGS}
        self.nwaits = 0
        for e in self.ENGS:
            known = {}
            for ins in self.prog[e]:
                need = {}
                for p in ins.cdeps:
                    if not self._needs(ins, p):
                        continue
                    need[p.eng] = max(need.get(p.eng, (0, 0)), p.val)
                for pe_, v in need.items():
                    if known.get(pe_, (0, 0)) < v:
                        ins.waits.append((self.esem[pe_][v[0]], v[1]))
                        known[pe_] = v
                for t, v in ins.dwaits.items():
                    k = ("d", id(t))
                    if known.get(k, 0) < v:
                        ins.waits.append((t.dsem, v))
                        known[k] = v
                self.nwaits += len(ins.waits)

    def simulate(self):
        sem = {}
        pc = {e: 0 for e in self.ENGS}
        progress = True
        while progress:
            progress = False
            for e in self.ENGS:
                while pc[e] < len(self.prog[e]):
                    ins = self.prog[e][pc[e]]
                    if all(sem.get(id(s), 0) >= v for (s, v) in ins.waits):
                        if ins.dma is not None:
                            sem[id(ins.dma.dsem)] = sem.get(id(ins.dma.dsem), 0) + 16
                        elif ins.sig:
                            k = id(self.esem[e][ins.val[0]])
                            sem[k] = sem.get(k, 0) + 1
                        pc[e] += 1
                        progress = True
                    else:
                        break
        stuck = {e: (pc[e], len(self.prog[e])) for e in self.ENGS if pc[e] < len(self.prog[e])}
        if stuck:
            for e in stuck:
                ins = self.prog[e][pc[e]]
                print("STUCK", e, pc[e], [(s.name, v, sem.get(id(s), 0)) for (s, v) in ins.waits])
            raise RuntimeError("deadlock in generated program: %s" % stuck)

    def run_engine(self, ename, e):
        for ins in self.prog[ename]:
            for (s, v) in ins.waits:
                e.wait_ge(s, v)
            bi = ins.fn(e)
            if ins.dma is not None:
                bi.then_inc(ins.dma.dsem, 16)
            elif ins.sig:
                bi.then_inc(self.esem[ename][ins.val[0]], 1)

    def build(self):
        fin = {}
        for t in self.toks:
            if t.dsem is not None:
                fin[t] = 16 * t.dcnt
        self.finalize()
        self.simulate()
        nc = self.nc
        with nc.Block() as block:
            @block.tensor
            def _(e):
                self.run_engine("pe", e)

            @block.scalar
            def _(e):
                self.run_engine("act", e)

            @block.vector
            def _(e):
                self.run_engine("dve", e)

            @block.gpsimd
            def _(e):
                self.run_engine("pool", e)

            @block.sync
            def _(e):
                self.run_engine("sp", e)
                for t, v in fin.items():
                    e.wait_ge(t.dsem, v)


STEP_LOG = []
EPI_D1 = 7
EPI_D2 = 9
NCOLS = 80
C_GPRE, C_GMEM, C_BM, C_CB, C_LNG, C_LNB, C_GSUB, C_CFAR = 0, 8, 16, 40, 48, 56, 64, 65


def build_program(NSEQ=2, debug=None, phases="0ABCD"):
    nc = bass.Bass("TRN2", target_bir_lowering=False)
    dram = lambda n, shp, kind="ExternalInput": nc.dram_tensor(n, shp, F32, kind=kind).ap()
    x_d = dram("x", [NSEQ * SEQ, DM])
    mem_d = dram("mem", [NSEQ * 256, DM])
    w_in = dram("w_in", [DM, 12288])
    w_oa = dram("w_oa", [DM, DM])
    w_ob = dram("w_ob", [DM, DM])
    w_oc = dram("w_oc", [DM, DM])
    w_out = dram("w_out", [DM, DM])
    w_kv = dram("w_mem_kv", [DM, 2048])
    cols_d = dram("cols", [128, NCOLS])
    convw_d = dram("convw", [128, 8 * 31])
    gpost_d = dram("g_post", [DM])
    lamv_d = dram("lamv", [256])
    bias_d = dram("biasdu", [128, 8 * 256])
    out_d = dram("out", [NSEQ * SEQ, DM], kind="ExternalOutput")
    dbg_d = {}
    if debug:
        for name, shp in debug.items():
            dbg_d[name] = dram("dbg_" + name, shp, kind="ExternalOutput")

    with ExitStack() as st:
        S = Sched(nc, st)
        sb = lambda name, shape, dt: st.enter_context(nc.sbuf_tensor("s_" + name, shape, dt))
        psb = lambda name, shape, dt: st.enter_context(nc.psum_tensor("p_" + name, shape, dt))

        cols = sb("cols", [128, NCOLS], F32)
        convw = sb("convw", [128, 8, 31], F32)
        gpost = sb("gpost", [128, DM], F32)
        lamv = sb("lamv", [128, 256], F32)
        bhi = sb("bhi", [128, 8, 256], BF16)
        blo = sb("blo", [128, 8, 256], BF16)
        identf = sb("identf", [128, 128], F32)
        identb = sb("identb", [128, 128], BF16)
        ones_b = sb("ones_b", [128, 128], BF16)
        ones128 = sb("ones128", [128, 128], BF16)
        ones1024 = sb("ones1024", [128, 128], BF16)
        mhalf = sb("mhalf", [128, 512], F32)
        small = sb("small", [128, 16], F32)
        epsc = sb("epsc", [128, 1], F32)
        hbm = sb("hbm", [128, 24], F32)
        hlng = sb("hlng", [128, 8], F32)
        hlnb = sb("hlnb", [128, 8], F32)
        t_const = S.tok("const")
        S.dma_load("sp", t_const, cols[:], cols_d)
        S.dma_load("sp", t_const, convw[:], convw_d.rearrange("p (c j) -> p c j", j=31))
        S.dma_load("sp", t_const, gpost[:], gpost_d.partition_broadcast(128))
        S.dma_load("sp", t_const, lamv[:], lamv_d.partition_broadcast(128))
        t_c2 = S.tok("const2")
        S.emit("pool", lambda e: e.memset(identf[:], 0.0), writes=[t_c2])
        S.emit("pool", lambda e: e.affine_select(out=identf[:], in_=identf[:], pattern=[[-1, 128]],
                                                 compare_op=ALU.not_equal, fill=1.0, base=0,
                                                 channel_multiplier=1), reads=[t_c2], writes=[t_c2])
        S.emit("pool", lambda e: e.memset(mhalf[:], -0.5), writes=[t_c2])
        S.emit("dve", lambda e: e.tensor_copy(out=identb[:], in_=identf[:]), reads=[t_c2], writes=[t_c2])
        S.emit("dve", lambda e: e.memset(ones_b[:], 1.0), writes=[t_c2])
        S.emit("dve", lambda e: e.memset(epsc[:], 1e-6), writes=[t_c2])
        S.emit("dve", lambda e: e.memset(ones128[:], 1.0 / 128), writes=[t_c2])
        S.emit("dve", lambda e: e.memset(ones1024[:], 1.0 / 1024), writes=[t_c2])
        lam_init = 0.8 - 0.6 * math.exp(-0.3 * 0)
        S.emit("dve", lambda e: e.tensor_tensor(out=lamv[:, 0:64], in0=lamv[:, 0:64], in1=lamv[:, 64:128], op=ALU.mult),
               reads=[t_const], writes=[t_c2])
        S.emit("dve", lambda e: e.tensor_tensor(out=lamv[:, 128:192], in0=lamv[:, 128:192], in1=lamv[:, 192:256], op=ALU.mult),
               reads=[t_const], writes=[t_c2])
        S.emit("dve", lambda e: e.reduce_sum(out=small[:, 2:3], in_=lamv[:, 0:64], axis=AX.X), reads=[t_c2], writes=[t_c2])
        S.emit("dve", lambda e: e.reduce_sum(out=small[:, 3:4], in_=lamv[:, 128:192], axis=AX.X), reads=[t_c2], writes=[t_c2])
        S.emit("act", lambda e: e.activation(out=small[:, 4:6], in_=small[:, 2:4], func=AF.Exp), reads=[t_c2], writes=[t_c2])
        S.emit("dve", lambda e: e.tensor_tensor(out=small[:, 0:1], in0=small[:, 5:6], in1=small[:, 4:5], op=ALU.subtract),
               reads=[t_c2], writes=[t_c2])
        S.emit("dve", lambda e: e.tensor_scalar(out=small[:, 0:1], in0=small[:, 0:1], scalar1=-lam_init, scalar2=None, op0=ALU.add),
               reads=[t_c2], writes=[t_c2])
        S.emit("dve", lambda e: e.tensor_scalar(out=small[:, 1:2], in0=cols[:, C_GSUB:C_GSUB + 1], scalar1=0.5 * (1.0 - lam_init),
                                                scalar2=None, op0=ALU.mult), reads=[t_const], writes=[t_c2])
        S.emit("dve", lambda e: e.tensor_scalar(out=hbm[:], in0=cols[:, C_BM:C_BM + 24], scalar1=0.5, scalar2=None, op0=ALU.mult),
               reads=[t_const], writes=[t_c2])
        S.emit("dve", lambda e: e.tensor_scalar(out=hlng[:], in0=cols[:, C_LNG:C_LNG + 8], scalar1=0.5, scalar2=None, op0=ALU.mult),
               reads=[t_const], writes=[t_c2])
        S.emit("dve", lambda e: e.tensor_scalar(out=hlnb[:], in0=cols[:, C_LNB:C_LNB + 8], scalar1=0.5, scalar2=None, op0=ALU.mult),
               reads=[t_const], writes=[t_c2])
        S.emit("dve", lambda e: e.tensor_scalar(out=convw[:], in0=convw[:], scalar1=0.5, scalar2=None, op0=ALU.mult),
               reads=[t_const], writes=[t_c2])
        CT = [t_const, t_c2]
        nlam = small[:, 0:1]
        gsub = small[:, 1:2]

        hT = sb("hT", [128, 8, SEQ], BF16)
        m2T = sb("m2T", [128, 8, SEQ], BF16)
        big = sb("big", [128, 8192], F32)
        oagT = big[:].bitcast(BF16).rearrange("p (c t) -> p c t", c=8)
        xt = [big[:, i * 1024:(i + 1) * 1024] for i in range(2)]
        xo = [big[:, 2048 + i * 1024:2048 + (i + 1) * 1024] for i in range(2)]
        scr = sb("scr", [128, 16384], BF16)
        junk = sb("junk", [128, 1024], BF16)
        et = [sb("et%d" % i, [128, 512], BF16) for i in range(4)]
        tf = [sb("tf%d" % i, [128, 512], F32) for i in range(8)]
        sqb = [sb("sqb%d" % i, [128, 512], BF16) for i in range(2)]
        NPE = 23
        diagb = [sb("diag%d" % i, [128, NPE, 128], BF16) for i in range(2)]
        dcnt = [0]
        ssum = sb("ssum", [128, 8], F32)
        NB = 3
        wb = [sb("wb%d" % i, [128, 8, 512], BF16) for i in range(NB)]
        t_wb = [S.tok("wb%d" % i) for i in range(NB)]
        pbank = [psb("pb%d" % i, [128, 512], F32) for i in range(8)]
        t_pb = [S.tok("pb%d" % i) for i in range(8)]
        PJ, PS_, PO, PL = (0, 1), (2, 3), (4, 5), (6, 7)

        t_hT = [[S.tok("hT%d_%d" % (kc, tt)) for tt in range(4)] for kc in range(8)]
        t_m2 = [[S.tok("m2%d_%d" % (kc, tt)) for tt in range(4)] for kc in range(8)]
        t_oag = [[S.tok("oag%d_%d" % (kc, tt)) for tt in range(4)] for kc in range(8)]
        t_xt = [S.tok("xt%d" % i) for i in range(2)]
        t_xo = [S.tok("xo%d" % i) for i in range(2)]
        t_junk = S.tok("junk")
        t_et = [S.tok("et%d" % i) for i in range(4)]
        t_tf = [S.tok("tf%d" % i) for i in range(8)]
        t_sqb = [S.tok("sqb0"), S.tok("sqb1")]
        t_diagb = [S.tok("diag0"), S.tok("diag1")]
        t_ssum = S.tok("ssum")
        hT_all = [t for r in t_hT for t in r]
        print("sbuf bytes remaining:", nc.sbuf_bytes_remaining)

        def MM(out, lhsT, rhs, start, stop, reads, writes):
            S.emit("pe", lambda e: e.matmul(out, lhsT=lhsT, rhs=rhs, start=start, stop=stop), reads, writes)

        def ACT(out, in_, func, reads, writes, eng="act", **kw):
            S.emit(eng, lambda e: e.activation(out=out, in_=in_, func=func, **kw), reads, writes)

        def TS(eng, out, in0, s1, s2, op0, op1, reads, writes):
            if s2 is None:
                S.emit(eng, lambda e: e.tensor_scalar(out=out, in0=in0, scalar1=s1, scalar2=None, op0=op0), reads, writes)
            else:
                S.emit(eng, lambda e: e.tensor_scalar(out=out, in0=in0, scalar1=s1, scalar2=s2, op0=op0, op1=op1), reads, writes)

        def STT(eng, out, in0, scalar, in1, op0, op1, reads, writes):
            S.emit(eng, lambda e: e.scalar_tensor_tensor(out=out, in0=in0, scalar=scalar, in1=in1, op0=op0, op1=op1), reads, writes)

        def TT(eng, out, in0, in1, op, reads, writes):
            S.emit(eng, lambda e: e.tensor_tensor(out=out, in0=in0, in1=in1, op=op), reads, writes)

        def RSQRT(ap, n, reads, writes):
            S.emit("act", lambda e: e.activation(out=ap, in_=ap, func=AF.Sqrt), reads, writes)
            S.emit("dve", lambda e: e.reciprocal(out=ap, in_=ap), writes, writes)

        class Rot:
            def __init__(self, items):
                self.items = items
                self.i = 0

            def next(self):
                v = self.items[self.i]
                self.i = (self.i + 1) % len(self.items)
                return v

        pj = Rot(list(PJ))
        ps_r = Rot(list(PS_))
        et_r = Rot([0, 1, 2, 3])
        ps4 = Rot([PS_[0], PS_[1], PJ[1], PJ[0]])
        pj6 = Rot([0, 1, 4, 5, 6, 7])
        pjz = Rot([0, 1, 6, 7])
        pjx = Rot([0, 1, 7])

        wstate = {"next": 0}

        def wload(slices):
            i = wstate["next"]
            wstate["next"] = (i + 1) % NB
            off = 0
            for ap in slices:
                n = ap.shape[1]
                S.dma_load("pool", t_wb[i], wb[i][:, :, off:off + n], ap.rearrange("(kc p) n -> p kc n", p=128))
                off += n
            return i

        steps = []

        def add_step(panels, fn):
            steps.append((panels, fn))

        def run_steps():
            loaded = {}
            for i, (panels, fn) in enumerate(steps):
                cur = loaded.setdefault(i, [])
                while len(cur) < len(panels):
                    cur.append(wload(panels[len(cur)]))
                if i + 1 < len(steps):
                    nxt_panels = steps[i + 1][0]
                    nxt = loaded.setdefault(i + 1, [])
                    free = NB - len(cur)
                    while len(nxt) < len(nxt_panels) and free > 0:
                        nxt.append(wload(nxt_panels[len(nxt)]))
                        free -= 1
                STEP_LOG.append((getattr(fn, "__name__", "?"), len(S.prog["pe"])))
                fn(*cur)

        def rmsnorm_T(src_rows_ap, ntb, gcol0, dst, dst_toks_fn, width):
            for tb in range(ntb):
                bi = tb % 2
                S.dma_load("sp", t_xt[bi], xt[bi], src_rows_ap[tb * 128:(tb + 1) * 128, :])
                S.emit("dve", lambda e: e.memset(ssum[:, 0:1], 0.0), [], [t_ssum])
                ACT(junk[:], xt[bi], AF.Square, [t_xt[bi], t_ssum], [t_junk, t_ssum], accum_out=ssum[:, 0:1])
                TS("dve", ssum[:, 1:2], ssum[:, 0:1], 1.0 / DM, 1e-6, ALU.mult, ALU.add, [t_ssum], [t_ssum])
                RSQRT(ssum[:, 1:2], 1, [t_ssum] + CT, [t_ssum])
                TS("dve", xo[bi], xt[bi], ssum[:, 1:2], None, ALU.mult, None, [t_xt[bi], t_ssum], [t_xo[bi]])
                for half in range(2):
                    b = ps_r.next()
                    for i in range(4):
                        kc = half * 4 + i
                        S.emit("pe", lambda e, b=b, i=i, kc=kc, bi=bi: e.transpose(
                            out=pbank[b][:, i * 128:(i + 1) * 128], in_=xo[bi][:, kc * 128:(kc + 1) * 128],
                            identity=identf[:]), [t_xo[bi]] + CT, [t_pb[b]])
                    for i in range(4):
                        kc = half * 4 + i
                        dt = dst_toks_fn(kc, tb)
                        ACT(dst[:, kc, tb * 128:(tb + 1) * 128], pbank[b][:, i * 128:(i + 1) * 128], AF.Copy,
                            [t_pb[b]] + CT, [dt], scale=cols[:, gcol0 + kc:gcol0 + kc + 1])

        qT = scr[:, 0:4096].rearrange("p (j t) -> p j t", j=2)
        kT = scr[:, 4096:8192].rearrange("p (j t) -> p j t", j=2)
        vv = scr[:, 8192:12288].rearrange("p (b n) -> p b n", b=16)
        sza = scr[:, 12288:16384].rearrange("p (j t) -> p j t", j=2)
        t_q, t_k, t_v, t_sz = S.tok("qT"), S.tok("kT"), S.tok("vv"), S.tok("sza")
        bsf = scr[:, 0:4096].bitcast(F32).rearrange("p (h k) -> p h k", k=256)
        S.dma_load("sp", t_q, bsf, bias_d.rearrange("p (h k) -> p h k", k=256))
        t_bias = S.tok("biashl")
        for h in range(8):
            TS("dve", bsf[:, h, :], bsf[:, h, :], cols[:, C_CFAR + h:C_CFAR + h + 1], None, ALU.subtract, None, [t_q] + CT, [t_q])
            S.emit("dve", lambda e, h=h: e.tensor_copy(out=bhi[:, h, :], in_=bsf[:, h, :]), [t_q], [t_bias])
            TT("dve", blo[:, h, :], bsf[:, h, :], bhi[:, h, :], ALU.subtract, [t_q, t_bias], [t_bias])
        CW = SEQ + 30
        cT = [scr[:, i * 2 * CW:(i + 1) * 2 * CW].rearrange("p (j t) -> p j t", j=2) for i in range(2)]
        t_cT = [S.tok("cT0"), S.tok("cT1")]
        memT = scr[:, 0:2048].rearrange("p (c t) -> p c t", c=8)
        kmT = scr[:, 2048:4096].rearrange("p (c t) -> p c t", c=8)
        vm = scr[:, 4096:6144].rearrange("p (b n) -> p b n", b=2)
        qcT = scr[:, 6144:10240].rearrange("p (j t) -> p j t", j=2)
        szc = scr[:, 10240:14336].rearrange("p (j t) -> p j t", j=2)
        t_memT, t_kmT, t_vm, t_qc, t_szc = S.tok("memT"), S.tok("kmT"), S.tok("vm"), S.tok("qcT"), S.tok("szc")

        def proj_fm(widx, col0, tt, bank):
            for kc in range(8):
                MM(pbank[bank][:], wb[widx][:, kc, col0:col0 + 128], hT[:, kc, tt * 512:(tt + 1) * 512],
                   kc == 0, kc == 7, [t_wb[widx], t_hT[kc][tt]], [t_pb[bank]])

        def silu2_from_psum(bank, dst, dst_tok, tfi):
            ACT(tf[tfi][:], pbank[bank][:], AF.Tanh, [t_pb[bank]], [t_tf[tfi]], scale=0.5)
            STT("dve", dst, tf[tfi][:], 1.0, pbank[bank][:], ALU.add, ALU.mult, [t_tf[tfi], t_pb[bank]], [dst_tok])

        first_b = min(i for i, p in enumerate("ABC") if p in phases) if any(p in phases for p in "ABC") else 0
        def merge_step(seq, b, qd, w_o):
            def fn(widx):
                for dci in range(2):
                    dc = qd * 2 + dci
                    for tt in range(4):
                        by = pj6.next()
                        for kc in range(8):
                            MM(pbank[by][:], wb[widx][:, kc, dci * 128:(dci + 1) * 128], oagT[:, kc, tt * 512:(tt + 1) * 512],
                               kc == 0, kc == 7, [t_wb[widx], t_oag[kc][tt]], [t_pb[by]])
                        bg = pj6.next()
                        for kc in range(8):
                            MM(pbank[bg][:], wb[widx][:, kc, 256 + dci * 128:256 + (dci + 1) * 128], hT[:, kc, tt * 512:(tt + 1) * 512],
                               kc == 0, kc == 7, [t_wb[widx], t_hT[kc][tt]], [t_pb[bg]])
                        ACT(tf[4][:], pbank[bg][:], AF.Tanh, [t_pb[bg]] + CT, [t_tf[4]], scale=0.5,
                            bias=hbm[:, b * 8 + dc:b * 8 + dc + 1])
                        dst = m2T[:, dc, tt * 512:(tt + 1) * 512]
                        if b == first_b:
                            STT("dve", dst, tf[4][:], 1.0, pbank[by][:], ALU.add, ALU.mult, [t_tf[4], t_pb[by]], [t_m2[dc][tt]])
                        else:
                            STT("dve", tf[5][:], tf[4][:], 1.0, pbank[by][:], ALU.add, ALU.mult, [t_tf[4], t_pb[by]], [t_tf[5]])
                            TT("dve", dst, tf[5][:], dst, ALU.add, [t_tf[5], t_m2[dc][tt]], [t_m2[dc][tt]])
            gcol = 9216 + b * 1024 + qd * 256
            add_step([[w_o[:, qd * 256:(qd + 1) * 256], w_in[:, gcol:gcol + 256]]], fn)

        def attn_group(hg):
            tasks = []
            for j in range(2):
                for G in range(4):
                    nkb = 4 * G + 4
                    for kb in range(nkb):
                        tasks.append((j, G, kb, nkb))
            state = {}
            deferred = []
            epi_n = [0]

            def stage1(t):
                j, G, kb, nkb = t
                h = 2 * hg + j
                jj = kb - 4 * G
                c0 = max(jj, 0) * 128
                W = 512 - c0
                if jj >= 0:
                    boff, nw = 0, min(256, W)
                elif jj == -1:
                    boff, nw = 128, 128
                else:
                    boff, nw = 0, 0
                bss = [ps4.next(), ps4.next()]
                for c in range(2):
                    r0, r1 = c * 64, (c + 1) * 64
                    MM(pbank[bss[c]][:, 0:W], kT[r0:r1, j, kb * 128:(kb + 1) * 128],
                       qT[r0:r1, j, G * 512 + c0:(G + 1) * 512], True, nw == 0, [t_k, t_q], [t_pb[bss[c]]])
                eis = []
                for c in range(2):
                    bs = bss[c]
                    if nw > 0:
                        MM(pbank[bs][:, 0:nw], identb[:], bhi[:, h, boff:boff + nw], False, False, [t_bias] + CT, [t_pb[bs]])
                        MM(pbank[bs][:, 0:nw], identb[:], blo[:, h, boff:boff + nw], False, True, [t_bias] + CT, [t_pb[bs]])
                    ei = et_r.next()
                    ACT(et[ei][:, 0:W], pbank[bs][:, 0:W], AF.Exp, [t_pb[bs]], [t_et[ei]])
                    eis.append(ei)
                state[t] = (eis, c0, W)

            def stage2(t):
                j, G, kb, nkb = t
                h = 2 * hg + j
                eis, c0, W = state.pop(t)
                for c in range(2):
                    po, pl, ei = PO[c], PL[c], eis[c]
                    MM(pbank[po][:, c0:512], vv[:, kb, j * 128:(j + 1) * 128], et[ei][:, 0:W],
                       kb == 0, kb == nkb - 1, [t_v, t_et[ei]], [t_pb[po]])
                    MM(pbank[pl][:, c0:512], ones_b[:], et[ei][:, 0:W],
                       kb == 0, kb == nkb - 1, [t_et[ei]] + CT, [t_pb[pl]])
                if kb == nkb - 1:
                    epilogue(j, G, h)

            def epilogue(j, G, h):
                k = epi_n[0] % 2
                epi_n[0] += 1
                to, tr = (0, 1) if k == 0 else (6, 7)
                ACT(tf[4][:], pbank[PL[0]][:], AF.Ln, [t_pb[PL[0]]], [t_tf[4]])
                ACT(tf[5][:], pbank[PL[1]][:], AF.Ln, [t_pb[PL[1]]], [t_tf[5]])
                S.emit("dve", lambda e: e.tensor_copy(out=tf[to][:], in_=pbank[PO[0]][:]), [t_pb[PO[0]]], [t_tf[to]])
                S.emit("dve", lambda e: e.tensor_copy(out=tf[tr][:], in_=pbank[PO[1]][:]), [t_pb[PO[1]]], [t_tf[tr]])
                ACT(tf[4][:], tf[4][:], AF.Exp, [t_tf[4]], [t_tf[4]], scale=-1.0)
                ACT(tf[5][:], tf[5][:], AF.Exp, [t_tf[5]], [t_tf[5]], scale=-1.0)
                TT("dve", tf[to][:], tf[to][:], tf[4][:], ALU.mult, [t_tf[to], t_tf[4]], [t_tf[to]])
                TT("dve", tf[tr][:], tf[tr][:], tf[5][:], ALU.mult, [t_tf[tr], t_tf[5]], [t_tf[tr]])
                STT("dve", tf[to][:], tf[tr][:], nlam, tf[to][:], ALU.mult, ALU.add, [t_tf[to], t_tf[tr]] + CT, [t_tf[to]])
                TT("dve", sqb[k][:], tf[to][:], tf[to][:], ALU.mult, [t_tf[to]], [t_sqb[k]])

                def part3a():
                    bq = ps4.next()
                    MM(pbank[bq][:], ones128[:], sqb[k][:], True, True, [t_sqb[k]] + CT, [t_pb[bq]])
                    ACT(tf[tr][:], pbank[bq][:], AF.Ln, [t_pb[bq]] + CT, [t_tf[tr]], bias=epsc[:, 0:1])

                def part3b():
                    ACT(tf[tr][:], tf[tr][:], AF.Exp, [t_tf[tr]], [t_tf[tr]], scale=-0.5)
                    STT("dve", tf[to][:], tf[to][:], gsub, tf[tr][:], ALU.mult, ALU.mult, [t_tf[to], t_tf[tr]] + CT, [t_tf[to]])
                    TT("dve", oagT[:, h, G * 512:(G + 1) * 512], tf[to][:], sza[:, j, G * 512:(G + 1) * 512], ALU.mult,
                       [t_tf[to], t_sz], [t_oag[h][G]])
                deferred.append([EPI_D1, part3a])
                deferred.append([EPI_D2, part3b])

            def tick():
                for d in list(deferred):
                    d[0] -= 1
                    if d[0] <= 0:
                        deferred.remove(d)
                        d[1]()

            n = len(tasks)
            LOOK = 1
            for i in range(min(LOOK, n)):
                stage1(tasks[i])
            for i in range(n):
                if i + LOOK < n:
                    stage1(tasks[i + LOOK])
                stage2(tasks[i])
                tick()
            while deferred:
                tick()

        def phase_A(seq):
            for hg in range(4):
                def fq(widx, hg=hg):
                    for j in range(2):
                        for tt in range(4):
                            b = pj6.next()
                            proj_fm(widx, j * 128, tt, b)
                            ACT(qT[:, j, tt * 512:(tt + 1) * 512], pbank[b][:], AF.Copy, [t_pb[b]], [t_q], scale=0.125)
                add_step([[w_in[:, hg * 256:(hg + 1) * 256]]], fq)

                def fk(widx, hg=hg):
                    for j in range(2):
                        for tt in range(4):
                            b = pj6.next()
                            proj_fm(widx, j * 128, tt, b)
                            S.emit("dve", lambda e, b=b, j=j, tt=tt: e.tensor_copy(out=kT[:, j, tt * 512:(tt + 1) * 512], in_=pbank[b][:]),
                                   [t_pb[b]], [t_k])
                add_step([[w_in[:, 1024 + hg * 256:1024 + (hg + 1) * 256]]], fk)

                def fv(widx, hg=hg):
                    for tb in range(16):
                        b = pj6.next()
                        tt = tb // 4
                        for kc in range(8):
                            MM(pbank[b][:, 0:256], hT[:, kc, tb * 128:(tb + 1) * 128], wb[widx][:, kc, 0:256],
                               kc == 0, kc == 7, [t_wb[widx], t_hT[kc][tt]], [t_pb[b]])
                        if tb % 2 == 0:
                            ACT(vv[:, tb, :], pbank[b][:, 0:256], AF.Copy, [t_pb[b]], [t_v])
                        else:
                            S.emit("dve", lambda e, b=b, tb=tb: e.tensor_copy(out=vv[:, tb, :], in_=pbank[b][:, 0:256]), [t_pb[b]], [t_v])
                add_step([[w_in[:, 2048 + hg * 256:2048 + (hg + 1) * 256]]], fv)

                def fz(widx, hg=hg):
                    for j in range(2):
                        for tt in range(4):
                            b = pj6.next()
                            proj_fm(widx, j * 128, tt, b)
                            silu2_from_psum(b, sza[:, j, tt * 512:(tt + 1) * 512], t_sz, 4 + (tt % 2))
                add_step([[w_in[:, 3072 + hg * 256:3072 + (hg + 1) * 256]]], fz)

                def fa(hg=hg):
                    attn_group(hg)
                add_step([], fa)
            for qd in range(4):
                merge_step(seq, 0, qd, w_oa)

        def phase_B(seq):
            def bar():
                S.barrier(S.COMPUTE)
            add_step([], bar)
            for qd in range(4):
                def fc(widx, qd=qd):
                    ci = qd % 2
                    S.emit("dve", lambda e, ci=ci: e.memset(cT[ci][:, :, 0:30], 0.0), [], [t_cT[ci]])
                    for dci in range(2):
                        dc = qd * 2 + dci
                        for tt in range(4):
                            ba = pj6.next()
                            proj_fm(widx, dci * 128, tt, ba)
                            bg = pj6.next()
                            proj_fm(widx, 256 + dci * 128, tt, bg)
                            ACT(tf[4][:], pbank[bg][:], AF.Tanh, [t_pb[bg]], [t_tf[4]], scale=0.5)
                            STT("dve", cT[ci][:, dci, 30 + tt * 512:30 + (tt + 1) * 512], tf[4][:], 1.0, pbank[ba][:],
                                ALU.add, ALU.mult, [t_tf[4], t_pb[ba]], [t_cT[ci]])
                    for dci in range(2):
                        dc = qd * 2 + dci
                        k = dcnt[0] % 2
                        dcnt[0] += 1
                        dg, tdg = diagb[k], t_diagb[k]
                        for jt in range(NPE):
                            ACT(dg[:, jt, :], identf[:], AF.Copy, CT, [tdg], scale=convw[:, dc, jt:jt + 1])
                        for tt in range(4):
                            b = pj6.next()
                            for jt in range(NPE):
                                MM(pbank[b][:], dg[:, jt, :], cT[ci][:, dci, tt * 512 + jt:tt * 512 + jt + 512],
                                   jt == 0, jt == NPE - 1, [tdg, t_cT[ci]], [t_pb[b]])
                            ta = tt % 2
                            TS("dve", tf[ta][:], cT[ci][:, dci, tt * 512 + NPE:tt * 512 + NPE + 512], convw[:, dc, NPE:NPE + 1],
                               cols[:, C_CB + dc:C_CB + dc + 1], ALU.mult, ALU.add, [t_cT[ci]] + CT, [t_tf[ta]])
                            for jt in range(NPE + 1, 31):
                                STT("dve", tf[ta][:], cT[ci][:, dci, tt * 512 + jt:tt * 512 + jt + 512], convw[:, dc, jt:jt + 1],
                                    tf[ta][:], ALU.mult, ALU.add, [t_cT[ci], t_tf[ta]] + CT, [t_tf[ta]])
                            TT("dve", oagT[:, dc, tt * 512:(tt + 1) * 512], pbank[b][:], tf[ta][:], ALU.add,
                               [t_pb[b], t_tf[ta]], [t_oag[dc][tt]])
                add_step([[w_in[:, 4096 + qd * 256:4096 + (qd + 1) * 256], w_in[:, 5120 + qd * 256:5120 + (qd + 1) * 256]]], fc)

            def f2(w0, w1):
                wz = (w0, w1)
                for tt in range(4):
                    sl = slice(tt * 512, (tt + 1) * 512)
                    bmu, bms = PO[0], PO[1]
                    for dc in range(8):
                        MM(pbank[bmu][:], ones1024[:], oagT[:, dc, sl], dc == 0, dc == 7, [t_oag[dc][tt]] + CT, [t_pb[bmu]])
                    for dc in range(8):
                        ei = et_r.next()
                        ACT(et[ei][:], oagT[:, dc, sl], AF.Square, [t_oag[dc][tt]], [t_et[ei]])
                        MM(pbank[bms][:], ones1024[:], et[ei][:], dc == 0, dc == 7, [t_et[ei]] + CT, [t_pb[bms]])
                    ACT(tf[0][:], pbank[bmu][:], AF.Copy, [t_pb[bmu]], [t_tf[0]])
                    TT("dve", tf[1][:], tf[0][:], tf[0][:], ALU.mult, [t_tf[0]], [t_tf[1]])
                    TT("dve", tf[1][:], pbank[bms][:], tf[1][:], ALU.subtract, [t_pb[bms], t_tf[1]], [t_tf[1]])
                    TS("dve", tf[1][:], tf[1][:], 1e-6, None, ALU.add, None, [t_tf[1]], [t_tf[1]])
                    RSQRT(tf[1][:], 512, [t_tf[1]] + CT, [t_tf[1]])
                    STT("dve", tf[0][:], tf[0][:], -1.0, tf[1][:], ALU.mult, ALU.mult, [t_tf[0], t_tf[1]], [t_tf[0]])
                    def stA(dc):
                        a, bq_ = (2, 3) if dc % 2 == 0 else (4, 5)
                        TT("dve", tf[a][:], oagT[:, dc, sl], tf[1][:], ALU.mult, [t_oag[dc][tt], t_tf[1]], [t_tf[a]])
                        TT("dve", tf[a][:], tf[a][:], tf[0][:], ALU.add, [t_tf[a], t_tf[0]], [t_tf[a]])
                        ACT(tf[bq_][:], tf[a][:], AF.Tanh, [t_tf[a]] + CT, [t_tf[bq_]], scale=hlng[:, dc:dc + 1], bias=hlnb[:, dc:dc + 1])
                        ACT(tf[a][:], tf[a][:], AF.Identity, [t_tf[a]] + CT, [t_tf[a]], scale=cols[:, C_LNG + dc:C_LNG + dc + 1],
                            bias=cols[:, C_LNB + dc:C_LNB + dc + 1])
                        bz = pjz.next()
                        proj_fm(wz[dc // 4], (dc % 4) * 128, tt, bz)
                        return bz

                    def stB(dc, bz):
                        a, bq_ = (2, 3) if dc % 2 == 0 else (4, 5)
                        STT("dve", tf[a][:], tf[bq_][:], 1.0, tf[a][:], ALU.add, ALU.mult, [t_tf[bq_], t_tf[a]], [t_tf[a]])
                        ACT(tf[bq_][:], pbank[bz][:], AF.Tanh, [t_pb[bz]], [t_tf[bq_]], scale=0.5)
                        STT("dve", tf[bq_][:], tf[bq_][:], 1.0, pbank[bz][:], ALU.add, ALU.mult, [t_tf[bq_], t_pb[bz]], [t_tf[bq_]])
                        STT("dve", oagT[:, dc, sl], tf[a][:], 0.25, tf[bq_][:], ALU.mult, ALU.mult, [t_tf[a], t_tf[bq_]], [t_oag[dc][tt]])
                    bzs = {0: stA(0)}
                    for dc in range(8):
                        if dc + 1 < 8:
                            bzs[dc + 1] = stA(dc + 1)
                        stB(dc, bzs[dc])
            add_step([[w_in[:, 6144:6656]], [w_in[:, 6656:7168]]], f2)
            for qd in range(4):
                merge_step(seq, 1, qd, w_ob)

        def phase_C(seq):
            def fmem():
                S.barrier(S.ENGS, dma_toks=t_xt + t_xo)
                rmsnorm_T(mem_d[seq * 256:(seq + 1) * 256, :], 2, C_GMEM, memT, lambda kc, tb: t_memT, 256)
                S.barrier(S.COMPUTE, dma_toks=t_xt + t_xo)
            add_step([], fmem)
            for p in range(2):
                def fkk(widx, p=p):
                    for ci in range(4):
                        b = pj.next()
                        for kc in range(8):
                            MM(pbank[b][:, 0:256], wb[widx][:, kc, ci * 128:(ci + 1) * 128], memT[:, kc, :], kc == 0, kc == 7,
                               [t_wb[widx], t_memT], [t_pb[b]])
                        ACT(kmT[:, p * 4 + ci, :], pbank[b][:, 0:256], AF.Copy, [t_pb[b]], [t_kmT], scale=1.0 / 16)
                add_step([[w_kv[:, p * 512:(p + 1) * 512]]], fkk)
            for p in range(2):
                def fvv(widx, p=p):
                    for kb in range(2):
                        b = pj.next()
                        for kc in range(8):
                            MM(pbank[b][:], memT[:, kc, kb * 128:(kb + 1) * 128], wb[widx][:, kc, :], kc == 0, kc == 7,
                               [t_wb[widx], t_memT], [t_pb[b]])
                        ACT(vm[:, kb, p * 512:(p + 1) * 512], pbank[b][:], AF.Copy, [t_pb[b]], [t_vm])
                add_step([[w_kv[:, 1024 + p * 512:1024 + (p + 1) * 512]]], fvv)
            for hx in range(4):
                def fx(widx, hx=hx):
                    for dci in range(2):
                        for tt in range(4):
                            b = pjx.next()
                            proj_fm(widx, dci * 128, tt, b)
                            ACT(qcT[:, dci, tt * 512:(tt + 1) * 512], pbank[b][:], AF.Copy, [t_pb[b]], [t_qc])
                            b2 = pjx.next()
                            proj_fm(widx, 256 + dci * 128, tt, b2)
                            silu2_from_psum(b2, szc[:, dci, tt * 512:(tt + 1) * 512], t_szc, 4 + (tt % 2))
                    def s1(G):
                        sl = slice(G * 512, (G + 1) * 512)
                        eis = []
                        for kb in range(2):
                            bs = ps_r.next()
                            for dci in range(2):
                                MM(pbank[bs][:], kmT[:, hx * 2 + dci, kb * 128:(kb + 1) * 128], qcT[:, dci, sl], dci == 0, dci == 1,
                                   [t_kmT, t_qc], [t_pb[bs]])
                            ei = et_r.next()
                            ACT(et[ei][:], pbank[bs][:], AF.Exp, [t_pb[bs]], [t_et[ei]])
                            eis.append(ei)
                        return eis

                    def s2(G, eis):
                        sl = slice(G * 512, (G + 1) * 512)
                        for ec in range(2):
                            for kb in range(2):
                                MM(pbank[PO[ec]][:], vm[:, kb, hx * 256 + ec * 128:hx * 256 + (ec + 1) * 128], et[eis[kb]][:],
                                   kb == 0, kb == 1, [t_vm, t_et[eis[kb]]], [t_pb[PO[ec]]])
                        for kb in range(2):
                            MM(pbank[PL[0]][:], ones_b[:], et[eis[kb]][:], kb == 0, kb == 1, [t_et[eis[kb]]] + CT, [t_pb[PL[0]]])
                        for ec in range(2):
                            ACT(tf[1 + ec][:], pbank[PO[ec]][:], AF.Copy, [t_pb[PO[ec]]], [t_tf[1 + ec]])
                        ACT(tf[0][:], pbank[PL[0]][:], AF.Ln, [t_pb[PL[0]]], [t_tf[0]])
                        ACT(tf[0][:], tf[0][:], AF.Exp, [t_tf[0]], [t_tf[0]], scale=-1.0)
                        for ec in range(2):
                            TT("dve", tf[1 + ec][:], tf[1 + ec][:], tf[0][:], ALU.mult, [t_tf[1 + ec], t_tf[0]], [t_tf[1 + ec]])
                            STT("dve", oagT[:, hx * 2 + ec, sl], tf[1 + ec][:], 0.5, szc[:, ec, sl], ALU.mult, ALU.mult,
                                [t_tf[1 + ec], t_szc], [t_oag[hx * 2 + ec][G]])
                    e_next = s1(0)
                    for G in range(4):
                        e_cur = e_next
                        if G < 3:
                            e_next = s1(G + 1)
                        s2(G, e_cur)
                add_step([[w_in[:, 7168 + hx * 256:7168 + (hx + 1) * 256], w_in[:, 8192 + hx * 256:8192 + (hx + 1) * 256]]], fx)
            for qd in range(4):
                merge_step(seq, 2, qd, w_oc)

        def phase_D(seq):
            def fd(w0, w1):
                S.barrier(S.ENGS, dma_toks=[])
                wz = (w0, w1)
                for tb in range(16):
                    bi = tb % 2
                    tt = tb // 4
                    S.dma_load("sp", t_xt[bi], xt[bi], x_d[seq * SEQ + tb * 128:seq * SEQ + (tb + 1) * 128, :])
                    banks = (PO[bi], PL[bi])
                    for hf in range(2):
                        for kc in range(8):
                            MM(pbank[banks[hf]][:], m2T[:, kc, tb * 128:(tb + 1) * 128], wb[wz[hf]][:, kc, :], kc == 0, kc == 7,
                               [t_wb[wz[hf]], t_m2[kc][tt]], [t_pb[banks[hf]]])
                    S.emit("dve", lambda e: e.memset(ssum[:, 2:4], 0.0), [], [t_ssum])
                    for hf in range(2):
                        ACT(junk[:, 0:512], pbank[banks[hf]][:], AF.Square, [t_pb[banks[hf]], t_ssum], [t_junk, t_ssum],
                            accum_out=ssum[:, 2 + hf:3 + hf])
                    TT("dve", ssum[:, 4:5], ssum[:, 2:3], ssum[:, 3:4], ALU.add, [t_ssum], [t_ssum])
                    TS("dve", ssum[:, 4:5], ssum[:, 4:5], 0.25 / DM, 1e-6, ALU.mult, ALU.add, [t_ssum], [t_ssum])
                    RSQRT(ssum[:, 4:5], 1, [t_ssum] + CT, [t_ssum])
                    TS("dve", ssum[:, 5:6], ssum[:, 4:5], 0.5, None, ALU.mult, None, [t_ssum], [t_ssum])
                    for hf in range(2):
                        cs = slice(hf * 512, (hf + 1) * 512)
                        STT("dve", xo[bi][:, cs], pbank[banks[hf]][:], ssum[:, 5:6], gpost[:, cs], ALU.mult, ALU.mult,
                            [t_pb[banks[hf]], t_ssum] + CT, [t_xo[bi]])
                        TT("dve", xo[bi][:, cs], xo[bi][:, cs], xt[bi][:, cs], ALU.add, [t_xo[bi], t_xt[bi]], [t_xo[bi]])
                    S.dma_store("sp", t_xo[bi], out_d[seq * SEQ + tb * 128:seq * SEQ + (tb + 1) * 128, :], xo[bi])
            add_step([[w_out[:, 0:512]], [w_out[:, 512:1024]]], fd)

        for seq in range(NSEQ):
            def f0(seq=seq):
                S.barrier(S.ENGS, dma_toks=t_xt + t_xo)
                rmsnorm_T(x_d[seq * SEQ:(seq + 1) * SEQ, :], 16, C_GPRE, hT, lambda kc, tb: t_hT[kc][tb // 4], SEQ)
                S.barrier(S.COMPUTE, dma_toks=t_xt + t_xo)
            add_step([], f0)
            if "A" in phases:
                phase_A(seq)
            if "B" in phases:
                phase_B(seq)
            if "C" in phases:
                phase_C(seq)
            if debug and seq == 0:
                def fdbg():
                    t_dbg = S.tok("dbg")
                    for name in debug:
                        src = {"hT": hT, "m2T": m2T, "oagT": oagT}[name]
                        rd = hT_all if name == "hT" else [t for r in (t_m2 if name == "m2T" else t_oag) for t in r]
                        for kc in range(8):
                            for tt in range(4):
                                S.emit("dve", lambda e, kc=kc, tt=tt, src=src: e.tensor_copy(out=tf[0][:], in_=src[:, kc, tt * 512:(tt + 1) * 512]),
                                       rd, [t_tf[0]])
                                S.dma_store("sp", t_tf[0], dbg_d[name][kc * 128:(kc + 1) * 128, tt * 512:(tt + 1) * 512], tf[0][:])
                add_step([], fdbg)
            if "D" in phases:
                phase_D(seq)
        if phases != "0ABCD":
            def ftouch():
                t_touch = S.tok("touch")
                for ap in (mem_d, w_in, w_oa, w_ob, w_oc, w_out, w_kv):
                    S.dma_load("sp", t_touch, ssum[0:1, 0:8], ap[0:1, 0:8])
                S.emit("dve", lambda e: e.memset(tf[1][:], 0.0), [], [t_tf[1]])
                S.dma_store("sp", t_tf[1], out_d[0:128, 0:512], tf[1][:])
            add_step([], ftouch)
        run_steps()
        S.build()
        print("instr counts", {e: len(S.prog[e]) for e in S.ENGS}, "signals", S.sigcount, "waits", S.nwaits)
    return nc


def _t5_bucket_host(rel):
    nb, max_exact = 16, 8
    try:
        import jax
        import jax.numpy as jnp
        cpu = jax.devices("cpu")[0]
        with jax.default_device(cpu):
            r = jnp.asarray(rel, dtype=jnp.int32)
            ret = (r > 0).astype(jnp.int32) * nb
            n = jnp.abs(r)
            nf = jnp.maximum(n, 1).astype(jnp.float32)
            large = max_exact + (jnp.log(nf / max_exact) / math.log(128 / max_exact) * (nb - max_exact)).astype(jnp.int32)
            large = jnp.minimum(large, nb - 1)
            return np.asarray(ret + jnp.where(n < max_exact, n, large))
    except Exception:
        ret = (rel > 0).astype(np.int32) * nb
        n = np.abs(rel)
        nf = np.maximum(n, 1).astype(np.float32)
        large = max_exact + (np.log(nf / np.float32(max_exact)) / np.float32(math.log(128 / max_exact))
                             * np.float32(nb - max_exact)).astype(np.int32)
        large = np.minimum(large, nb - 1)
        return ret + np.where(n < max_exact, n, large)


def _colvec(v):
    v = np.asarray(v, np.float32).reshape(-1)
    return np.ascontiguousarray(v.reshape(-1, 128).T)


def prep_shared(rel_bias, g_pre, g_mem, w_in, b_merge, lam_q1, lam_k1, lam_q2, lam_k2, g_subln, w_oa, conv_w,
                conv_b, ln_g, ln_b, w_ob, w_mem_kv, w_oc, w_out, g_post):
    f = lambda a: np.ascontiguousarray(np.asarray(a, np.float32))
    rel_bias = f(rel_bias)
    cols = np.zeros((128, NCOLS), np.float32)
    cols[:, C_GPRE:C_GPRE + 8] = _colvec(g_pre[0])
    cols[:, C_GMEM:C_GMEM + 8] = _colvec(g_mem[0])
    cols[:, C_BM:C_BM + 24] = _colvec(b_merge[0])
    cols[:, C_CB:C_CB + 8] = _colvec(conv_b[0])
    cols[:, C_LNG:C_LNG + 8] = _colvec(ln_g[0])
    cols[:, C_LNB:C_LNB + 8] = _colvec(ln_b[0])
    cols[:, C_GSUB:C_GSUB + 1] = _colvec(g_subln[0])
    far_b = int(_t5_bucket_host(np.array([-1000], np.int32))[0])
    cols[:, C_CFAR:C_CFAR + 8] = np.broadcast_to(rel_bias[far_b][None, :], (128, 8))
    cw = f(conv_w)[0]
    convw = np.ascontiguousarray(cw.reshape(31, 8, 128).transpose(2, 1, 0)).reshape(128, 8 * 31)
    kk = np.arange(128, dtype=np.int32)[:, None]
    qq = np.arange(128, dtype=np.int32)[None, :]
    relD = kk - qq
    relU = kk - qq - 128
    bD = _t5_bucket_host(relD)
    bU = _t5_bucket_host(relU)
    allowed = (kk // 64) <= (qq // 64)
    ext = np.concatenate([rel_bias, np.full((1, 8), NEG, np.float32)], axis=0)
    idxD = np.where(allowed, bD, 32)
    D = ext[idxD]
    U = ext[bU]
    biasdu = np.ascontiguousarray(np.concatenate([D, U], axis=1).transpose(0, 2, 1)).reshape(128, 8 * 256)
    lamv = np.concatenate([f(lam_q1)[0], f(lam_k1)[0], f(lam_q2)[0], f(lam_k2)[0]]).astype(np.float32)
    return {
        "w_in": f(w_in)[0], "w_oa": f(w_oa)[0], "w_ob": f(w_ob)[0], "w_oc": f(w_oc)[0], "w_out": f(w_out)[0],
        "w_mem_kv": f(w_mem_kv)[0], "cols": cols, "convw": convw, "g_post": f(g_post)[0], "lamv": lamv,
        "biasdu": biasdu.astype(np.float32),
    }


_PROG = {}


def kernel(x, mem, rel_bias, g_pre, g_mem, w_in, b_merge, lam_q1, lam_k1, lam_q2, lam_k2, g_subln, w_oa, conv_w,
           conv_b, ln_g, ln_b, w_ob, w_mem_kv, w_oc, w_out, g_post):
    shared = prep_shared(rel_bias, g_pre, g_mem, w_in, b_merge, lam_q1, lam_k1, lam_q2, lam_k2, g_subln, w_oa,
                         conv_w, conv_b, ln_g, ln_b, w_ob, w_mem_kv, w_oc, w_out, g_post)
    x = np.asarray(x, np.float32)
    mem = np.asarray(mem, np.float32)
    B = x.shape[0]
    per = B // NCORES
    if "p" not in _PROG:
        _PROG["p"] = build_program(NSEQ=per)
    nc = _PROG["p"]
    in_maps = []
    for c in range(NCORES):
        m = dict(shared)
        m["x"] = np.ascontiguousarray(x[c * per:(c + 1) * per].reshape(per * SEQ, DM))
        m["mem"] = np.ascontiguousarray(mem[c * per:(c + 1) * per].reshape(per * 256, DM))
        in_maps.append(m)
    res = run_bass_kernel_spmd(nc, in_maps, core_ids=list(range(NCORES)))
    outs = [np.asarray(r["out"], np.float32).reshape(per, SEQ, DM) for r in res.results]
    return np.concatenate(outs, axis=0)
```
